# Optimizing a Trainium2 kernel written in Bass

```python
import jax, jax.numpy as jnp
from jax import lax
import numpy as np

D_MODEL = 1024
BATCH = 2
SEQ = 8192
DEPTH = 2
DEC_BATCH = 32
DEC_SEQ = 32
PAST_LEN = 1024

CHUNK = 64
EPS = 1e-6
NEG_INF = -1e30
ROPE_BASE = 10000.0
Q_BLOCK = 128
RET_HEADS = 4
RET_DK = 64
RET_DV = 64
RET_W = RET_HEADS * RET_DV
MLA_HEADS = 8
MLA_NOPE = 64
MLA_ROPE = 32
MLA_V = 64
MLA_QK = MLA_NOPE + MLA_ROPE
Q_LORA = 256
KV_LORA = 128
MLA_W = MLA_HEADS * MLA_V
BAND_HEADS = 4
BAND_DH = 64
BAND_W = BAND_HEADS * BAND_DH
BAND_PREV_CHUNKS = 8
BAND_PAST = BAND_PREV_CHUNKS * CHUNK
BAND_KEYS = BAND_PAST + CHUNK
MAX_REL = 128
N_REL = 2 * MAX_REL + 1
D_MIX = RET_W + MLA_W + BAND_W
_SEG = (RET_HEADS * RET_DK, RET_HEADS * RET_DK, RET_W, RET_W,
        Q_LORA, KV_LORA, MLA_ROPE, MLA_W,
        BAND_W, BAND_W, BAND_W, BAND_W)
D_IN = sum(_SEG)

kernel_name = 'hybrid_retention_mla_chunkband_stream_step'


def _rmsnorm(x, g):
    xf = x.astype(jnp.float32)
    y = xf * lax.rsqrt(jnp.mean(xf * xf, axis=-1, keepdims=True) + EPS)
    return (y * g.astype(jnp.float32)).astype(x.dtype)


def _rope(x, pos):
    half = x.shape[-1] // 2
    inv = ROPE_BASE ** (-jnp.arange(half, dtype=jnp.float32) / half)
    ang = pos.astype(jnp.float32)[:, None] * inv[None, :]
    cos = jnp.cos(ang)[None, :, None, :]
    sin = jnp.sin(ang)[None, :, None, :]
    xf = x.astype(jnp.float32)
    x1, x2 = xf[..., :half], xf[..., half:]
    return jnp.concatenate([x1 * cos - x2 * sin, x1 * sin + x2 * cos], axis=-1).astype(x.dtype)


def _project(x, norm_g, w_in):
    h = _rmsnorm(x, norm_g)
    z = jnp.einsum('btd,de->bte', h, w_in)
    cuts, acc = [], 0
    for s in _SEG[:-1]:
        acc += s
        cuts.append(acc)
    return jnp.split(z, cuts, axis=-1)


def _ret_log_gamma():
    return jnp.log1p(-jnp.exp2(-5.0 - jnp.arange(RET_HEADS, dtype=jnp.float32)))


def _retention_block(q, k, v, S):
    q = q.astype(jnp.float32)
    k = k.astype(jnp.float32)
    v = v.astype(jnp.float32)
    S = S.astype(jnp.float32)
    L = q.shape[1]
    lg = _ret_log_gamma()
    idx = jnp.arange(L, dtype=jnp.float32)
    diff = idx[:, None] - idx[None, :]
    dmask = jnp.where(diff >= 0, jnp.exp(lg[:, None, None] * jnp.maximum(diff, 0.0)), 0.0)
    inner = jnp.einsum('blhd,bmhd->bhlm', q, k) * dmask[None]
    o = jnp.einsum('bhlm,bmhe->blhe', inner, v)
    xi = jnp.exp(lg[None, :] * (idx[:, None] + 1.0))
    o = o + jnp.einsum('blhd,bhde->blhe', q, S) * xi[None, :, :, None]
    zeta = jnp.exp(lg[None, :] * (L - 1.0 - idx)[:, None])
    S_new = (jnp.exp(lg * L)[None, :, None, None] * S
             + jnp.einsum('blhd,blhe->bhde', k * zeta[None, :, :, None], v))
    return o, S_new


def _retention_prompt(q, k, v):
    B, S, H, dk = q.shape
    nc = S // CHUNK

    def to_chunks(t):
        return t.reshape(B, nc, CHUNK, H, t.shape[-1]).swapaxes(0, 1)

    S0 = jnp.zeros((B, H, dk, v.shape[-1]), jnp.float32)

    def step(state, qkv):
        o, state = _retention_block(qkv[0], qkv[1], qkv[2], state)
        return state, o

    S_fin, o = lax.scan(step, S0, (to_chunks(q), to_chunks(k), to_chunks(v)))
    return o.swapaxes(0, 1).reshape(B, S, H, v.shape[-1]), S_fin


def _mla_prompt_attn(q, k, v):
    B, S, H, dq = q.shape
    nb = S // Q_BLOCK
    scale = dq ** -0.5
    key_chunk = jnp.arange(S) // CHUNK
    qb = q.reshape(B, nb, Q_BLOCK, H, dq).swapaxes(0, 1)

    def one_block(args):
        q_blk, b_idx = args
        q_chunk = (b_idx * Q_BLOCK + jnp.arange(Q_BLOCK)) // CHUNK
        s = jnp.einsum('bqhd,bkhd->bhqk', q_blk, k).astype(jnp.float32) * scale
        s = jnp.where(key_chunk[None, :] <= q_chunk[:, None], s, NEG_INF)
        p = jax.nn.softmax(s, axis=-1).astype(v.dtype)
        return jnp.einsum('bhqk,bkhd->bqhd', p, v)

    o = lax.map(one_block, (qb, jnp.arange(nb)))
    return o.swapaxes(0, 1).reshape(B, S, H, v.shape[-1])


def _dense_attn(q, k, v, bias):
    s = jnp.einsum('bqhd,bkhd->bhqk', q, k).astype(jnp.float32) * (q.shape[-1] ** -0.5)
    if bias is not None:
        s = s + bias[None].astype(jnp.float32)
    p = jax.nn.softmax(s, axis=-1).astype(v.dtype)
    return jnp.einsum('bhqk,bkhd->bqhd', p, v)


def _band_prompt_attn(q, k, v, band_bias):
    B, S, H, d = q.shape
    nc = S // CHUNK

    def gather_band(t):
        tc = t.reshape(B, nc, CHUNK, H, d)
        tp = jnp.pad(tc, ((0, 0), (BAND_PREV_CHUNKS, 0), (0, 0), (0, 0), (0, 0)))
        return jnp.concatenate([tp[:, j:j + nc] for j in range(BAND_PREV_CHUNKS + 1)], axis=2)

    kb, vb = gather_band(k), gather_band(v)
    qc = q.reshape(B, nc, CHUNK, H, d)
    s = jnp.einsum('bcqhd,bckhd->bchqk', qc, kb).astype(jnp.float32) * (d ** -0.5)
    qi = jnp.arange(CHUNK)
    kj = jnp.arange(BAND_KEYS)
    dist = qi[:, None] + BAND_PAST - kj[None, :]
    bias = band_bias[:, jnp.clip(dist, -MAX_REL, MAX_REL) + MAX_REL]
    s = s + bias[None, None].astype(jnp.float32)
    key_chunk = jnp.arange(nc)[:, None] - BAND_PREV_CHUNKS + kj[None, :] // CHUNK
    s = jnp.where((key_chunk >= 0)[None, :, None, None, :], s, NEG_INF)
    p = jax.nn.softmax(s, axis=-1).astype(v.dtype)
    o = jnp.einsum('bchqk,bckhd->bcqhd', p, vb)
    return o.reshape(B, S, H, d)


def _layer(x, pos, norm_g, w_in, ret_gn_g, mla_qa_g, mla_w_uq, mla_qn_g, mla_qr_g,
           mla_kva_g, mla_kr_g, mla_w_ukv, mla_kn_g, band_qn_g, band_kn_g, band_bias,
           w_out, past):
    B, L, _ = x.shape
    (a_q, a_k, a_v, a_g, b_cq, b_ckv, b_kr, b_g,
     c_q, c_k, c_v, c_g) = _project(x, norm_g, w_in)

    q = _rope(a_q.reshape(B, L, RET_HEADS, RET_DK), pos)
    k = _rope(a_k.reshape(B, L, RET_HEADS, RET_DK), pos) * (RET_DK ** -0.5)
    v = a_v.reshape(B, L, RET_HEADS, RET_DV)
    if past is None:
        ret_o, ret_s = _retention_prompt(q, k, v)
    else:
        ret_o, ret_s = _retention_block(q, k, v, past[0])
    ret_o = _rmsnorm(ret_o, ret_gn_g.reshape(RET_HEADS, RET_DV)).astype(x.dtype).reshape(B, L, RET_W)

    cq = _rmsnorm(b_cq, mla_qa_g)
    qf = jnp.einsum('btr,re->bte', cq, mla_w_uq).reshape(B, L, MLA_HEADS, MLA_QK)
    q_m = jnp.concatenate([_rmsnorm(qf[..., :MLA_NOPE], mla_qn_g),
                           _rope(_rmsnorm(qf[..., MLA_NOPE:], mla_qr_g), pos)], axis=-1)
    ckv_new = _rmsnorm(b_ckv, mla_kva_g)
    kr_new = _rope(_rmsnorm(b_kr, mla_kr_g)[:, :, None, :], pos)[:, :, 0, :]
    if past is None:
        ckv_all, kr_all = ckv_new, kr_new
    else:
        ckv_all = jnp.concatenate([past[1], ckv_new], axis=1)
        kr_all = jnp.concatenate([past[2], kr_new], axis=1)
    T = ckv_all.shape[1]
    kv = jnp.einsum('btc,ce->bte', ckv_all, mla_w_ukv).reshape(B, T, MLA_HEADS, MLA_NOPE + MLA_V)
    k_m = jnp.concatenate([_rmsnorm(kv[..., :MLA_NOPE], mla_kn_g),
                           jnp.broadcast_to(kr_all[:, :, None, :], (B, T, MLA_HEADS, MLA_ROPE))], axis=-1)
    v_m = kv[..., MLA_NOPE:]
    if past is None:
        mla_o = _mla_prompt_attn(q_m, k_m, v_m)
    else:
        mla_o = _dense_attn(q_m, k_m, v_m, None)
    mla_o = mla_o.reshape(B, L, MLA_W)

    qb = _rmsnorm(c_q.reshape(B, L, BAND_HEADS, BAND_DH), band_qn_g)
    kb = _rmsnorm(c_k.reshape(B, L, BAND_HEADS, BAND_DH), band_kn_g)
    vb = c_v.reshape(B, L, BAND_HEADS, BAND_DH)
    if past is None:
        band_o = _band_prompt_attn(qb, kb, vb, band_bias)
        n_keep = min(BAND_PAST, L)
        band_k_state, band_v_state = kb[:, L - n_keep:], vb[:, L - n_keep:]
    else:
        n_c = past[3].shape[1]
        k_all = jnp.concatenate([past[3], kb], axis=1)
        v_all = jnp.concatenate([past[4], vb], axis=1)
        kpos = jnp.concatenate([pos[0] - n_c + jnp.arange(n_c), pos])
        dist = pos[:, None] - kpos[None, :]
        bias = band_bias[:, jnp.clip(dist, -MAX_REL, MAX_REL) + MAX_REL]
        band_o = _dense_attn(qb, k_all, v_all, bias)
        band_k_state, band_v_state = kb, vb
    band_o = band_o.reshape(B, L, BAND_W)

    mix = jnp.concatenate([ret_o * jax.nn.silu(a_g),
                           mla_o * jax.nn.silu(b_g),
                           band_o * jax.nn.silu(c_g)], axis=-1)
    y = x + jnp.einsum('bte,ed->btd', mix, w_out)
    return y, (ret_s.astype(x.dtype), ckv_new, kr_new, band_k_state, band_v_state)


def setup_inputs(seed: int = 0) -> dict:
    key = jax.random.key(seed)
    ks = jax.random.split(key, 24)

    def nrm(k, shape, s=1.0):
        return s * jax.random.normal(k, shape, jnp.float32)

    def gain(k, shape):
        return 1.0 + 0.1 * jax.random.normal(k, shape, jnp.float32)

    band_cache = min(BAND_PAST, PAST_LEN)
    return {
        'x_prompt': nrm(ks[0], (BATCH, SEQ, D_MODEL)),
        'x_sample': nrm(ks[1], (DEC_BATCH, DEC_SEQ, D_MODEL)),
        'state_ret': nrm(ks[2], (DEPTH, DEC_BATCH, RET_HEADS, RET_DK, RET_DV), 0.3),
        'cache_mla_ckv': nrm(ks[3], (DEPTH, DEC_BATCH, PAST_LEN, KV_LORA)),
        'cache_mla_krope': nrm(ks[4], (DEPTH, DEC_BATCH, PAST_LEN, MLA_ROPE)),
        'cache_band_k': nrm(ks[5], (DEPTH, DEC_BATCH, band_cache, BAND_HEADS, BAND_DH)),
        'cache_band_v': nrm(ks[6], (DEPTH, DEC_BATCH, band_cache, BAND_HEADS, BAND_DH)),
        'norm_g': gain(ks[7], (DEPTH, D_MODEL)),
        'w_in': nrm(ks[8], (DEPTH, D_MODEL, D_IN), D_MODEL ** -0.5),
        'ret_gn_g': gain(ks[9], (DEPTH, RET_W)),
        'mla_qa_g': gain(ks[10], (DEPTH, Q_LORA)),
        'mla_w_uq': nrm(ks[11], (DEPTH, Q_LORA, MLA_HEADS * MLA_QK), Q_LORA ** -0.5),
        'mla_qn_g': gain(ks[12], (DEPTH, MLA_NOPE)),
        'mla_qr_g': gain(ks[13], (DEPTH, MLA_ROPE)),
        'mla_kva_g': gain(ks[14], (DEPTH, KV_LORA)),
        'mla_kr_g': gain(ks[15], (DEPTH, MLA_ROPE)),
        'mla_w_ukv': nrm(ks[16], (DEPTH, KV_LORA, MLA_HEADS * (MLA_NOPE + MLA_V)), KV_LORA ** -0.5),
        'mla_kn_g': gain(ks[17], (DEPTH, MLA_NOPE)),
        'band_qn_g': gain(ks[18], (DEPTH, BAND_DH)),
        'band_kn_g': gain(ks[19], (DEPTH, BAND_DH)),
        'band_bias': nrm(ks[20], (DEPTH, BAND_HEADS, N_REL), 0.1),
        'w_out': nrm(ks[21], (DEPTH, D_MIX, D_MODEL), D_MIX ** -0.5),
    }


def reference(x_prompt, x_sample, state_ret, cache_mla_ckv, cache_mla_krope, cache_band_k,
              cache_band_v, norm_g, w_in, ret_gn_g, mla_qa_g, mla_w_uq, mla_qn_g, mla_qr_g,
              mla_kva_g, mla_kr_g, mla_w_ukv, mla_kn_g, band_qn_g, band_kn_g, band_bias, w_out):
    past_len = cache_mla_ckv.shape[2]
    pos_p = jnp.arange(x_prompt.shape[1])
    pos_s = past_len + jnp.arange(x_sample.shape[1])
    xp, xs = x_prompt, x_sample
    p_st, s_st = [], []
    for l in range(DEPTH):
        w = (norm_g[l], w_in[l], ret_gn_g[l], mla_qa_g[l], mla_w_uq[l], mla_qn_g[l], mla_qr_g[l],
             mla_kva_g[l], mla_kr_g[l], mla_w_ukv[l], mla_kn_g[l], band_qn_g[l], band_kn_g[l],
             band_bias[l], w_out[l])
        xp, sp = _layer(xp, pos_p, *w, None)
        xs, ss = _layer(xs, pos_s, *w, (state_ret[l], cache_mla_ckv[l], cache_mla_krope[l],
                                        cache_band_k[l], cache_band_v[l]))
        p_st.append(sp)
        s_st.append(ss)
    p_state_ret = jnp.stack([s[0] for s in p_st])
    p_mla_ckv = jnp.stack([s[1] for s in p_st])
    p_mla_krope = jnp.stack([s[2] for s in p_st])
    p_band_k = jnp.stack([s[3] for s in p_st])
    p_band_v = jnp.stack([s[4] for s in p_st])
    s_state_ret = jnp.stack([s[0] for s in s_st])
    s_mla_ckv = jnp.stack([s[1] for s in s_st])
    s_mla_krope = jnp.stack([s[2] for s in s_st])
    s_band_k = jnp.stack([s[3] for s in s_st])
    s_band_v = jnp.stack([s[4] for s in s_st])
    return (xp, xs, p_state_ret, p_mla_ckv, p_mla_krope, p_band_k, p_band_v,
            s_state_ret, s_mla_ckv, s_mla_krope, s_band_k, s_band_v)
```

```python
import contextlib
import numpy as np
import ml_dtypes
import concourse.bass as bass
import concourse.mybir as mybir
from concourse.bass_utils import run_bass_kernel_spmd

F32 = mybir.dt.float32
BF16 = mybir.dt.bfloat16
ALU = mybir.AluOpType
AF = mybir.ActivationFunctionType
AX = mybir.AxisListType

D = 1024
EPS = 1e-6
NCT = 24
CHUNK = 64


class Buf:
    def __init__(self, name):
        self.name = name
        self.w = []
        self.r = []
        self.sem = None
        self.dma_total = 0


class Op:
    __slots__ = ("eng", "fn", "deps", "is_dma", "dsem", "dval", "sig", "sval", "idx", "tag")

    def __init__(self, eng, fn):
        self.eng = eng
        self.fn = fn
        self.deps = []
        self.is_dma = False
        self.dsem = None
        self.dval = 0
        self.sig = False
        self.sval = 0
        self.tag = ''


class Prog:
    ENGS = ("pe", "act", "dve", "pool", "sp")

    def __init__(self):
        self.ops = {e: [] for e in self.ENGS}
        self.dma_bufs = []
        self.out_dmas = []
        self.trace = None

    SKIP_SAME_ENGINE_WAW_WAR = True

    def _dep(self, op, prod, kind="raw"):
        if prod is op:
            return
        if (not prod.is_dma) and prod.eng == "pe" and op.eng == "pe" and not op.is_dma:
            return
        if (self.SKIP_SAME_ENGINE_WAW_WAR and kind != "raw" and (not prod.is_dma) and (not op.is_dma)
                and prod.eng == op.eng):
            return
        op.deps.append(prod)

    def emit(self, eng, fn, reads=(), writes=(), dma_buf=None, is_out=False):
        op = Op(eng, fn)
        for b in reads:
            for p in b.w:
                self._dep(op, p)
        for b in writes:
            for p in b.w:
                self._dep(op, p, "waw")
            for p in b.r:
                self._dep(op, p, "war")
        if dma_buf is not None:
            op.is_dma = True
            if dma_buf.sem is None:
                dma_buf.sem = len(self.dma_bufs)
                self.dma_bufs.append(dma_buf)
            dma_buf.dma_total += 16
            op.dsem = dma_buf
            op.dval = dma_buf.dma_total
        for b in reads:
            b.r.append(op)
        for b in writes:
            b.w = [op]
            b.r = []
        op.tag = ','.join(b.name for b in reads) + '->' + ','.join(b.name for b in writes)
        self.ops[eng].append(op)
        if is_out:
            self.out_dmas.append(op)
        return op

    def finish(self):
        op = Op("sp", None)
        op.deps = list(self.out_dmas)
        self.ops["sp"].append(op)

    def build(self, nc, es):
        n_dma_sems = len(self.dma_bufs)
        dsems = [es.enter_context(nc.semaphore(f"d{i}")) for i in range(n_dma_sems)]
        esems = {e: es.enter_context(nc.semaphore(f"e_{e}")) for e in self.ENGS}
        for e in self.ENGS:
            for op in self.ops[e]:
                for p in op.deps:
                    if not p.is_dma:
                        p.sig = True
        for e in self.ENGS:
            c = 0
            for op in self.ops[e]:
                if op.sig:
                    c += 1
                    op.sval = c
        block = es.enter_context(nc.Block())

        def run(ename, eng):
            waited = {}
            for op in self.ops[ename]:
                need = {}
                for p in op.deps:
                    if p.is_dma:
                        key = ("d", p.dsem.sem)
                        sem = dsems[p.dsem.sem]
                        val = p.dval
                    else:
                        key = ("e", p.eng)
                        sem = esems[p.eng]
                        val = p.sval
                    if waited.get(key, 0) >= val:
                        continue
                    if key not in need or need[key][1] < val:
                        need[key] = (sem, val)
                for key, (sem, val) in need.items():
                    eng.wait_ge(sem, val)
                    waited[key] = val
                if self.trace is not None:
                    self.trace.append((ename, [(k, v[1]) for k, v in need.items()], op.sval if op.sig else None,
                                       (op.dsem.sem, op.dval) if op.is_dma else None, op.tag))
                if op.fn is None:
                    continue
                ins = op.fn(eng)
                if op.is_dma:
                    ins.then_inc(dsems[op.dsem.sem], 16)
                elif op.sig:
                    ins.then_inc(esems[ename], 1)

        @block.tensor
        def _(e):
            run("pe", e)

        @block.scalar
        def _(e):
            run("act", e)

        @block.vector
        def _(e):
            run("dve", e)

        @block.gpsimd
        def _(e):
            run("pool", e)

        @block.sync
        def _(e):
            run("sp", e)


def _rope_tables(pos, dim, rows):
    half = dim // 2
    inv = (10000.0 ** (-np.arange(half, dtype=np.float32) / half)).astype(np.float32)
    ang = pos.astype(np.float32)[None, :] * inv[:, None]
    cos = np.cos(ang).astype(np.float32)
    sin = np.sin(ang).astype(np.float32)
    c = np.concatenate([cos, cos], 0)
    s = np.concatenate([-sin, sin], 0)
    reps = rows // dim
    return np.tile(c, (reps, 1)).copy(), np.tile(s, (reps, 1)).copy()


def _swap_cols(w, dim):
    n = w.shape[-1]
    idx = np.arange(n).reshape(n // dim, dim)
    idx = np.concatenate([idx[:, dim // 2:], idx[:, :dim // 2]], 1).reshape(-1)
    return w[..., idx]


def _ret_gamma():
    return (1.0 - np.exp2(-5.0 - np.arange(4, dtype=np.float64)))


class Cfg:
    def __init__(self, seq=8192, depth=2, past=1024, with_sample=True):
        self.seq = seq
        self.depth = depth
        self.past = past
        self.nblk = seq // 512
        self.with_sample = with_sample
        self.stage = 99
        self.dbg = 0


def build_program(cfg):
    SEQ, DEPTH, NBLK = cfg.seq, cfg.depth, cfg.nblk
    nc = bass.Bass("TRN2", target_bir_lowering=False)
    P = Prog()
    P.trace = [] if getattr(cfg, 'trace', False) else None
    cfg.P = P
    es = contextlib.ExitStack()

    def din(name, shape, dt=F32):
        return nc.dram_tensor(name, list(shape), dt, kind="ExternalInput").ap()

    def dout(name, shape, dt=F32):
        return nc.dram_tensor(name, list(shape), dt, kind="ExternalOutput").ap()

    def dscr(name, shape, dt):
        return nc.dram_tensor(name, list(shape), dt).ap()

    def sb(name, shape, dt):
        return es.enter_context(nc.sbuf_tensor(name, list(shape), dt))

    def ps(name, shape, dt=F32):
        return es.enter_context(nc.psum_tensor(name, list(shape), dt))

    x_p = din("x_p", [SEQ, D])
    W1 = din("W1", [DEPTH, D, NCT * 128])
    WT = din("WT", [DEPTH, D, 1024])
    WUQ = din("WUQ", [DEPTH, 256, 1024])
    WUKV = din("WUKV", [DEPTH, 128, 1024])
    WOUT = din("WOUT", [DEPTH, D, D])
    GN = din("GN", [DEPTH, 128, D])
    GC = din("GC", [DEPTH, 128, 16])
    BT = din("BT", [DEPTH, 128, 5, 4, 128])
    cosA = din("cosA", [128, SEQ + 512])
    sinA = din("sinA", [128, SEQ + 512])
    cosB = din("cosB", [64, SEQ + 512])
    sinB = din("sinB", [64, SEQ + 512])
    c_idb = din("c_idb", [128, 128], BF16)
    c_idf = din("c_idf", [128, 128])
    c_bd64 = din("c_bd64", [128, 128], BF16)
    c_bd32 = din("c_bd32", [128, 128], BF16)
    c_ones = din("c_ones", [128, 128], BF16)
    c_perm64 = din("c_perm64", [128, 128], BF16)
    c_sel = din("c_sel", [128, 2, 128])
    c_mmask = din("c_mmask", [128, 4, 512], BF16)
    c_dec = din("c_dec", [128, 4, 4, 128], BF16)
    c_dq = din("c_dq", [128, 2, 512])
    c_dv = din("c_dv", [128, 4, 4])
    c_onesp = din("c_onesp", [128, 512])
    c_bmk = din("c_bmk", [128, 5, 128])

    x_s = din("x_s", [512, D])
    st_in = din("st_in", [DEPTH, 4, 4, 64, 64])
    cckvT = din("cckvT", [DEPTH, 4, 128, 1024])
    ckrT = din("ckrT", [DEPTH, 4, 32, 1024])
    cbkT = din("cbkT", [DEPTH, 4, 256, 512])
    cbv = din("cbv", [DEPTH, 4, 512, 512])
    BTS = din("BTS", [DEPTH, 128, 3, 4, 128])
    c_bmkS = din("c_bmkS", [128, 3, 128])
    c_dqS = din("c_dqS", [128, 2, 512])
    c_dvS = din("c_dvS", [128, 4, 4])
    c_decS = din("c_decS", [128, 4, 128], BF16)
    c_smask = din("c_smask", [128, 128], BF16)
    y_s = dout("y_s", [128, D])
    s_state = dout("s_state", [DEPTH, 4, 4, 64, 64])
    s_ckv = dout("s_ckv", [DEPTH, 128, 128])
    s_kr = dout("s_kr", [DEPTH, 128, 32])
    s_bk = dout("s_bk", [DEPTH, 128, 256])
    s_bv = dout("s_bv", [DEPTH, 128, 256])
    kTs_d = [dscr(f"kTs_{l}", [8, 96, 512], BF16) for l in range(DEPTH)]
    vs_d = [dscr(f"vs_{l}", [512, 512], BF16) for l in range(DEPTH)]
    kTc_d = [[[dscr(f"kTc_{l}_{e}_{hf}", [8, 96, 512], BF16) for hf in range(2)] for e in range(4)] for l in range(DEPTH)]
    vc_d = [[[dscr(f"vc_{l}_{e}_{hf}", [512, 512], BF16) for hf in range(2)] for e in range(4)] for l in range(DEPTH)]
    ys_d = dscr("ys_d", [128, D], F32)
    B_kTs = [Buf(f"kTs{l}") for l in range(DEPTH)]
    B_vs = [Buf(f"vs{l}") for l in range(DEPTH)]
    B_kTc = [[[Buf(f"kTc{l}{e}{hf}") for hf in range(2)] for e in range(4)] for l in range(DEPTH)]
    B_vc = [[[Buf(f"vc{l}{e}{hf}") for hf in range(2)] for e in range(4)] for l in range(DEPTH)]
    B_ys = Buf("ys")
    y_p = dout("y_p", [SEQ, D])
    o_state = dout("o_state", [DEPTH, 4, 64, 64])
    o_ckv = dout("o_ckv", [DEPTH, SEQ, 128])
    o_kr = dout("o_kr", [DEPTH, SEQ, 32])
    o_bk = dout("o_bk", [DEPTH, 512, 256])
    o_bv = dout("o_bv", [DEPTH, 512, 256])

    kT_d = [[dscr(f"kT_{l}_{j}", [8, 96, 512], BF16) for j in range(NBLK)] for l in range(DEPTH)]
    v_d = [[dscr(f"v_{l}_{j}", [512, 512], BF16) for j in range(NBLK)] for l in range(DEPTH)]
    y1_d = [dscr(f"y1_{j}", [512, D], F32) for j in range(NBLK)]
    CH = (0, 1, 'T', 2, 3)
    wbf_d = {(l, c, hf): dscr(f"wbf_{l}_{c}_{hf}", [128, 8, 512], BF16) for l in range(DEPTH) for c in CH for hf in range(2)}
    B_wbf = {k: Buf(f"wbf{k}") for k in wbf_d}

    B_kT = [[Buf(f"kT{l}{j}") for j in range(NBLK)] for l in range(DEPTH)]
    B_v = [[Buf(f"v{l}{j}") for j in range(NBLK)] for l in range(DEPTH)]
    B_y1 = [Buf(f"y1{j}") for j in range(NBLK)]

    def T(name, shape, dt):
        t = sb(name, shape, dt)
        return t, Buf(name)

    w1h = [T("w1a", [128, 8, 512], BF16), T("w1b", [128, 8, 512], BF16)]
    wuq, b_wuq = T("wuq", [128, 2, 1024], BF16)
    wukv, b_wukv = T("wukv", [128, 1024], BF16)
    stg, b_stg = T("stg", [128, 1024], F32)
    gn, b_gn = T("gn", [128, D], F32)
    gc, b_gc = T("gc", [128, 16], F32)
    idb, b_idb = T("idb", [128, 128], BF16)
    idf, b_idf = T("idf", [128, 128], F32)
    bd64, b_bd64 = T("bd64", [128, 128], BF16)
    bd32, b_bd32 = T("bd32", [128, 128], BF16)
    ones, b_ones = T("ones", [128, 128], BF16)
    perm64, b_perm64 = T("perm64", [128, 128], BF16)
    sel, b_sel = T("sel", [128, 2, 128], F32)
    mmask, b_mmask = T("mmask", [128, 4, 512], BF16)
    dec, b_dec = T("dec", [128, 4, 4, 128], BF16)
    dq, b_dq = T("dq", [128, 2, 512], F32)
    dvt, b_dvt = T("dvt", [128, 4, 4], F32)
    onesp, b_onesp = T("onesp", [128, 512], F32)
    bbias_b, b_bbias_b = T("bbias_b", [128, 5, 4, 128], BF16)
    bmk, b_bmk = T("bmk", [128, 5, 128], F32)
    rbs, b_rbs = T("rbs", [128, 512], F32)

    decS, b_decS = T("decS", [128, 4, 128], BF16)
    smask, b_smask = T("smask", [128, 128], BF16)
    dvS, b_dvS = T("dvS", [128, 4, 4], F32)
    vdecS, b_vdecS = T("vdecS", [128, 4, 512], BF16)
    stS, b_stS = T("stS", [64, 16, 64], F32)
    stpadS, b_stpadS = T("stpadS", [128, 4, 2, 128], BF16)
    bbiasS_b, b_bbiasS_b = T("bbiasS_b", [128, 3, 4, 128], BF16)
    bmkS, b_bmkS = T("bmkS", [128, 3, 128], F32)
    epsb, b_epsb = T("epsb", [128, 1], F32)
    dmy, b_dmy = T("dmy", [128, 1], F32)
    xres, b_xres = T("xres", [128, 4, D], F32)
    b_xr = [Buf(f"xres{t_}") for t_ in range(4)]
    hb, b_hb = T("hb", [128, D], BF16)
    junk, b_junk = T("junk", [128, D], BF16)
    ss1, b_ss1 = T("ss1", [128, 8], F32)
    b_ss1t = [Buf(f"ss1_{t_}") for t_ in range(4)]
    hT, b_hT = T("hT", [128, 8, 512], BF16)
    tcos, b_tcos = T("tcos", [128, 512], F32)
    tsin, b_tsin = T("tsin", [128, 512], F32)
    tcosB, b_tcosB = T("tcosB", [64, 512], F32)
    tsinB, b_tsinB = T("tsinB", [64, 512], F32)
    tA, b_tA = T("tA", [128, 512], F32)
    tB, b_tB = T("tB", [128, 512], F32)
    sq, b_sq = T("sq", [128, 512], BF16)
    sq2, b_sq2 = T("sq2", [128, 512], BF16)
    rstd, b_rstd = T("rstd", [128, 512], F32)
    qTr, b_qTr = T("qTr", [128, 2, 512], BF16)
    qTd, b_qTd = T("qTd", [128, 2, 512], BF16)
    kTr, b_kTr = T("kTr", [128, 2, 512], BF16)
    k_tm, b_k_tm = T("k_tm", [128, 4, 256], BF16)
    vpad, b_vpad = T("vpad", [128, 4, 4, 128], BF16)
    vdec, b_vdec = T("vdec", [128, 4, 512], BF16)
    gates, b_gates = T("gates", [128, 8, 512], BF16)
    cqT, b_cqT = T("cqT", [128, 2, 512], BF16)
    ckvf, b_ckvf = T("ckvf", [128, 512], F32)
    ckvb, b_ckvb = T("ckvb", [128, 512], BF16)
    krf, b_krf = T("krf", [32, 512], F32)
    qTm, b_qTm = T("qTm", [96, 8, 512], BF16)
    kTm, b_kTm = T("kTm", [96, 8, 512], BF16)
    vtm, b_vtm = T("vtm", [128, 4, 512], BF16)
    qTb, b_qTb = T("qTb", [128, 2, 512], BF16)
    qTbz, b_qTbz = T("qTbz", [128, 4, 512], BF16)
    kTb = [T(f"kTb{i}", [128, 2, 512], BF16) for i in range(2)]
    kbf, b_kbf = T("kbf", [128, 2, 512], F32)
    rinv, b_rinv = kbf, b_kbf
    vbf, b_vbf = T("vbf", [128, 4, 256], F32)
    vba = [T(f"vba{i}", [128, 4, 4, 128], BF16) for i in range(2)]
    ktb = [T(f"ktb{i}", [96, 512], BF16) for i in range(2)]
    vtb = [[T(f"vtb{p}{i}", [128, 4, 128], BF16) for i in range(2)] for p in range(2)]
    pT = [T(f"pT{i}", [128, 512], BF16) for i in range(2)]
    PT_EXTRA = True
    tmpm, b_tmpm = T("tmpm", [128, 512], F32)
    mixT, b_mixT = hT, b_hT
    innT, b_innT = T("innT", [128, 512], BF16)
    state, b_state = T("state", [64, 4, 64], F32)
    otr, b_otr = T("otr", [128, 512], F32)

    pz = [(ps(f"pz{i}", [128, 512]), Buf(f"pz{i}")) for i in range(2)]
    ptr, b_ptr = ps("ptr", [128, 8, 128], BF16), Buf("ptr")
    pS = [(ps(f"pS{i}", [128, 512]), Buf(f"pS{i}")) for i in range(2)]
    pO = [(ps(f"pO{i}", [128, 512]), Buf(f"pO{i}")) for i in range(2)]
    pn, b_pn = ps("pn", [128, 512]), Buf("pn")
    ring4 = [pz[0], pz[1], pS[0], pS[1]]
    pS = ring4

    pT = pT + [(innT, b_innT), (sq2, b_sq2)]
    dq_rr = [0]

    def dma(out, in_, reads, writes, sbuf_buf, is_out=False, eng=None):
        if eng is None:
            eng = ("sp", "pool")[dq_rr[0] % 2] if False else "sp"
        return P.emit(eng, lambda e: e.dma_start(out=out, in_=in_), reads=reads, writes=writes,
                      dma_buf=sbuf_buf, is_out=is_out)

    def mm(out, lhsT, rhs, start, stop, reads, writes):
        return P.emit("pe", lambda e: e.matmul(out, lhsT, rhs, start=start, stop=stop),
                      reads=reads, writes=writes)

    def tr(out, in_, ident, reads, writes):
        return P.emit("pe", lambda e: e.transpose(out, in_, ident), reads=reads, writes=writes)

    def act(out, in_, func, reads, writes, scale=1.0, bias=0.0, accum=None):
        def fn(e):
            kw = {}
            if accum is not None:
                kw["accum_out"] = accum
            return e.activation(out, in_, func, bias=bias, scale=scale, **kw)
        return P.emit("act", fn, reads=reads, writes=writes)

    def tt(eng, out, in0, in1, op, reads, writes):
        return P.emit(eng, lambda e: e.tensor_tensor(out=out, in0=in0, in1=in1, op=op),
                      reads=reads, writes=writes)

    def tsc(eng, out, in0, s1, s2, op0, op1, reads, writes):
        if op1 == ALU.pow:
            P.emit("act", lambda e: e.activation(out, in0, AF.Ln, bias=epsb[0:out.shape[0], 0:1], scale=1.0),
                   reads=list(reads) + [b_epsb], writes=writes)
            return P.emit("act", lambda e: e.activation(out, out, AF.Exp, scale=-0.5), reads=writes, writes=writes)

        def fn(e):
            if op1 is None:
                return e.tensor_scalar(out=out, in0=in0, scalar1=s1, scalar2=None, op0=op0)
            return e.tensor_scalar(out=out, in0=in0, scalar1=s1, scalar2=s2, op0=op0, op1=op1)
        return P.emit(eng, fn, reads=reads, writes=writes)

    def stt(eng, out, in0, scalar, in1, op0, op1, reads, writes):
        return P.emit(eng, lambda e: e.scalar_tensor_tensor(out=out, in0=in0, scalar=scalar, in1=in1,
                                                            op0=op0, op1=op1),
                      reads=reads, writes=writes)

    def cp(eng, out, in_, reads, writes):
        if eng == "act":
            return P.emit(eng, lambda e: e.copy(out, in_), reads=reads, writes=writes)
        return P.emit(eng, lambda e: e.tensor_copy(out=out, in_=in_), reads=reads, writes=writes)

    def memset(eng, ap, val, writes):
        return P.emit(eng, lambda e: e.memset(ap, val), reads=(), writes=writes)

    pz_rr = [0]

    def next_pz():
        i = pz_rr[0] % 4
        pz_rr[0] += 1
        return ring4[i]

    Bc = Buf("consts_dram")
    for t_, b_, src in ((idb, b_idb, c_idb), (idf, b_idf, c_idf), (bd64, b_bd64, c_bd64), (bd32, b_bd32, c_bd32),
                        (ones, b_ones, c_ones), (perm64, b_perm64, c_perm64), (sel, b_sel, c_sel), (mmask, b_mmask, c_mmask),
                        (dec, b_dec, c_dec), (dvt, b_dvt, c_dv), (bmk, b_bmk, c_bmk), (onesp, b_onesp, c_onesp),
                        (decS, b_decS, c_decS), (smask, b_smask, c_smask), (dvS, b_dvS, c_dvS), (bmkS, b_bmkS, c_bmkS)):
        dma(t_[:], src, [Bc], [b_], b_)
    memset("dve", qTbz[:], 0.0, [b_qTbz])
    memset("dve", epsb[:], EPS, [b_epsb])
    for i in range(2):
        memset("dve", vtb[0][i][0][:, :, 64:128], 1.0, [vtb[0][i][1]])
        memset("dve", vtb[1][i][0][:, :, 0:64], 1.0, [vtb[1][i][1]])

    def load_weights(l):
        def chunked(dst, dst_buf, src2d, rows_tiles, cols):
            for k in range(rows_tiles):
                for c0 in range(0, cols, 1024):
                    cw = min(1024, cols - c0)
                    dma(stg[:, 0:cw], src2d[k * 128:(k + 1) * 128, c0:c0 + cw], [Bc], [b_stg], b_stg)
                    if len(dst.shape) == 3:
                        o = dst[:, k, c0:c0 + cw]
                    else:
                        o = dst[:, c0:c0 + cw]
                    cp("pool", o, stg[:, 0:cw], [b_stg], [dst_buf])
        chunked(wuq, b_wuq, WUQ[l], 2, 1024)
        chunked(wukv, b_wukv, WUKV[l], 1, 1024)
        precast_weights(l)
        dma(gn[:], GN[l], [Bc], [b_gn], b_gn)
        dma(gc[:], GC[l], [Bc], [b_gc], b_gc)
        dma(xres[:, 0:3, :].rearrange("p a b -> p (a b)")[:, 0:2560], BT[l].rearrange("p r h q -> p (r h q)"),
            [Bc], b_xr[0:3], b_xr[0])
        xv = xres[:, 0:3, :].rearrange("p a b -> p (a b)")[:, 0:2560].rearrange("p (r h q) -> p r h q", r=5, h=4)
        for h in range(4):
            stt("dve", bbias_b[:, :, h, :], xv[:, :, h, :], 8.0, bmk[:, :, :], ALU.mult, ALU.add,
                b_xr[0:3] + [b_bmk], [b_bbias_b])

        dma(xres[:, 0:2, :].rearrange("p a b -> p (a b)")[:, 0:1536], BTS[l].rearrange("p r h q -> p (r h q)"),
            [Bc], b_xr[0:2], b_xr[0])
        xv2 = xres[:, 0:2, :].rearrange("p a b -> p (a b)")[:, 0:1536].rearrange("p (r h q) -> p r h q", r=3, h=4)
        for h in range(4):
            stt("dve", bbiasS_b[:, :, h, :], xv2[:, :, h, :], 8.0, bmkS[:, :, :], ALU.mult, ALU.add,
                b_xr[0:2] + [b_bmkS], [b_bbiasS_b])

    cur_half = [None, None]

    def need_half(l, c, hf):
        t_, b_ = w1h[hf]
        if cur_half[hf] != (l, c):
            cur_half[hf] = (l, c)
            dma(t_[:], wbf_d[(l, c, hf)], [B_wbf[(l, c, hf)]], [b_], b_)
        return t_, b_

    b_stgh = [Buf("stgA"), Buf("stgB")]

    def stg_barrier():
        P.emit("dve", lambda e: e.memset(dmy[:], 0.0), reads=(), writes=[b_dmy, b_stg, b_stgh[0], b_stgh[1]])

    def precast_weights(l):
        stg_barrier()
        for c in CH:
            src = WOUT[l] if c == 3 else (WT[l] if c == 'T' else W1[l][:, c * 1024:(c + 1) * 1024])
            for k in range(8):
                for hf in range(2):
                    dma(stg[:, hf * 512:(hf + 1) * 512], src[k * 128:(k + 1) * 128, hf * 512:(hf + 1) * 512],
                        [Bc], [b_stgh[hf]], b_stgh[hf])
                    cp("dve", w1h[hf][0][:, k, :], stg[:, hf * 512:(hf + 1) * 512], [b_stgh[hf]], [w1h[hf][1]])
            for hf in range(2):
                dma(wbf_d[(l, c, hf)], w1h[hf][0][:], [w1h[hf][1]], [B_wbf[(l, c, hf)]], w1h[hf][1])
                cur_half[hf] = (l, c)
        stg_barrier()

    GCOL = dict(qa0=0, qa1=1, qn=2, qr=3, kva=4, kr=5, kn=6, bq=7, bk=8, rg0=9, rg1=10)

    rstd_ring = [(rstd, b_rstd), (stg[:, 0:512], b_stg)]
    sq_ring = [(sq, b_sq), (junk[:, 0:512], b_junk)]
    hn_cnt = [0]

    def headnorm(raw, rows, N, bdmat, b_bd, n, reads_raw):
        nonlocal rstd, b_rstd
        hn_cnt[0] += 1
        rstd, b_rstd = rstd_ring[hn_cnt[0] % 2]
        sq_t, b_sq_t = sq_ring[hn_cnt[0] % 2]
        act(sq_t[0:rows, 0:N], raw, AF.Square, reads_raw, [b_sq_t], scale=float(n) ** -0.5)
        mm(pn[0:rows, 0:N], bdmat[0:rows, 0:rows], sq_t[0:rows, 0:N], True, True, [b_bd, b_sq_t], [b_pn])
        tsc("dve", rstd[0:rows, 0:N], pn[0:rows, 0:N], EPS, -0.5, ALU.add, ALU.pow, [b_pn], [b_rstd])

    def out_T(src_f32, rows, tok0, ntok, dst_rows_ap, reads):
        pzt, b_pzt = next_pz()
        tr(pzt[0:ntok, 0:rows], src_f32, idf[0:rows, 0:rows], reads + [b_idf], [b_pzt])
        cp("dve", otr[0:ntok, 0:rows], pzt[0:ntok, 0:rows], [b_pzt], [b_otr])
        dma(dst_rows_ap, otr[0:ntok, 0:rows], [b_otr], [], b_otr, is_out=True)

    def gen_kv(dstK, BK, dstV, BV):
        N = 512
        for pr in range(4):
            pzt, b_pzt = next_pz()
            mm(pzt[:, 0:N], wukv[:, pr * 128:(pr + 1) * 128], ckvb[:], True, True, [b_wukv, b_ckvb], [b_pzt])
            headnorm(pzt[:, 0:N], 128, N, bd64, b_bd64, 64, [b_pzt])
            for e in range(2):
                stt("dve", kTm[0:64, 2 * pr + e, :], pzt[64 * e:64 * e + 64, 0:N], gc[64 * e:64 * e + 64, 6:7],
                    rstd[64 * e:64 * e + 64, 0:N], ALU.mult, ALU.mult, [b_pzt, b_gc, b_rstd], [b_kTm])
        for t in range(4):
            pzt, b_pzt = next_pz()
            mm(pzt[:, :], ckvb[:, t * 128:(t + 1) * 128], wukv[:, 512:1024], True, True, [b_ckvb, b_wukv], [b_pzt])
            cp("dve", vtm[:, t, :], pzt[:, :], [b_pzt], [b_vtm])
        dma(dstK.rearrange("h r t -> r h t"), kTm[:], [b_kTm], [BK], b_kTm)
        dma(dstV.rearrange("(t p) c -> p t c", p=128), vtm[:], [b_vtm], [BV], b_vtm)


    def prompt_block(l, j, sample=False):
        N = 512
        t0 = SEQ if sample else j * 512
        bi = 1 if sample else j % 2
        dma(dq[:], c_dqS if sample else c_dq, [Bc], [b_dq], b_dq)
        dma(tcos[:], cosA[:, t0:t0 + N], [Bc], [b_tcos], b_tcos)
        dma(tsin[:], sinA[:, t0:t0 + N], [Bc], [b_tsin], b_tsin)
        dma(tcosB[:], cosB[:, t0:t0 + N], [Bc], [b_tcosB], b_tcosB)
        dma(tsinB[:], sinB[:, t0:t0 + N], [Bc], [b_tsinB], b_tsinB)
        for t in range(4):
            if sample and (l == 0 or t > 0):
                dma(xres[:, t, :], x_s[t * 128:(t + 1) * 128, :], [Bc], [b_xr[t]], b_xr[t])
            elif sample:
                dma(xres[:, t, :], ys_d[:, :], [B_ys], [b_xr[t]], b_xr[t])
            elif l == 0:
                dma(xres[:, t, :], x_p[t0 + t * 128:t0 + (t + 1) * 128, :], [Bc], [b_xr[t]], b_xr[t])
            else:
                dma(xres[:, t, :], y1_d[j][t * 128:(t + 1) * 128, :], [B_y1[j]], [b_xr[t]], b_xr[t])
            act(junk[:], xres[:, t, :], AF.Square, [b_xr[t]], [b_junk, b_ss1t[t]], scale=1.0 / 32.0,
                accum=ss1[:, 2 * t:2 * t + 1])
            tsc("dve", ss1[:, 2 * t + 1:2 * t + 2], ss1[:, 2 * t:2 * t + 1], EPS, -0.5, ALU.add, ALU.pow,
                [b_ss1t[t]], [b_ss1t[t]])
            stt("dve", hb[:], xres[:, t, :], ss1[:, 2 * t + 1:2 * t + 2], gn[:], ALU.mult, ALU.mult,
                [b_xr[t], b_ss1t[t], b_gn], [b_hb])
            for g in range(2):
                for kk in range(4):
                    k = g * 4 + kk
                    tr(ptr[:, kk, :], hb[:, k * 128:(k + 1) * 128], idb[:], [b_hb, b_idb], [b_ptr])
                cp("dve", hT[:, g * 4:(g + 1) * 4, t * 128:(t + 1) * 128], ptr[:, 0:4, :],
                   [b_ptr], [b_hT])

        if cfg.stage < 2:
            return

        issued = {}
        runs = [[0, 1, 4, 5], list(range(8, 16)), [16, 17], [20, 22, 21, 23]]
        nxt_of = {}
        for r_ in runs:
            for a_, b__ in zip(r_[:-1], r_[1:]):
                nxt_of[a_] = b__

        def issue(ct):
            cc = ct % 8
            wq, b_wq = need_half(l, ct // 8, cc // 4)
            c4 = cc % 4
            pzt, b_pzt = next_pz()
            for k in range(8):
                mm(pzt[:, 0:N], wq[:, k, c4 * 128:(c4 + 1) * 128], hT[:, k, 0:N], k == 0, k == 7,
                   [b_wq, b_hT], [b_pzt])
            issued[ct] = (pzt, b_pzt)

        def inproj(ct):
            if ct not in issued:
                issue(ct)
            if ct in nxt_of and nxt_of[ct] not in issued:
                issue(nxt_of[ct])
            return issued.pop(ct)

        for which, (dst, b_dst) in enumerate(((qTr, b_qTr), (kTr, b_kTr))):
            for pr in range(2):
                praw, b_praw = inproj(which * 4 + pr)
                cp("dve", ckvb[:], praw[:, 0:N], [b_praw], [b_ckvb])
                tt("dve", tA[:], praw[:, 0:N], tcos[:], ALU.mult, [b_praw, b_tcos], [b_tA])
                pswp, b_pswp = next_pz()
                mm(pswp[:, 0:N], perm64[:], ckvb[:], True, True, [b_perm64, b_ckvb], [b_pswp])
                tt("dve", tB[:], pswp[:, 0:N], tsin[:], ALU.mult, [b_pswp, b_tsin], [b_tB])
                tt("dve", dst[:, pr, :], tA[:], tB[:], ALU.add, [b_tA, b_tB], [b_dst])
        for pr in range(2):
            tt("dve", qTd[:, pr, :], qTr[:, pr, :], dq[:, pr, :], ALU.mult, [b_qTr, b_dq], [b_qTd])
        for t in range(4):
            for pr in range(2):
                tr(ptr[:, pr, :], kTr[:, pr, t * 128:(t + 1) * 128], idb[:], [b_kTr, b_idb], [b_ptr])
            cp("dve", k_tm[:, t, :], ptr[:, 0:2, :], [b_ptr], [b_k_tm])
        if cfg.stage < 2.1:
            return
        for ft, ct in enumerate(range(8, 16)):
            pg, b_pg = inproj(ct)
            act(gates[:, ft, :], pg[:, 0:N], AF.Silu, [b_pg], [b_gates])
        if cfg.stage < 2.3:
            return
        vb_t, b_vb = vba[bi]
        def issue_T(hf_, t_):
            wq_, b_wq_ = need_half(l, 'T', hf_)
            pz_, b_pz_ = next_pz()
            for k in range(8):
                mm(pz_[:, :], hT[:, k, t_ * 128:(t_ + 1) * 128], wq_[:, k, :], k == 0, k == 7,
                   [b_hT, b_wq_], [b_pz_])
            return pz_, b_pz_
        seqT = [(hf_, t_) for hf_ in range(2) for t_ in range(4)]
        pendT = {seqT[0]: issue_T(*seqT[0])}

        def get_T(hf_, t_):
            i_ = seqT.index((hf_, t_))
            if i_ + 1 < len(seqT):
                pendT[seqT[i_ + 1]] = issue_T(*seqT[i_ + 1])
            return pendT.pop((hf_, t_))
        for t in range(4):
            pzA, b_pzA = get_T(0, t)
            tsc("dve", vpad[:, t, :, :].rearrange("p h c -> p (h c)"), pzA[:, :], 0.125, None, ALU.mult, None,
                [b_pzA], [b_vpad])
            for h in range(4):
                tsc("dve", vdec[:, t, h * 128:(h + 1) * 128], pzA[:, h * 128:(h + 1) * 128], dvt[:, t, h:h + 1],
                    None, ALU.mult, None, [b_pzA, b_dvt], [b_vdec])
            if sample and t == 0:
                for e_ in range(4):
                    for h in range(4):
                        tsc("dve", vdecS[:, e_, h * 128:(h + 1) * 128], pzA[:, h * 128:(h + 1) * 128],
                            dvS[:, e_, h:h + 1], None, ALU.mult, None, [b_pzA, b_dvS], [b_vdecS])
        for t in range(4):
            pzB, b_pzB = get_T(1, t)
            tt("dve", vb_t[:, t, :, :].rearrange("p h c -> p (h c)"), pzB[:, :], onesp[:], ALU.add,
               [b_pzB, b_onesp], [b_vb])
            if (j == NBLK - 1 and not sample) or (sample and t == 0):
                for h in range(4):
                    lo = 0 if h % 2 == 0 else 64
                    cp("dve", vbf[:, t, h * 64:(h + 1) * 64], pzB[:, h * 128 + lo:h * 128 + lo + 64],
                       [b_pzB], [b_vbf])
        if cfg.stage < 2.6:
            return
        pc0, b_pc0 = inproj(16)
        act(sq[:, 0:N], pc0[:, 0:N], AF.Square, [b_pc0], [b_sq], scale=1.0 / 16.0)
        mm(pn[:, 0:N], ones[:], sq[:, 0:N], True, False, [b_ones, b_sq], [b_pn])
        pc1, b_pc1 = inproj(17)
        act(sq2[:, 0:N], pc1[:, 0:N], AF.Square, [b_pc1], [b_sq2], scale=1.0 / 16.0)
        mm(pn[:, 0:N], ones[:], sq2[:, 0:N], False, True, [b_ones, b_sq2], [b_pn])
        tsc("dve", rstd[:, 0:N], pn[:, 0:N], EPS, -0.5, ALU.add, ALU.pow, [b_pn], [b_rstd])
        stt("dve", cqT[:, 0, :], pc0[:, 0:N], gc[:, 0:1], rstd[:, 0:N], ALU.mult, ALU.mult,
            [b_pc0, b_gc, b_rstd], [b_cqT])
        stt("dve", cqT[:, 1, :], pc1[:, 0:N], gc[:, 1:2], rstd[:, 0:N], ALU.mult, ALU.mult,
            [b_pc1, b_gc, b_rstd], [b_cqT])

        def uq(c0, M):
            pzt, b_pzt = next_pz()
            for k in range(2):
                mm(pzt[0:M, 0:N], wuq[:, k, c0:c0 + M], cqT[:, k, :], k == 0, k == 1, [b_wuq, b_cqT], [b_pzt])
            return pzt, b_pzt
        for pr in range(4):
            pq, b_pq = uq(pr * 128, 128)
            headnorm(pq[:, 0:N], 128, N, bd64, b_bd64, 64, [b_pq])
            for e in range(2):
                stt("dve", qTm[0:64, 2 * pr + e, :], pq[64 * e:64 * e + 64, 0:N], gc[64 * e:64 * e + 64, 2:3],
                    rstd[64 * e:64 * e + 64, 0:N], ALU.mult, ALU.mult, [b_pq, b_gc, b_rstd], [b_qTm])
        for rp in range(4):
            pq, b_pq = uq(512 + rp * 64, 64)
            headnorm(pq[0:64, 0:N], 64, N, bd32, b_bd32, 32, [b_pq])
            stt("dve", tA[0:64, :], pq[0:64, 0:N], gc[0:64, 3:4], rstd[0:64, 0:N], ALU.mult, ALU.mult,
                [b_pq, b_gc, b_rstd], [b_tA])
            tt("dve", tA[0:64, :], tA[0:64, :], tcosB[:], ALU.mult, [b_tA, b_tcosB], [b_tA])
            pq2, b_pq2 = uq(768 + rp * 64, 64)
            stt("dve", tB[0:64, :], pq2[0:64, 0:N], gc[0:64, 11:12], rstd[0:64, 0:N], ALU.mult, ALU.mult,
                [b_pq2, b_gc, b_rstd], [b_tB])
            tt("dve", tB[0:64, :], tB[0:64, :], tsinB[:], ALU.mult, [b_tB, b_tsinB], [b_tB])
            for e in range(2):
                tt("dve", qTm[64:96, 2 * rp + e, :], tA[32 * e:32 * e + 32, :], tB[32 * e:32 * e + 32, :], ALU.add,
                   [b_tA, b_tB], [b_qTm])
        if cfg.stage < 4:
            return
        pk, b_pk = inproj(18)
        headnorm(pk[:, 0:N], 128, N, ones, b_ones, 128, [b_pk])
        stt("dve", ckvf[:], pk[:, 0:N], gc[:, 4:5], rstd[:, 0:N], ALU.mult, ALU.mult, [b_pk, b_gc, b_rstd], [b_ckvf])
        cp("dve", ckvb[:], ckvf[:], [b_ckvf], [b_ckvb])
        for t in range(1 if sample else 4):
            out_T(ckvf[:, t * 128:(t + 1) * 128], 128, t0 + t * 128, 128,
                  s_ckv[l] if sample else o_ckv[l, t0 + t * 128:t0 + (t + 1) * 128, :], [b_ckvf])
        pr_, b_pr = inproj(19)
        headnorm(pr_[0:64, 0:N], 64, N, bd32, b_bd32, 32, [b_pr])
        stt("dve", tA[0:64, :], pr_[0:64, 0:N], gc[0:64, 5:6], rstd[0:64, 0:N], ALU.mult, ALU.mult,
            [b_pr, b_gc, b_rstd], [b_tA])
        tt("dve", tB[0:32, :], tA[0:32, :], tcosB[0:32, :], ALU.mult, [b_tA, b_tcosB], [b_tB])
        tt("dve", tB[32:64, :], tA[32:64, :], tsinB[32:64, :], ALU.mult, [b_tA, b_tsinB], [b_tB])
        cp("dve", tA[0:32, :], tB[32:64, :], [b_tB], [b_tA])
        tt("dve", krf[:], tB[0:32, :], tA[0:32, :], ALU.add, [b_tA, b_tB], [b_krf])
        for h in range(8):
            cp("pool", kTm[64:96, h, :], krf[:], [b_krf], [b_kTm])
        for t in range(1 if sample else 4):
            out_T(krf[:, t * 128:(t + 1) * 128], 32, t0 + t * 128, 128,
                  s_kr[l] if sample else o_kr[l, t0 + t * 128:t0 + (t + 1) * 128, :], [b_krf])
        if sample:
            gen_kv(kTs_d[l], B_kTs[l], vs_d[l], B_vs[l])
        else:
            gen_kv(kT_d[l][j], B_kT[l][j], v_d[l][j], B_v[l][j])
        if cfg.stage < 5:
            return
        kTb_t, b_kTb = kTb[bi]
        for pr in range(2):
            pq, b_pq = inproj(20 + pr)
            headnorm(pq[:, 0:N], 128, N, bd64, b_bd64, 64, [b_pq])
            for e in range(2):
                stt("dve", qTbz[64 * e:64 * e + 64, 2 * pr + e, :], pq[64 * e:64 * e + 64, 0:N],
                    gc[64 * e:64 * e + 64, 7:8], rstd[64 * e:64 * e + 64, 0:N], ALU.mult, ALU.mult,
                    [b_pq, b_gc, b_rstd], [b_qTbz])
            pk2, b_pk2 = inproj(22 + pr)
            headnorm(pk2[:, 0:N], 128, N, bd64, b_bd64, 64, [b_pk2])
            stt("dve", kbf[:, pr, :], pk2[:, 0:N], gc[:, 8:9], rstd[:, 0:N], ALU.mult, ALU.mult,
                [b_pk2, b_gc, b_rstd], [b_kbf])
            cp("pool", kTb_t[:, pr, :], kbf[:, pr, :], [b_kbf], [b_kTb])
        if sample:
            for pr in range(2):
                out_T(kbf[:, pr, 0:128], 128, 0, 128, s_bk[l, :, pr * 128:(pr + 1) * 128], [b_kbf])
            dma(s_bv[l, :, :], vbf[:, 0, :], [b_vbf], [], b_vbf, is_out=True)
            sample_mixers(l)
            return
        if j == NBLK - 1:
            for t in range(4):
                for pr in range(2):
                    out_T(kbf[:, pr, t * 128:(t + 1) * 128], 128, 0, 128,
                          o_bk[l, t * 128:(t + 1) * 128, pr * 128:(pr + 1) * 128], [b_kbf])
                dma(o_bv[l, t * 128:(t + 1) * 128, :], vbf[:, t, :], [b_vbf], [], b_vbf, is_out=True)

        if cfg.stage < 5.1:
            return
        if j == 0:
            memset("dve", state[:], 0.0, [b_state])
            memset("dve", stpad_t[:], 0.0, [b_stpad])
        for pr in range(2):
            po_t, b_po = pO[pr % 2]
            first = True
            for e in range(2):
                h = 2 * pr + e
                r0 = 64 * e
                for kt in range(4):
                    psT, b_psT = pS[(h * 4 + kt) % 2]
                    l0 = kt * 128
                    mm(psT[:, l0:N], kTr[r0:r0 + 64, pr, l0:l0 + 128], qTr[r0:r0 + 64, pr, l0:N], True, True,
                       [b_kTr, b_qTr], [b_psT])
                    for lt in range(kt, 4):
                        tt("dve", innT[:, lt * 128:(lt + 1) * 128], psT[:, lt * 128:(lt + 1) * 128],
                           dec[:, h, lt - kt, :], ALU.mult, [b_psT, b_dec], [b_innT])
                    mm(po_t[:, l0:N], vpad[:, kt, h, :], innT[:, l0:N], first and kt == 0, False,
                       [b_vpad, b_innT], [b_po])
                    first = False
            if cfg.stage < 5.4:
                continue
            P_state_term(pr, po_t, b_po)
            if cfg.stage < 5.6:
                continue
            headnorm(po_t[:, 0:N], 128, N, bd64, b_bd64, 64, [b_po])
            stt("dve", tmpm[:], po_t[:, 0:N], gc[:, 9 + pr:10 + pr], rstd[:, 0:N], ALU.mult, ALU.mult,
                [b_po, b_gc, b_rstd], [b_tmpm])
            tt("dve", mixT[:, pr, :], tmpm[:], gates[:, pr, :], ALU.mult, [b_tmpm, b_gates], [b_mixT])
        if cfg.stage < 5.8:
            return
        gam = _ret_gamma()
        for h in range(4):
            pzt, b_pzt = next_pz()
            for t in range(4):
                mm(pzt[0:64, 0:64], k_tm[:, t, h * 64:(h + 1) * 64],
                   vdec[:, t, h * 128 + 64 * (h % 2):h * 128 + 64 * (h % 2) + 64],
                   t == 0, t == 3, [b_k_tm, b_vdec], [b_pzt])
            stt("dve", state[:, h, :], state[:, h, :], float(gam[h] ** 512), pzt[0:64, 0:64], ALU.mult, ALU.add,
                [b_state, b_pzt], [b_state])
            cp("dve", stpad_t[64 * (h % 2):64 * (h % 2) + 64, h // 2, 64 * (h % 2):64 * (h % 2) + 64],
               state[:, h, :], [b_state], [b_stpad])
        if j == NBLK - 1 and cfg.stage >= 5.9:
            dma(o_state[l].rearrange("h k v -> k h v"), state[:], [b_state], [], b_state, is_out=True)
        if cfg.stage < 7:
            return
        qscale = 96.0 ** -0.5
        cnt = [0]
        SKEW = 3
        items = [(h, kb, kt) for h in range(8) for kb in range(j + 1) for kt in range(4)]
        loaded = {}
        sbuf_of = {}

        def emit_S(idx):
            h, kb, kt = items[idx]
            if kt == 0:
                kt_t, b_kt = ktb[cnt[0] % 2]
                vt_t, b_vt = vtb[h % 2][(cnt[0] // 1) % 2]
                cnt[0] += 1
                dma(kt_t[:, :], kT_d[l][kb][h], [B_kT[l][kb]], [b_kt], b_kt)
                lo = 0 if h % 2 == 0 else 64
                dma(vt_t[:, :, lo:lo + 64],
                    v_d[l][kb].rearrange("(t p) c -> p t c", p=128)[:, :, h * 64:(h + 1) * 64],
                    [B_v[l][kb]], [b_vt], b_vt)
                loaded[(h, kb)] = (kt_t, b_kt, vt_t, b_vt)
            kt_t, b_kt, vt_t, b_vt = loaded[(h, kb)]
            psT, b_psT = pS[idx % 4]
            pT_t, b_pT = pT[idx % 4]
            mm(psT[:, 0:N], kt_t[:, kt * 128:(kt + 1) * 128], qTm[:, h, :], True, True, [b_kt, b_qTm], [b_psT])
            act(pT_t[:, 0:N], psT[:, 0:N], AF.Exp, [b_psT], [b_pT], scale=qscale)
            if kb == j:
                tt("dve", pT_t[:, 0:N], pT_t[:, 0:N], mmask[:, kt, :], ALU.mult, [b_pT, b_mmask], [b_pT])

        def emit_PV(idx):
            h, kb, kt = items[idx]
            kt_t, b_kt, vt_t, b_vt = loaded[(h, kb)]
            po_t, b_po = pO[h % 2]
            pT_t, b_pT = pT[idx % 4]
            first = (kb == 0 and kt == 0)
            last = (kb == j and kt == 3)
            mm(po_t[:, 0:N], vt_t[:, kt, :], pT_t[:, 0:N], first, last, [b_vt, b_pT], [b_po])
            if last and h % 2 == 1:
                finalize_pair(2 + h // 2)
        for i in range(len(items) + SKEW):
            if i < len(items):
                emit_S(i)
            if i >= SKEW:
                emit_PV(i - SKEW)
        bscale = 0.125
        kcur, b_kcur = kTb[j % 2]
        kprev, b_kprev = kTb[(j - 1) % 2]
        vcur, b_vcur = vba[j % 2]
        vprev, b_vprev = vba[(j - 1) % 2]
        groups = []
        for h in range(4):
            for qt in range(4):
                units = []
                if j > 0:
                    for i in range(qt, 4):
                        units.append((kprev, b_kprev, vprev, b_vprev, i, i - 4 - qt))
                for i in range(0, qt + 1):
                    units.append((kcur, b_kcur, vcur, b_vcur, i, i - qt))
                for g0 in range(0, len(units), 4):
                    groups.append((h, qt, g0, units[g0:g0 + 4], len(units)))

        def band_S(gi):
            h, qt, g0, grp, nun = groups[gi]
            pr = h // 2
            psT, b_psT = pS[gi % 4]
            pT_t, b_pT = pT[gi % 4]
            for ui, (kt_, b_k_, v_, b_v_, i, r) in enumerate(grp):
                mm(psT[:, ui * 128:(ui + 1) * 128], kt_[:, pr, i * 128:(i + 1) * 128],
                   qTbz[:, h, qt * 128:(qt + 1) * 128], True, False, [b_k_, b_qTbz], [b_psT])
                mm(psT[:, ui * 128:(ui + 1) * 128], idb[:], bbias_b[:, r + 4, h, :], False, True,
                   [b_idb, b_bbias_b], [b_psT])
            w = len(grp) * 128
            act(pT_t[:, 0:w], psT[:, 0:w], AF.Exp, [b_psT], [b_pT], scale=bscale)

        def band_PV(gi):
            h, qt, g0, grp, nun = groups[gi]
            po_t, b_po = pO[h % 2]
            pT_t, b_pT = pT[gi % 4]
            for ui, (kt_, b_k_, v_, b_v_, i, r) in enumerate(grp):
                u_ = g0 + ui
                mm(po_t[:, qt * 128:(qt + 1) * 128], v_[:, i, h, :], pT_t[:, ui * 128:(ui + 1) * 128],
                   (qt == 0 and u_ == 0), (qt == 3 and u_ == nun - 1), [b_v_, b_pT], [b_po])
            if h % 2 == 1 and qt == 3 and g0 + len(grp) == nun:
                finalize_pair(6 + h // 2)
        BSK = 2
        for gi in range(len(groups) + BSK):
            if gi < len(groups):
                band_S(gi)
            if gi >= BSK:
                band_PV(gi - BSK)
        for half in range(2):
            wq, b_wq = need_half(l, 3, half)
            for t in range(4):
                pzt, b_pzt = next_pz()
                for ft in range(8):
                    mm(pzt[:, :], mixT[:, ft, t * 128:(t + 1) * 128], wq[:, ft, :],
                       ft == 0, ft == 7, [b_mixT, b_wq], [b_pzt])
                tt("dve", xres[:, t, half * 512:(half + 1) * 512], xres[:, t, half * 512:(half + 1) * 512],
                   pzt[:, :], ALU.add, [b_xr[t], b_pzt], [b_xr[t]])
        for t in range(4):
            if l == DEPTH - 1:
                dma(y_p[t0 + t * 128:t0 + (t + 1) * 128, :], xres[:, t, :], [b_xr[t]], [], b_xr[t], is_out=True)
            else:
                dma(y1_d[j][t * 128:(t + 1) * 128, :], xres[:, t, :], [b_xr[t]], [B_y1[j]], b_xr[t])

    def sample_mixers(l):
        NS = 128
        gam = _ret_gamma()
        dma(stS[:].rearrange("k (e h) v -> k e h v", e=4), st_in[l].rearrange("e h k v -> k e h v"),
            [Bc], [b_stS], b_stS)
        memset("dve", stpadS[:], 0.0, [b_stpadS])
        for e in range(4):
            for h in range(4):
                o = 64 * (h % 2)
                cp("dve", stpadS[o:o + 64, e, h // 2, o:o + 64], stS[:, e * 4 + h, :], [b_stS], [b_stpadS])
        for pr in range(2):
            po_t, b_po = pO[pr % 2]
            for e2 in range(2):
                h = 2 * pr + e2
                r0 = 64 * e2
                psT, b_psT = pS[h % 2]
                mm(psT[:, 0:NS], kTr[r0:r0 + 64, pr, 0:NS], qTr[r0:r0 + 64, pr, 0:NS], True, True,
                   [b_kTr, b_qTr], [b_psT])
                tt("dve", innT[:, 0:NS], psT[:, 0:NS], decS[:, h, :], ALU.mult, [b_psT, b_decS], [b_innT])
                mm(po_t[:, 0:NS], vpad[:, 0, h, :], innT[:, 0:NS], e2 == 0, False, [b_vpad, b_innT], [b_po])
            for e in range(4):
                mm(po_t[:, e * 32:(e + 1) * 32], stpadS[:, e, pr, :], qTd[:, pr, e * 32:(e + 1) * 32], False, e == 3,
                   [b_stpadS, b_qTd], [b_po])
            headnorm(po_t[:, 0:NS], 128, NS, bd64, b_bd64, 64, [b_po])
            stt("dve", tmpm[:, 0:NS], po_t[:, 0:NS], gc[:, 9 + pr:10 + pr], rstd[:, 0:NS], ALU.mult, ALU.mult,
                [b_po, b_gc, b_rstd], [b_tmpm])
            tt("dve", mixT[:, pr, 0:NS], tmpm[:, 0:NS], gates[:, pr, 0:NS], ALU.mult, [b_tmpm, b_gates], [b_mixT])
        for e in range(4):
            pzt, b_pzt = next_pz()
            for h in range(4):
                lo = 64 * (h % 2)
                mm(pzt[0:64, h * 64:(h + 1) * 64], k_tm[:, 0, h * 64:(h + 1) * 64],
                   vdecS[:, e, h * 128 + lo:h * 128 + lo + 64], True, True, [b_k_tm, b_vdecS], [b_pzt])
            for h in range(4):
                stt("dve", stS[:, e * 4 + h, :], stS[:, e * 4 + h, :], float(gam[h] ** 32),
                    pzt[0:64, h * 64:(h + 1) * 64], ALU.mult, ALU.add, [b_stS, b_pzt], [b_stS])
        dma(s_state[l].rearrange("e h k v -> k (e h) v"), stS[:], [b_stS], [], b_stS, is_out=True)
        for e in range(4):
            for hf in range(2):
                dma(stg[:, 0:512], cckvT[l, e][:, hf * 512:(hf + 1) * 512], [Bc], [b_stg], b_stg)
                cp("dve", ckvb[:], stg[:, 0:512], [b_stg], [b_ckvb])
                dma(stg[0:32, 512:1024], ckrT[l, e][:, hf * 512:(hf + 1) * 512], [Bc], [b_stg], b_stg)
                for h in range(8):
                    cp("dve", kTm[64:96, h, :], stg[0:32, 512:1024], [b_stg], [b_kTm])
                gen_kv(kTc_d[l][e][hf], B_kTc[l][e][hf], vc_d[l][e][hf], B_vc[l][e][hf])
        qscale = 96.0 ** -0.5
        cnt = [0]

        def load_kv(h, srcK, BK, srcV, BV):
            kt_t, b_kt = ktb[cnt[0] % 2]
            vt_t, b_vt = vtb[h % 2][cnt[0] % 2]
            cnt[0] += 1
            lo = 0 if h % 2 == 0 else 64
            dma(kt_t[:, :], srcK[h], [BK], [b_kt], b_kt)
            dma(vt_t[:, :, lo:lo + 64], srcV.rearrange("(t p) c -> p t c", p=128)[:, :, h * 64:(h + 1) * 64],
                [BV], [b_vt], b_vt)
            return kt_t, b_kt, vt_t, b_vt
        for h in range(8):
            po_t, b_po = pO[h % 2]
            kt_t, b_kt, vt_t, b_vt = load_kv(h, kTs_d[l], B_kTs[l], vs_d[l], B_vs[l])
            psT, b_psT = pS[0]
            pT_t, b_pT = pT[0]
            mm(psT[:, 0:NS], kt_t[:, 0:128], qTm[:, h, 0:NS], True, True, [b_kt, b_qTm], [b_psT])
            act(pT_t[:, 0:NS], psT[:, 0:NS], AF.Exp, [b_psT], [b_pT], scale=qscale)
            tt("dve", pT_t[:, 0:NS], pT_t[:, 0:NS], smask[:, :], ALU.mult, [b_pT, b_smask], [b_pT])
            mm(po_t[:, 0:NS], vt_t[:, 0, :], pT_t[:, 0:NS], True, False, [b_vt, b_pT], [b_po])
            for e in range(4):
                for hf in range(2):
                    kt_t, b_kt, vt_t, b_vt = load_kv(h, kTc_d[l][e][hf], B_kTc[l][e][hf], vc_d[l][e][hf], B_vc[l][e][hf])
                    psT, b_psT = pS[cnt[0] % 4]
                    pT_t, b_pT = pT[cnt[0] % 4]
                    for kt in range(4):
                        mm(psT[:, kt * 32:(kt + 1) * 32], kt_t[:, kt * 128:(kt + 1) * 128],
                           qTm[:, h, e * 32:(e + 1) * 32], True, True, [b_kt, b_qTm], [b_psT])
                    act(pT_t[:, 0:128], psT[:, 0:128], AF.Exp, [b_psT], [b_pT], scale=qscale)
                    for kt in range(4):
                        mm(po_t[:, e * 32:(e + 1) * 32], vt_t[:, kt, :], pT_t[:, kt * 32:(kt + 1) * 32], False,
                           (e == 3 and hf == 1 and kt == 3), [b_vt, b_pT], [b_po])
            if h % 2 == 1:
                finalize_pair(2 + h // 2, NS)
        kn_t, b_kn = kTb[1]
        vn_t, b_vn = vba[1]
        kc_t, b_kc = kTb[0]
        vc_t, b_vc = vba[0]
        for pr in range(2):
            for e2 in range(2):
                h = 2 * pr + e2
                po_t, b_po = pO[h % 2]
                psT, b_psT = pS[cnt[0] % 4]
                pT_t, b_pT = pT[cnt[0] % 4]
                cnt[0] += 1
                mm(psT[:, 0:NS], kn_t[:, pr, 0:128], qTbz[:, h, 0:NS], True, False, [b_kn, b_qTbz], [b_psT])
                mm(psT[:, 0:NS], idb[:], bbiasS_b[:, 2, h, :], False, True, [b_idb, b_bbiasS_b], [b_psT])
                act(pT_t[:, 0:NS], psT[:, 0:NS], AF.Exp, [b_psT], [b_pT], scale=0.125)
                mm(po_t[:, 0:NS], vn_t[:, 0, h, :], pT_t[:, 0:NS], True, False, [b_vn, b_pT], [b_po])
            for e in range(4):
                dma(stg[:, 0:512], cbkT[l, e][pr * 128:(pr + 1) * 128, :], [Bc], [b_stg], b_stg)
                cp("dve", kc_t[:, 0, :], stg[:, 0:512], [b_stg], [b_kc])
                dma(stg[:, :].rearrange("p (t c) -> p t c", t=4),
                    cbv[l, e].rearrange("(t p) c -> p t c", p=128)[:, :, pr * 256:(pr + 1) * 256],
                    [Bc], [b_stg], b_stg)
                for kt in range(4):
                    tt("dve", vc_t[:, kt, 2 * pr:2 * pr + 2, :].rearrange("p h c -> p (h c)"),
                       stg[:, kt * 256:(kt + 1) * 256], onesp[:, pr * 256:(pr + 1) * 256], ALU.add,
                       [b_stg, b_onesp], [b_vc])
                for e2 in range(2):
                    h = 2 * pr + e2
                    po_t, b_po = pO[h % 2]
                    psT, b_psT = pS[cnt[0] % 4]
                    pT_t, b_pT = pT[cnt[0] % 4]
                    cnt[0] += 1
                    for kt in range(4):
                        mm(psT[:, kt * 32:(kt + 1) * 32], kc_t[:, 0, kt * 128:(kt + 1) * 128],
                           qTbz[:, h, e * 32:(e + 1) * 32], True, False, [b_kc, b_qTbz], [b_psT])
                        mm(psT[:, kt * 32:(kt + 1) * 32], idb[:], bbiasS_b[:, 0 if kt < 3 else 1, h, e * 32:(e + 1) * 32],
                           False, True, [b_idb, b_bbiasS_b], [b_psT])
                    act(pT_t[:, 0:128], psT[:, 0:128], AF.Exp, [b_psT], [b_pT], scale=0.125)
                    for kt in range(4):
                        mm(po_t[:, e * 32:(e + 1) * 32], vc_t[:, kt, h, :], pT_t[:, kt * 32:(kt + 1) * 32], False,
                           (e == 3 and kt == 3), [b_vc, b_pT], [b_po])
            finalize_pair(6 + pr, NS)
        for half in range(2):
            wq, b_wq = need_half(l, 3, half)
            pzt, b_pzt = next_pz()
            for ft in range(8):
                mm(pzt[:, :], mixT[:, ft, 0:128], wq[:, ft, :], ft == 0, ft == 7,
                   [b_mixT, b_wq], [b_pzt])
            tt("dve", xres[:, 0, half * 512:(half + 1) * 512], xres[:, 0, half * 512:(half + 1) * 512],
               pzt[:, :], ALU.add, [b_xr[0], b_pzt], [b_xr[0]])
        if l == DEPTH - 1:
            dma(y_s[:, :], xres[:, 0, :], [b_xr[0]], [], b_xr[0], is_out=True)
        else:
            dma(ys_d[:, :], xres[:, 0, :], [b_xr[0]], [B_ys], b_xr[0])

    def finalize_pair(ft, N=512):
        (pe_t, b_pe), (po2_t, b_po2) = pO
        act(rinv[64:128, 0, 0:N], pe_t[64:128, 0:N], AF.Ln, [b_pe], [b_rinv])
        act(rinv[0:64, 0, 0:N], po2_t[0:64, 0:N], AF.Ln, [b_po2], [b_rinv])
        act(rinv[:, 0, 0:N], rinv[:, 0, 0:N], AF.Exp, [b_rinv], [b_rinv], scale=-1.0)
        mm(pn[:, 0:N], sel[:, 0, :], rinv[:, 0, 0:N], True, True, [b_sel, b_rinv], [b_pn])
        cp("dve", rbs[:, 0:N], pn[:, 0:N], [b_pn], [b_rbs])
        tt("dve", tmpm[0:64, 0:N], pe_t[0:64, 0:N], rbs[0:64, 0:N], ALU.mult, [b_pe, b_rbs], [b_tmpm])
        tt("dve", tmpm[64:128, 0:N], po2_t[64:128, 0:N], rbs[64:128, 0:N], ALU.mult, [b_po2, b_rbs], [b_tmpm])
        tt("dve", mixT[:, ft, 0:N], tmpm[:, 0:N], gates[:, ft, 0:N], ALU.mult, [b_tmpm, b_gates], [b_mixT])

    stpad_t, b_stpad = T("stpad", [128, 2, 128], BF16)

    def P_state_term(pr, po_t, b_po):
        mm(po_t[:, 0:512], stpad_t[:, pr, :], qTd[:, pr, :], False, True, [b_stpad, b_qTd], [b_po])

    for l in range(DEPTH):
        load_weights(l)
        if cfg.with_sample:
            prompt_block(l, 0, sample=True)
        for j in range(NBLK if cfg.stage >= 1 else 0):
            prompt_block(l, j)
    P.finish()
    P.build(nc, es)
    es.close()
    return nc


def _bf(a):
    return np.ascontiguousarray(a.astype(ml_dtypes.bfloat16))


def make_consts(cfg):
    SEQ = cfg.seq
    c = {}
    pos = np.concatenate([np.arange(SEQ), cfg.past + (np.arange(128) % 32), np.zeros(384, np.int64)])
    c["cosA"], c["sinA"] = _rope_tables(pos, 64, 128)
    c["cosB"], c["sinB"] = _rope_tables(pos, 32, 64)
    c["c_idb"] = _bf(np.eye(128, dtype=np.float32))
    c["c_idf"] = np.eye(128, dtype=np.float32)
    bd64 = np.kron(np.eye(2), np.ones((64, 64))).astype(np.float32)
    bd32 = np.kron(np.eye(4), np.ones((32, 32))).astype(np.float32)
    c["c_bd64"] = _bf(bd64)
    c["c_bd32"] = _bf(bd32)
    c["c_ones"] = _bf(np.ones((128, 128), np.float32))
    pm = np.zeros((128, 128), np.float32)
    for m_ in range(128):
        pm[(m_ // 64) * 64 + ((m_ % 64) + 32) % 64, m_] = 1.0
    c["c_perm64"] = _bf(pm)
    sel = np.zeros((128, 2, 128), np.float32)
    for k in range(64):
        sel[64 + k, 0, k] = 1.0
        sel[k, 0, 64 + k] = 1.0
    c["c_sel"] = sel
    kpos = (np.arange(4)[None, :, None] * 128 + np.arange(128)[:, None, None])
    qpos = np.arange(512)[None, None, :]
    vis = (kpos // CHUNK) <= (qpos // CHUNK)
    c["c_mmask"] = _bf(np.where(vis, 1.0, 0.0).astype(np.float32))
    gam = _ret_gamma()
    m = np.arange(128)[:, None]
    ll = np.arange(128)[None, :]
    dec = np.zeros((128, 4, 4, 128), np.float64)
    for h in range(4):
        for d in range(4):
            dist = ll - m + 128 * d
            dec[:, h, d, :] = np.where(dist >= 0, gam[h] ** np.maximum(dist, 0), 0.0)
    c["c_dec"] = _bf(dec.astype(np.float32))
    p = np.arange(512)
    dqt = np.zeros((128, 2, 512), np.float64)
    for pr in range(2):
        for e in range(2):
            dqt[64 * e:64 * e + 64, pr, :] = (gam[2 * pr + e] ** (p + 1.0))[None, :]
    c["c_dq"] = dqt.astype(np.float32)
    dv = np.zeros((128, 4, 4), np.float64)
    onesp = np.zeros((128, 4, 128), np.float32)
    for h in range(4):
        lo = 0 if h % 2 == 0 else 64
        onesp[:, h, 64 - lo:128 - lo] = 1.0
        for t in range(4):
            dv[:, t, h] = 0.125 * gam[h] ** (511.0 - (t * 128 + np.arange(128)))
    c["c_dv"] = dv.astype(np.float32)
    c["c_onesp"] = onesp.reshape(128, 512)
    kp = np.arange(128)[:, None, None]
    qp = np.arange(128)[None, None, :]
    rr = (np.arange(5) - 4)[None, :, None]
    dchunk = -2 * rr + qp // 64 - kp // 64
    c["c_bmk"] = np.where((dchunk >= 0) & (dchunk <= 8), 0.0, -30000.0).astype(np.float32)
    gam = _ret_gamma()
    tok = np.arange(128)
    el, ii = tok // 32, tok % 32
    dqs = np.zeros((128, 2, 512), np.float64)
    for pr in range(2):
        for e in range(2):
            dqs[64 * e:64 * e + 64, pr, :] = (gam[2 * pr + e] ** ((np.arange(512) % 32) + 1.0))[None, :]
    c["c_dqS"] = dqs.astype(np.float32)
    dvs = np.zeros((128, 4, 4), np.float64)
    for e in range(4):
        for h in range(4):
            dvs[:, e, h] = np.where(el == e, 0.125 * gam[h] ** (31.0 - ii), 0.0)
    c["c_dvS"] = dvs.astype(np.float32)
    same = (el[:, None] == el[None, :])
    dif = ii[None, :] - ii[:, None]
    decs = np.zeros((128, 4, 128), np.float64)
    for h in range(4):
        decs[:, h, :] = np.where(same & (dif >= 0), gam[h] ** np.maximum(dif, 0), 0.0)
    c["c_decS"] = _bf(decs.astype(np.float32))
    c["c_smask"] = _bf(same.astype(np.float32))
    bms = np.zeros((128, 3, 128), np.float32)
    bms[:, 2, :] = np.where(same, 0.0, -30000.0)
    c["c_bmkS"] = bms
    return c


def make_weights(cfg, inp):
    L = cfg.depth
    w_in = np.asarray(inp["w_in"])[:L]
    cuts = np.cumsum([256, 256, 256, 256, 256, 128, 32, 512, 256, 256, 256, 256])[:-1]
    a_q, a_k, a_v, a_g, b_cq, b_ckv, b_kr, b_g, c_q, c_k, c_v, c_g = np.split(w_in, cuts, axis=-1)
    z64 = np.zeros(b_kr.shape[:-1] + (64,), np.float32)
    W1 = np.concatenate([a_q, _swap_cols(a_q, 64), a_k, _swap_cols(a_k, 64), a_g, b_g, c_g, b_cq, b_ckv,
                         np.concatenate([b_kr, _swap_cols(b_kr, 32), z64], -1), c_q, c_k], -1)
    assert W1.shape[-1] == NCT * 128, W1.shape
    def padheads(w):
        L_, R_, _ = w.shape
        o = np.zeros((L_, R_, 4, 128), np.float32)
        for h in range(4):
            lo = 0 if h % 2 == 0 else 64
            o[:, :, h, lo:lo + 64] = w[:, :, h * 64:(h + 1) * 64]
        return o.reshape(L_, R_, 512)
    WT = np.concatenate([padheads(a_v), padheads(c_v)], -1)
    wuq = np.asarray(inp["mla_w_uq"])[:L].reshape(L, 256, 8, 96)
    nope = wuq[..., :64].reshape(L, 256, 512)
    rope = wuq[..., 64:].reshape(L, 256, 256)
    WUQ = np.concatenate([nope, rope, _swap_cols(rope, 32)], -1)
    wukv = np.asarray(inp["mla_w_ukv"])[:L].reshape(L, 128, 8, 128)
    WUKV = np.concatenate([wukv[..., :64].reshape(L, 128, 512), wukv[..., 64:].reshape(L, 128, 512)], -1)
    GN = np.broadcast_to(np.asarray(inp["norm_g"])[:L, None, :], (L, 128, D)).copy()
    GC = np.zeros((L, 128, 16), np.float32)
    qa = np.asarray(inp["mla_qa_g"])[:L]
    GC[:, :, 0] = qa[:, :128]
    GC[:, :, 1] = qa[:, 128:]
    GC[:, :, 2] = np.tile(np.asarray(inp["mla_qn_g"])[:L], (1, 2))
    qr = np.asarray(inp["mla_qr_g"])[:L]
    GC[:, :, 3] = np.tile(qr, (1, 4))
    GC[:, :, 11] = np.tile(_swap_cols(qr, 32), (1, 4))
    GC[:, :, 4] = np.asarray(inp["mla_kva_g"])[:L]
    kr = np.asarray(inp["mla_kr_g"])[:L]
    GC[:, :64, 5] = np.concatenate([kr, _swap_cols(kr, 32)], -1)
    GC[:, :, 6] = np.tile(np.asarray(inp["mla_kn_g"])[:L], (1, 2))
    GC[:, :, 7] = np.tile(np.asarray(inp["band_qn_g"])[:L], (1, 2))
    GC[:, :, 8] = np.tile(np.asarray(inp["band_kn_g"])[:L], (1, 2))
    rg = np.asarray(inp["ret_gn_g"])[:L]
    GC[:, :, 9] = rg[:, :128]
    GC[:, :, 10] = rg[:, 128:]
    bb = np.asarray(inp["band_bias"])[:L]
    kp = np.arange(128)[:, None, None]
    qp = np.arange(128)[None, None, :]
    rr = (np.arange(5) - 4)[None, :, None]
    bidx = np.clip(qp - kp - 128 * rr, -128, 128) + 128
    BT = np.ascontiguousarray(bb[:, :, bidx].transpose(0, 2, 3, 1, 4))
    kp = np.arange(128)[:, None]
    iq = (np.arange(128) % 32)[None, :]
    ik = (np.arange(128) % 32)[:, None]
    i0 = np.full((128, 128), 256)
    i1 = np.clip(128 + iq - kp, -128, 128) + 128
    i2 = np.clip(iq - ik, -128, 128) + 128
    sidx = np.stack([i0, i1, i2], 1)
    BTS = np.ascontiguousarray(bb[:, :, sidx].transpose(0, 2, 3, 1, 4))
    return dict(BTS=BTS, W1=np.ascontiguousarray(W1), WT=np.ascontiguousarray(WT), WUQ=np.ascontiguousarray(WUQ),
                WUKV=np.ascontiguousarray(WUKV), WOUT=np.ascontiguousarray(np.asarray(inp["w_out"])[:L]),
                GN=GN, GC=GC, BT=BT)


def make_core_inputs(cfg, inp, c):
    L = cfg.depth
    m = {}
    m["x_p"] = np.ascontiguousarray(np.asarray(inp["x_prompt"])[c // 4, :cfg.seq])
    xs = np.zeros((512, D), np.float32)
    xs[:128] = np.asarray(inp["x_sample"])[4 * c:4 * c + 4].reshape(128, D)
    m["x_s"] = xs
    m["st_in"] = np.ascontiguousarray(np.asarray(inp["state_ret"])[:L, 4 * c:4 * c + 4])
    m["cckvT"] = np.ascontiguousarray(np.asarray(inp["cache_mla_ckv"])[:L, 4 * c:4 * c + 4].transpose(0, 1, 3, 2))
    m["ckrT"] = np.ascontiguousarray(np.asarray(inp["cache_mla_krope"])[:L, 4 * c:4 * c + 4].transpose(0, 1, 3, 2))
    bk = np.asarray(inp["cache_band_k"])[:L, 4 * c:4 * c + 4].reshape(L, 4, 512, 256)
    m["cbkT"] = np.ascontiguousarray(bk.transpose(0, 1, 3, 2))
    bv = np.asarray(inp["cache_band_v"])[:L, 4 * c:4 * c + 4]
    bvp = np.zeros((L, 4, 512, 4, 128), np.float32)
    for h in range(4):
        lo = 0 if h % 2 == 0 else 64
        bvp[:, :, :, h, lo:lo + 64] = bv[:, :, :, h, :]
    m["cbv"] = bvp.reshape(L, 4, 512, 512)
    return m


_PROG = {}


def kernel(**inputs):
    cfg = Cfg(seq=8192, depth=2, past=1024, with_sample=True)
    if "nc" not in _PROG:
        _PROG["nc"] = build_program(cfg)
    nc = _PROG["nc"]
    inp = {k: np.asarray(v) for k, v in inputs.items()}
    consts = make_consts(cfg)
    w = make_weights(cfg, inp)
    maps = []
    for c in range(8):
        m = dict(consts)
        m.update(w)
        m.update(make_core_inputs(cfg, inp, c))
        maps.append(m)
    res = run_bass_kernel_spmd(nc, maps, core_ids=list(range(8))).results
    L, S = cfg.depth, cfg.seq
    f = np.float32
    y_p = np.stack([res[0]["y_p"], res[4]["y_p"]]).astype(f)
    y_s = np.concatenate([res[c]["y_s"].reshape(4, 32, D) for c in range(8)], 0).astype(f)
    p_state = np.stack([res[0]["o_state"], res[4]["o_state"]], 1).astype(f)
    p_ckv = np.stack([res[0]["o_ckv"], res[4]["o_ckv"]], 1).astype(f)
    p_kr = np.stack([res[0]["o_kr"], res[4]["o_kr"]], 1).astype(f)
    p_bk = np.stack([res[0]["o_bk"], res[4]["o_bk"]], 1).reshape(L, 2, 512, 4, 64).astype(f)
    p_bv = np.stack([res[0]["o_bv"], res[4]["o_bv"]], 1).reshape(L, 2, 512, 4, 64).astype(f)
    s_state = np.concatenate([res[c]["s_state"] for c in range(8)], 1).astype(f)
    s_ckv = np.concatenate([res[c]["s_ckv"].reshape(L, 4, 32, 128) for c in range(8)], 1).astype(f)
    s_kr = np.concatenate([res[c]["s_kr"].reshape(L, 4, 32, 32) for c in range(8)], 1).astype(f)
    s_bk = np.concatenate([res[c]["s_bk"].reshape(L, 4, 32, 4, 64) for c in range(8)], 1).astype(f)
    s_bv = np.concatenate([res[c]["s_bv"].reshape(L, 4, 32, 4, 64) for c in range(8)], 1).astype(f)
    return (y_p, y_s, p_state, p_ckv, p_kr, p_bk, p_bv, s_state, s_ckv, s_kr, s_bk, s_bv)
```

```python
import contextlib
import numpy as np
import ml_dtypes
import concourse.bass as bass
import concourse.mybir as mybir
from concourse.bass_utils import run_bass_kernel_spmd

F32 = mybir.dt.float32
BF16 = mybir.dt.bfloat16
ALU = mybir.AluOpType
AF = mybir.ActivationFunctionType
AX = mybir.AxisListType

D = 1024
EPS = 1e-6
NCT = 24
CHUNK = 64


class Buf:
    def __init__(self, name):
        self.name = name
        self.w = []
        self.r = []
        self.sem = None
        self.dma_total = 0


class Op:
    __slots__ = ("eng", "fn", "deps", "is_dma", "dsem", "dval", "sig", "sval", "idx", "tag")

    def __init__(self, eng, fn):
        self.eng = eng
        self.fn = fn
        self.deps = []
        self.is_dma = False
        self.dsem = None
        self.dval = 0
        self.sig = False
        self.sval = 0
        self.tag = ''


class Prog:
    ENGS = ("pe", "act", "dve", "pool", "sp")

    def __init__(self):
        self.ops = {e: [] for e in self.ENGS}
        self.dma_bufs = []
        self.out_dmas = []
        self.trace = None

    SKIP_SAME_ENGINE_WAW_WAR = True

    def _dep(self, op, prod, kind="raw"):
        if prod is op:
            return
        if (not prod.is_dma) and prod.eng == "pe" and op.eng == "pe" and not op.is_dma:
            return
        if (self.SKIP_SAME_ENGINE_WAW_WAR and kind != "raw" and (not prod.is_dma) and (not op.is_dma)
                and prod.eng == op.eng):
            return
        op.deps.append(prod)

    def emit(self, eng, fn, reads=(), writes=(), dma_buf=None, is_out=False):
        op = Op(eng, fn)
        for b in reads:
            for p in b.w:
                self._dep(op, p)
        for b in writes:
            for p in b.w:
                self._dep(op, p, "waw")
            for p in b.r:
                self._dep(op, p, "war")
        if dma_buf is not None:
            op.is_dma = True
            if dma_buf.sem is None:
                dma_buf.sem = len(self.dma_bufs)
                self.dma_bufs.append(dma_buf)
            dma_buf.dma_total += 16
            op.dsem = dma_buf
            op.dval = dma_buf.dma_total
        for b in reads:
            b.r.append(op)
        for b in writes:
            b.w = [op]
            b.r = []
        op.tag = ','.join(b.name for b in reads) + '->' + ','.join(b.name for b in writes)
        self.ops[eng].append(op)
        if is_out:
            self.out_dmas.append(op)
        return op

    def finish(self):
        op = Op("sp", None)
        op.deps = list(self.out_dmas)
        self.ops["sp"].append(op)

    def build(self, nc, es):
        n_dma_sems = len(self.dma_bufs)
        dsems = [es.enter_context(nc.semaphore(f"d{i}")) for i in range(n_dma_sems)]
        esems = {e: es.enter_context(nc.semaphore(f"e_{e}")) for e in self.ENGS}
        for e in self.ENGS:
            for op in self.ops[e]:
                for p in op.deps:
                    if not p.is_dma:
                        p.sig = True
        for e in self.ENGS:
            c = 0
            for op in self.ops[e]:
                if op.sig:
                    c += 1
                    op.sval = c
        block = es.enter_context(nc.Block())

        def run(ename, eng):
            waited = {}
            for op in self.ops[ename]:
                need = {}
                for p in op.deps:
                    if p.is_dma:
                        key = ("d", p.dsem.sem)
                        sem = dsems[p.dsem.sem]
                        val = p.dval
                    else:
                        key = ("e", p.eng)
                        sem = esems[p.eng]
                        val = p.sval
                    if waited.get(key, 0) >= val:
                        continue
                    if key not in need or need[key][1] < val:
                        need[key] = (sem, val)
                for key, (sem, val) in need.items():
                    eng.wait_ge(sem, val)
                    waited[key] = val
                if self.trace is not None:
                    self.trace.append((ename, [(k, v[1]) for k, v in need.items()], op.sval if op.sig else None,
                                       (op.dsem.sem, op.dval) if op.is_dma else None, op.tag))
                if op.fn is None:
                    continue
                ins = op.fn(eng)
                if op.is_dma:
                    ins.then_inc(dsems[op.dsem.sem], 16)
                elif op.sig:
                    ins.then_inc(esems[ename], 1)

        @block.tensor
        def _(e):
            run("pe", e)

        @block.scalar
        def _(e):
            run("act", e)

        @block.vector
        def _(e):
            run("dve", e)

        @block.gpsimd
        def _(e):
            run("pool", e)

        @block.sync
        def _(e):
            run("sp", e)


def _rope_tables(pos, dim, rows):
    half = dim // 2
    inv = (10000.0 ** (-np.arange(half, dtype=np.float32) / half)).astype(np.float32)
    ang = pos.astype(np.float32)[None, :] * inv[:, None]
    cos = np.cos(ang).astype(np.float32)
    sin = np.sin(ang).astype(np.float32)
    c = np.concatenate([cos, cos], 0)
    s = np.concatenate([-sin, sin], 0)
    reps = rows // dim
    return np.tile(c, (reps, 1)).copy(), np.tile(s, (reps, 1)).copy()


def _swap_cols(w, dim):
    n = w.shape[-1]
    idx = np.arange(n).reshape(n // dim, dim)
    idx = np.concatenate([idx[:, dim // 2:], idx[:, :dim // 2]], 1).reshape(-1)
    return w[..., idx]


def _ret_gamma():
    return (1.0 - np.exp2(-5.0 - np.arange(4, dtype=np.float64)))


class Cfg:
    def __init__(self, seq=8192, depth=2, past=1024, with_sample=True):
        self.seq = seq
        self.depth = depth
        self.past = past
        self.nblk = seq // 512
        self.with_sample = with_sample
        self.stage = 99
        self.dbg = 0


def build_program(cfg):
    SEQ, DEPTH, NBLK = cfg.seq, cfg.depth, cfg.nblk
    nc = bass.Bass("TRN2", target_bir_lowering=False)
    P = Prog()
    P.trace = [] if getattr(cfg, 'trace', False) else None
    cfg.P = P
    es = contextlib.ExitStack()

    def din(name, shape, dt=F32):
        return nc.dram_tensor(name, list(shape), dt, kind="ExternalInput").ap()

    def dout(name, shape, dt=F32):
        return nc.dram_tensor(name, list(shape), dt, kind="ExternalOutput").ap()

    def dscr(name, shape, dt):
        return nc.dram_tensor(name, list(shape), dt).ap()

    def sb(name, shape, dt):
        return es.enter_context(nc.sbuf_tensor(name, list(shape), dt))

    def ps(name, shape, dt=F32):
        return es.enter_context(nc.psum_tensor(name, list(shape), dt))

    x_p = din("x_p", [SEQ, D])
    W1 = din("W1", [DEPTH, D, NCT * 128])
    WT = din("WT", [DEPTH, D, 1024])
    WUQ = din("WUQ", [DEPTH, 256, 1024])
    WUKV = din("WUKV", [DEPTH, 128, 1024])
    WOUT = din("WOUT", [DEPTH, D, D])
    GN = din("GN", [DEPTH, 128, D])
    GC = din("GC", [DEPTH, 128, 16])
    BT = din("BT", [DEPTH, 128, 5, 4, 128])
    cosA = din("cosA", [128, SEQ + 512])
    sinA = din("sinA", [128, SEQ + 512])
    cosB = din("cosB", [64, SEQ + 512])
    sinB = din("sinB", [64, SEQ + 512])
    c_idb = din("c_idb", [128, 128], BF16)
    c_idf = din("c_idf", [128, 128])
    c_bd64 = din("c_bd64", [128, 128], BF16)
    c_bd32 = din("c_bd32", [128, 128], BF16)
    c_ones = din("c_ones", [128, 128], BF16)
    c_perm64 = din("c_perm64", [128, 128], BF16)
    c_sel = din("c_sel", [128, 2, 128])
    c_mmask = din("c_mmask", [128, 4, 512], BF16)
    c_dec = din("c_dec", [128, 4, 4, 128], BF16)
    c_dq = din("c_dq", [128, 2, 512])
    c_dv = din("c_dv", [128, 4, 4])
    c_onesp = din("c_onesp", [128, 512])
    c_bmk = din("c_bmk", [128, 5, 128])

    x_s = din("x_s", [512, D])
    st_in = din("st_in", [DEPTH, 4, 4, 64, 64])
    cckvT = din("cckvT", [DEPTH, 4, 128, 1024])
    ckrT = din("ckrT", [DEPTH, 4, 32, 1024])
    cbkT = din("cbkT", [DEPTH, 4, 256, 512])
    cbv = din("cbv", [DEPTH, 4, 512, 512])
    BTS = din("BTS", [DEPTH, 128, 3, 4, 128])
    c_bmkS = din("c_bmkS", [128, 3, 128])
    c_dqS = din("c_dqS", [128, 2, 512])
    c_dvS = din("c_dvS", [128, 4, 4])
    c_decS = din("c_decS", [128, 4, 128], BF16)
    c_smask = din("c_smask", [128, 128], BF16)
    y_s = dout("y_s", [128, D])
    s_state = dout("s_state", [DEPTH, 4, 4, 64, 64])
    s_ckv = dout("s_ckv", [DEPTH, 128, 128])
    s_kr = dout("s_kr", [DEPTH, 128, 32])
    s_bk = dout("s_bk", [DEPTH, 128, 256])
    s_bv = dout("s_bv", [DEPTH, 128, 256])
    kTs_d = [dscr(f"kTs_{l}", [8, 96, 512], BF16) for l in range(DEPTH)]
    vs_d = [dscr(f"vs_{l}", [512, 512], BF16) for l in range(DEPTH)]
    kTc_d = [[[dscr(f"kTc_{l}_{e}_{hf}", [8, 96, 512], BF16) for hf in range(2)] for e in range(4)] for l in range(DEPTH)]
    vc_d = [[[dscr(f"vc_{l}_{e}_{hf}", [512, 512], BF16) for hf in range(2)] for e in range(4)] for l in range(DEPTH)]
    ys_d = dscr("ys_d", [128, D], F32)
    B_kTs = [Buf(f"kTs{l}") for l in range(DEPTH)]
    B_vs = [Buf(f"vs{l}") for l in range(DEPTH)]
    B_kTc = [[[Buf(f"kTc{l}{e}{hf}") for hf in range(2)] for e in range(4)] for l in range(DEPTH)]
    B_vc = [[[Buf(f"vc{l}{e}{hf}") for hf in range(2)] for e in range(4)] for l in range(DEPTH)]
    B_ys = Buf("ys")
    y_p = dout("y_p", [SEQ, D])
    o_state = dout("o_state", [DEPTH, 4, 64, 64])
    o_ckv = dout("o_ckv", [DEPTH, SEQ, 128])
    o_kr = dout("o_kr", [DEPTH, SEQ, 32])
    o_bk = dout("o_bk", [DEPTH, 512, 256])
    o_bv = dout("o_bv", [DEPTH, 512, 256])

    kT_d = [[dscr(f"kT_{l}_{j}", [8, 96, 512], BF16) for j in range(NBLK)] for l in range(DEPTH)]
    v_d = [[dscr(f"v_{l}_{j}", [512, 512], BF16) for j in range(NBLK)] for l in range(DEPTH)]
    y1_d = [dscr(f"y1_{j}", [512, D], F32) for j in range(NBLK)]
    CH = (0, 1, 'T', 2, 3)
    wbf_d = {(l, c, hf): dscr(f"wbf_{l}_{c}_{hf}", [128, 8, 512], BF16) for l in range(DEPTH) for c in CH for hf in range(2)}
    B_wbf = {k: Buf(f"wbf{k}") for k in wbf_d}

    B_kT = [[Buf(f"kT{l}{j}") for j in range(NBLK)] for l in range(DEPTH)]
    B_v = [[Buf(f"v{l}{j}") for j in range(NBLK)] for l in range(DEPTH)]
    B_y1 = [Buf(f"y1{j}") for j in range(NBLK)]

    def T(name, shape, dt):
        t = sb(name, shape, dt)
        return t, Buf(name)

    w1h = [T("w1a", [128, 8, 512], BF16), T("w1b", [128, 8, 512], BF16)]
    wuq, b_wuq = T("wuq", [128, 2, 1024], BF16)
    wukv, b_wukv = T("wukv", [128, 1024], BF16)
    stg, b_stg = T("stg", [128, 1024], F32)
    gn, b_gn = T("gn", [128, D], F32)
    gc, b_gc = T("gc", [128, 16], F32)
    idb, b_idb = T("idb", [128, 128], BF16)
    idf, b_idf = T("idf", [128, 128], F32)
    bd64, b_bd64 = T("bd64", [128, 128], BF16)
    bd32, b_bd32 = T("bd32", [128, 128], BF16)
    ones, b_ones = T("ones", [128, 128], BF16)
    perm64, b_perm64 = T("perm64", [128, 128], BF16)
    sel, b_sel = T("sel", [128, 2, 128], F32)
    mmask, b_mmask = T("mmask", [128, 4, 512], BF16)
    dec, b_dec = T("dec", [128, 4, 4, 128], BF16)
    dq, b_dq = T("dq", [128, 2, 512], F32)
    dvt, b_dvt = T("dvt", [128, 4, 4], F32)
    onesp, b_onesp = T("onesp", [128, 512], F32)
    bbias_b, b_bbias_b = T("bbias_b", [128, 5, 4, 128], BF16)
    bmk, b_bmk = T("bmk", [128, 5, 128], F32)
    rbs, b_rbs = T("rbs", [128, 512], F32)

    decS, b_decS = T("decS", [128, 4, 128], BF16)
    smask, b_smask = T("smask", [128, 128], BF16)
    dvS, b_dvS = T("dvS", [128, 4, 4], F32)
    vdecS, b_vdecS = T("vdecS", [128, 4, 512], BF16)
    stS, b_stS = T("stS", [64, 16, 64], F32)
    stpadS, b_stpadS = T("stpadS", [128, 4, 2, 128], BF16)
    bbiasS_b, b_bbiasS_b = T("bbiasS_b", [128, 3, 4, 128], BF16)
    bmkS, b_bmkS = T("bmkS", [128, 3, 128], F32)
    epsb, b_epsb = T("epsb", [128, 1], F32)
    dmy, b_dmy = T("dmy", [128, 1], F32)
    xres, b_xres = T("xres", [128, 4, D], F32)
    b_xr = [Buf(f"xres{t_}") for t_ in range(4)]
    hb, b_hb = T("hb", [128, D], BF16)
    junk, b_junk = T("junk", [128, D], BF16)
    ss1, b_ss1 = T("ss1", [128, 8], F32)
    b_ss1t = [Buf(f"ss1_{t_}") for t_ in range(4)]
    hT, b_hT = T("hT", [128, 8, 512], BF16)
    tcos, b_tcos = T("tcos", [128, 512], F32)
    tsin, b_tsin = T("tsin", [128, 512], F32)
    tcosB, b_tcosB = T("tcosB", [64, 512], F32)
    tsinB, b_tsinB = T("tsinB", [64, 512], F32)
    tA, b_tA = T("tA", [128, 512], F32)
    tB, b_tB = T("tB", [128, 512], F32)
    sq, b_sq = T("sq", [128, 512], BF16)
    sq2, b_sq2 = T("sq2", [128, 512], BF16)
    rstd, b_rstd = T("rstd", [128, 512], F32)
    qTr, b_qTr = T("qTr", [128, 2, 512], BF16)
    qTd, b_qTd = T("qTd", [128, 2, 512], BF16)
    kTr, b_kTr = T("kTr", [128, 2, 512], BF16)
    k_tm, b_k_tm = T("k_tm", [128, 4, 256], BF16)
    vpad, b_vpad = T("vpad", [128, 4, 4, 128], BF16)
    vdec, b_vdec = T("vdec", [128, 4, 512], BF16)
    gates, b_gates = T("gates", [128, 8, 512], BF16)
    cqT, b_cqT = T("cqT", [128, 2, 512], BF16)
    ckvf, b_ckvf = T("ckvf", [128, 512], F32)
    ckvb, b_ckvb = T("ckvb", [128, 512], BF16)
    krf, b_krf = T("krf", [32, 512], F32)
    qTm, b_qTm = T("qTm", [96, 8, 512], BF16)
    kTm, b_kTm = T("kTm", [96, 8, 512], BF16)
    vtm, b_vtm = T("vtm", [128, 4, 512], BF16)
    qTb, b_qTb = T("qTb", [128, 2, 512], BF16)
    qTbz, b_qTbz = T("qTbz", [128, 4, 512], BF16)
    kTb = [T(f"kTb{i}", [128, 2, 512], BF16) for i in range(2)]
    kbf, b_kbf = T("kbf", [128, 2, 512], F32)
    rinv, b_rinv = kbf, b_kbf
    vbf, b_vbf = T("vbf", [128, 4, 256], F32)
    vba = [T(f"vba{i}", [128, 4, 4, 128], BF16) for i in range(2)]
    ktb = [T(f"ktb{i}", [96, 512], BF16) for i in range(2)]
    vtb = [[T(f"vtb{p}{i}", [128, 4, 128], BF16) for i in range(2)] for p in range(2)]
    pT = [T(f"pT{i}", [128, 512], BF16) for i in range(2)]
    PT_EXTRA = True
    tmpm, b_tmpm = T("tmpm", [128, 512], F32)
    mixT, b_mixT = hT, b_hT
    innT, b_innT = T("innT", [128, 512], BF16)
    state, b_state = T("state", [64, 4, 64], F32)
    otr, b_otr = T("otr", [128, 512], F32)

    pz = [(ps(f"pz{i}", [128, 512]), Buf(f"pz{i}")) for i in range(2)]
    ptr, b_ptr = ps("ptr", [128, 8, 128], BF16), Buf("ptr")
    pS = [(ps(f"pS{i}", [128, 512]), Buf(f"pS{i}")) for i in range(2)]
    pO = [(ps(f"pO{i}", [128, 512]), Buf(f"pO{i}")) for i in range(2)]
    pn, b_pn = ps("pn", [128, 512]), Buf("pn")
    ring4 = [pz[0], pz[1], pS[0], pS[1]]
    pS = ring4

    pT = pT + [(innT, b_innT), (sq2, b_sq2)]
    dq_rr = [0]

    def dma(out, in_, reads, writes, sbuf_buf, is_out=False, eng=None):
        if eng is None:
            eng = ("sp", "pool")[dq_rr[0] % 2] if False else "sp"
        return P.emit(eng, lambda e: e.dma_start(out=out, in_=in_), reads=reads, writes=writes,
                      dma_buf=sbuf_buf, is_out=is_out)

    def mm(out, lhsT, rhs, start, stop, reads, writes):
        return P.emit("pe", lambda e: e.matmul(out, lhsT, rhs, start=start, stop=stop),
                      reads=reads, writes=writes)

    def tr(out, in_, ident, reads, writes):
        return P.emit("pe", lambda e: e.transpose(out, in_, ident), reads=reads, writes=writes)

    def act(out, in_, func, reads, writes, scale=1.0, bias=0.0, accum=None):
        def fn(e):
            kw = {}
            if accum is not None:
                kw["accum_out"] = accum
            return e.activation(out, in_, func, bias=bias, scale=scale, **kw)
        return P.emit("act", fn, reads=reads, writes=writes)

    def tt(eng, out, in0, in1, op, reads, writes):
        return P.emit(eng, lambda e: e.tensor_tensor(out=out, in0=in0, in1=in1, op=op),
                      reads=reads, writes=writes)

    def tsc(eng, out, in0, s1, s2, op0, op1, reads, writes):
        if op1 == ALU.pow:
            P.emit("act", lambda e: e.activation(out, in0, AF.Ln, bias=epsb[0:out.shape[0], 0:1], scale=1.0),
                   reads=list(reads) + [b_epsb], writes=writes)
            return P.emit("act", lambda e: e.activation(out, out, AF.Exp, scale=-0.5), reads=writes, writes=writes)

        def fn(e):
            if op1 is None:
                return e.tensor_scalar(out=out, in0=in0, scalar1=s1, scalar2=None, op0=op0)
            return e.tensor_scalar(out=out, in0=in0, scalar1=s1, scalar2=s2, op0=op0, op1=op1)
        return P.emit(eng, fn, reads=reads, writes=writes)

    def stt(eng, out, in0, scalar, in1, op0, op1, reads, writes):
        return P.emit(eng, lambda e: e.scalar_tensor_tensor(out=out, in0=in0, scalar=scalar, in1=in1,
                                                            op0=op0, op1=op1),
                      reads=reads, writes=writes)

    def cp(eng, out, in_, reads, writes):
        if eng == "act":
            return P.emit(eng, lambda e: e.copy(out, in_), reads=reads, writes=writes)
        return P.emit(eng, lambda e: e.tensor_copy(out=out, in_=in_), reads=reads, writes=writes)

    def memset(eng, ap, val, writes):
        return P.emit(eng, lambda e: e.memset(ap, val), reads=(), writes=writes)

    pz_rr = [0]

    def next_pz():
        i = pz_rr[0] % 4
        pz_rr[0] += 1
        return ring4[i]

    Bc = Buf("consts_dram")
    for t_, b_, src in ((idb, b_idb, c_idb), (idf, b_idf, c_idf), (bd64, b_bd64, c_bd64), (bd32, b_bd32, c_bd32),
                        (ones, b_ones, c_ones), (perm64, b_perm64, c_perm64), (sel, b_sel, c_sel), (mmask, b_mmask, c_mmask),
                        (dec, b_dec, c_dec), (dvt, b_dvt, c_dv), (bmk, b_bmk, c_bmk), (onesp, b_onesp, c_onesp),
                        (decS, b_decS, c_decS), (smask, b_smask, c_smask), (dvS, b_dvS, c_dvS), (bmkS, b_bmkS, c_bmkS)):
        dma(t_[:], src, [Bc], [b_], b_)
    memset("dve", qTbz[:], 0.0, [b_qTbz])
    memset("dve", epsb[:], EPS, [b_epsb])
    for i in range(2):
        memset("dve", vtb[0][i][0][:, :, 64:128], 1.0, [vtb[0][i][1]])
        memset("dve", vtb[1][i][0][:, :, 0:64], 1.0, [vtb[1][i][1]])

    def load_weights(l):
        def chunked(dst, dst_buf, src2d, rows_tiles, cols):
            for k in range(rows_tiles):
                for c0 in range(0, cols, 1024):
                    cw = min(1024, cols - c0)
                    dma(stg[:, 0:cw], src2d[k * 128:(k + 1) * 128, c0:c0 + cw], [Bc], [b_stg], b_stg)
                    if len(dst.shape) == 3:
                        o = dst[:, k, c0:c0 + cw]
                    else:
                        o = dst[:, c0:c0 + cw]
                    cp("pool", o, stg[:, 0:cw], [b_stg], [dst_buf])
        chunked(wuq, b_wuq, WUQ[l], 2, 1024)
        chunked(wukv, b_wukv, WUKV[l], 1, 1024)
        precast_weights(l)
        dma(gn[:], GN[l], [Bc], [b_gn], b_gn)
        dma(gc[:], GC[l], [Bc], [b_gc], b_gc)
        dma(xres[:, 0:3, :].rearrange("p a b -> p (a b)")[:, 0:2560], BT[l].rearrange("p r h q -> p (r h q)"),
            [Bc], b_xr[0:3], b_xr[0])
        xv = xres[:, 0:3, :].rearrange("p a b -> p (a b)")[:, 0:2560].rearrange("p (r h q) -> p r h q", r=5, h=4)
        for h in range(4):
            stt("dve", bbias_b[:, :, h, :], xv[:, :, h, :], 8.0, bmk[:, :, :], ALU.mult, ALU.add,
                b_xr[0:3] + [b_bmk], [b_bbias_b])

        dma(xres[:, 0:2, :].rearrange("p a b -> p (a b)")[:, 0:1536], BTS[l].rearrange("p r h q -> p (r h q)"),
            [Bc], b_xr[0:2], b_xr[0])
        xv2 = xres[:, 0:2, :].rearrange("p a b -> p (a b)")[:, 0:1536].rearrange("p (r h q) -> p r h q", r=3, h=4)
        for h in range(4):
            stt("dve", bbiasS_b[:, :, h, :], xv2[:, :, h, :], 8.0, bmkS[:, :, :], ALU.mult, ALU.add,
                b_xr[0:2] + [b_bmkS], [b_bbiasS_b])

    cur_half = [None, None]

    def need_half(l, c, hf):
        t_, b_ = w1h[hf]
        if cur_half[hf] != (l, c):
            cur_half[hf] = (l, c)
            dma(t_[:], wbf_d[(l, c, hf)], [B_wbf[(l, c, hf)]], [b_], b_)
        return t_, b_

    b_stgh = [Buf("stgA"), Buf("stgB")]

    def stg_barrier():
        P.emit("dve", lambda e: e.memset(dmy[:], 0.0), reads=(), writes=[b_dmy, b_stg, b_stgh[0], b_stgh[1]])

    def precast_weights(l):
        stg_barrier()
        for c in CH:
            src = WOUT[l] if c == 3 else (WT[l] if c == 'T' else W1[l][:, c * 1024:(c + 1) * 1024])
            for k in range(8):
                for hf in range(2):
                    dma(stg[:, hf * 512:(hf + 1) * 512], src[k * 128:(k + 1) * 128, hf * 512:(hf + 1) * 512],
                        [Bc], [b_stgh[hf]], b_stgh[hf])
                    cp("dve", w1h[hf][0][:, k, :], stg[:, hf * 512:(hf + 1) * 512], [b_stgh[hf]], [w1h[hf][1]])
            for hf in range(2):
                dma(wbf_d[(l, c, hf)], w1h[hf][0][:], [w1h[hf][1]], [B_wbf[(l, c, hf)]], w1h[hf][1])
                cur_half[hf] = (l, c)
        stg_barrier()

    GCOL = dict(qa0=0, qa1=1, qn=2, qr=3, kva=4, kr=5, kn=6, bq=7, bk=8, rg0=9, rg1=10)

    rstd_ring = [(rstd, b_rstd), (stg[:, 0:512], b_stg)]
    sq_ring = [(sq, b_sq), (junk[:, 0:512], b_junk)]
    hn_cnt = [0]

    def headnorm(raw, rows, N, bdmat, b_bd, n, reads_raw):
        nonlocal rstd, b_rstd
        hn_cnt[0] += 1
        rstd, b_rstd = rstd_ring[hn_cnt[0] % 2]
        sq_t, b_sq_t = sq_ring[hn_cnt[0] % 2]
        act(sq_t[0:rows, 0:N], raw, AF.Square, reads_raw, [b_sq_t], scale=float(n) ** -0.5)
        mm(pn[0:rows, 0:N], bdmat[0:rows, 0:rows], sq_t[0:rows, 0:N], True, True, [b_bd, b_sq_t], [b_pn])
        tsc("dve", rstd[0:rows, 0:N], pn[0:rows, 0:N], EPS, -0.5, ALU.add, ALU.pow, [b_pn], [b_rstd])

    def out_T(src_f32, rows, tok0, ntok, dst_rows_ap, reads):
        pzt, b_pzt = next_pz()
        tr(pzt[0:ntok, 0:rows], src_f32, idf[0:rows, 0:rows], reads + [b_idf], [b_pzt])
        cp("dve", otr[0:ntok, 0:rows], pzt[0:ntok, 0:rows], [b_pzt], [b_otr])
        dma(dst_rows_ap, otr[0:ntok, 0:rows], [b_otr], [], b_otr, is_out=True)

    def gen_kv(dstK, BK, dstV, BV):
        N = 512
        def k_issue(pr_):
            pz_, b_pz_ = next_pz()
            mm(pz_[:, 0:N], wukv[:, pr_ * 128:(pr_ + 1) * 128], ckvb[:], True, True, [b_wukv, b_ckvb], [b_pz_])
            return pz_, b_pz_
        k_pend = {0: k_issue(0)}
        for pr in range(4):
            if pr + 1 < 4:
                k_pend[pr + 1] = k_issue(pr + 1)
            pzt, b_pzt = k_pend.pop(pr)
            headnorm(pzt[:, 0:N], 128, N, bd64, b_bd64, 64, [b_pzt])
            for e in range(2):
                stt("dve", kTm[0:64, 2 * pr + e, :], pzt[64 * e:64 * e + 64, 0:N], gc[64 * e:64 * e + 64, 6:7],
                    rstd[64 * e:64 * e + 64, 0:N], ALU.mult, ALU.mult, [b_pzt, b_gc, b_rstd], [b_kTm])
        for t in range(4):
            pzt, b_pzt = next_pz()
            mm(pzt[:, :], ckvb[:, t * 128:(t + 1) * 128], wukv[:, 512:1024], True, True, [b_ckvb, b_wukv], [b_pzt])
            cp("dve", vtm[:, t, :], pzt[:, :], [b_pzt], [b_vtm])
        dma(dstK.rearrange("h r t -> r h t"), kTm[:], [b_kTm], [BK], b_kTm)
        dma(dstV.rearrange("(t p) c -> p t c", p=128), vtm[:], [b_vtm], [BV], b_vtm)


    def prompt_block(l, j, sample=False):
        N = 512
        t0 = SEQ if sample else j * 512
        bi = 1 if sample else j % 2
        dma(dq[:], c_dqS if sample else c_dq, [Bc], [b_dq], b_dq)
        dma(tcos[:], cosA[:, t0:t0 + N], [Bc], [b_tcos], b_tcos)
        dma(tsin[:], sinA[:, t0:t0 + N], [Bc], [b_tsin], b_tsin)
        dma(tcosB[:], cosB[:, t0:t0 + N], [Bc], [b_tcosB], b_tcosB)
        dma(tsinB[:], sinB[:, t0:t0 + N], [Bc], [b_tsinB], b_tsinB)
        for t in range(4):
            if sample and (l == 0 or t > 0):
                dma(xres[:, t, :], x_s[t * 128:(t + 1) * 128, :], [Bc], [b_xr[t]], b_xr[t])
            elif sample:
                dma(xres[:, t, :], ys_d[:, :], [B_ys], [b_xr[t]], b_xr[t])
            elif l == 0:
                dma(xres[:, t, :], x_p[t0 + t * 128:t0 + (t + 1) * 128, :], [Bc], [b_xr[t]], b_xr[t])
            else:
                dma(xres[:, t, :], y1_d[j][t * 128:(t + 1) * 128, :], [B_y1[j]], [b_xr[t]], b_xr[t])
            act(junk[:], xres[:, t, :], AF.Square, [b_xr[t]], [b_junk, b_ss1t[t]], scale=1.0 / 32.0,
                accum=ss1[:, 2 * t:2 * t + 1])
            tsc("dve", ss1[:, 2 * t + 1:2 * t + 2], ss1[:, 2 * t:2 * t + 1], EPS, -0.5, ALU.add, ALU.pow,
                [b_ss1t[t]], [b_ss1t[t]])
            stt("dve", hb[:], xres[:, t, :], ss1[:, 2 * t + 1:2 * t + 2], gn[:], ALU.mult, ALU.mult,
                [b_xr[t], b_ss1t[t], b_gn], [b_hb])
            for g in range(2):
                for kk in range(4):
                    k = g * 4 + kk
                    tr(ptr[:, kk, :], hb[:, k * 128:(k + 1) * 128], idb[:], [b_hb, b_idb], [b_ptr])
                cp("dve", hT[:, g * 4:(g + 1) * 4, t * 128:(t + 1) * 128], ptr[:, 0:4, :],
                   [b_ptr], [b_hT])

        if cfg.stage < 2:
            return

        issued = {}
        runs = [[0, 1, 4, 5], list(range(8, 16)), [16, 17], [20, 22, 21, 23]]
        nxt_of = {}
        for r_ in runs:
            for a_, b__ in zip(r_[:-1], r_[1:]):
                nxt_of[a_] = b__

        def issue(ct):
            cc = ct % 8
            wq, b_wq = need_half(l, ct // 8, cc // 4)
            c4 = cc % 4
            pzt, b_pzt = next_pz()
            for k in range(8):
                mm(pzt[:, 0:N], wq[:, k, c4 * 128:(c4 + 1) * 128], hT[:, k, 0:N], k == 0, k == 7,
                   [b_wq, b_hT], [b_pzt])
            issued[ct] = (pzt, b_pzt)

        def inproj(ct):
            if ct not in issued:
                issue(ct)
            if ct in nxt_of and nxt_of[ct] not in issued:
                issue(nxt_of[ct])
            return issued.pop(ct)

        for which, (dst, b_dst) in enumerate(((qTr, b_qTr), (kTr, b_kTr))):
            for pr in range(2):
                praw, b_praw = inproj(which * 4 + pr)
                cp("dve", ckvb[:], praw[:, 0:N], [b_praw], [b_ckvb])
                tt("dve", tA[:], praw[:, 0:N], tcos[:], ALU.mult, [b_praw, b_tcos], [b_tA])
                pswp, b_pswp = next_pz()
                mm(pswp[:, 0:N], perm64[:], ckvb[:], True, True, [b_perm64, b_ckvb], [b_pswp])
                tt("dve", tB[:], pswp[:, 0:N], tsin[:], ALU.mult, [b_pswp, b_tsin], [b_tB])
                tt("dve", dst[:, pr, :], tA[:], tB[:], ALU.add, [b_tA, b_tB], [b_dst])
        for pr in range(2):
            tt("dve", qTd[:, pr, :], qTr[:, pr, :], dq[:, pr, :], ALU.mult, [b_qTr, b_dq], [b_qTd])
        for t in range(4):
            for pr in range(2):
                tr(ptr[:, pr, :], kTr[:, pr, t * 128:(t + 1) * 128], idb[:], [b_kTr, b_idb], [b_ptr])
            cp("dve", k_tm[:, t, :], ptr[:, 0:2, :], [b_ptr], [b_k_tm])
        if cfg.stage < 2.1:
            return
        for ft, ct in enumerate(range(8, 16)):
            pg, b_pg = inproj(ct)
            act(gates[:, ft, :], pg[:, 0:N], AF.Silu, [b_pg], [b_gates])
        if cfg.stage < 2.3:
            return
        vb_t, b_vb = vba[bi]
        def issue_T(hf_, t_):
            wq_, b_wq_ = need_half(l, 'T', hf_)
            pz_, b_pz_ = next_pz()
            for k in range(8):
                mm(pz_[:, :], hT[:, k, t_ * 128:(t_ + 1) * 128], wq_[:, k, :], k == 0, k == 7,
                   [b_hT, b_wq_], [b_pz_])
            return pz_, b_pz_
        seqT = [(hf_, t_) for hf_ in range(2) for t_ in range(4)]
        pendT = {seqT[0]: issue_T(*seqT[0])}

        def get_T(hf_, t_):
            i_ = seqT.index((hf_, t_))
            if i_ + 1 < len(seqT):
                pendT[seqT[i_ + 1]] = issue_T(*seqT[i_ + 1])
            return pendT.pop((hf_, t_))
        for t in range(4):
            pzA, b_pzA = get_T(0, t)
            tsc("dve", vpad[:, t, :, :].rearrange("p h c -> p (h c)"), pzA[:, :], 0.125, None, ALU.mult, None,
                [b_pzA], [b_vpad])
            for h in range(4):
                tsc("dve", vdec[:, t, h * 128:(h + 1) * 128], pzA[:, h * 128:(h + 1) * 128], dvt[:, t, h:h + 1],
                    None, ALU.mult, None, [b_pzA, b_dvt], [b_vdec])
            if sample and t == 0:
                for e_ in range(4):
                    for h in range(4):
                        tsc("dve", vdecS[:, e_, h * 128:(h + 1) * 128], pzA[:, h * 128:(h + 1) * 128],
                            dvS[:, e_, h:h + 1], None, ALU.mult, None, [b_pzA, b_dvS], [b_vdecS])
        for t in range(4):
            pzB, b_pzB = get_T(1, t)
            tt("dve", vb_t[:, t, :, :].rearrange("p h c -> p (h c)"), pzB[:, :], onesp[:], ALU.add,
               [b_pzB, b_onesp], [b_vb])
            if (j == NBLK - 1 and not sample) or (sample and t == 0):
                for h in range(4):
                    lo = 0 if h % 2 == 0 else 64
                    cp("dve", vbf[:, t, h * 64:(h + 1) * 64], pzB[:, h * 128 + lo:h * 128 + lo + 64],
                       [b_pzB], [b_vbf])
        if cfg.stage < 2.6:
            return
        pc0, b_pc0 = inproj(16)
        act(sq[:, 0:N], pc0[:, 0:N], AF.Square, [b_pc0], [b_sq], scale=1.0 / 16.0)
        mm(pn[:, 0:N], ones[:], sq[:, 0:N], True, False, [b_ones, b_sq], [b_pn])
        pc1, b_pc1 = inproj(17)
        act(sq2[:, 0:N], pc1[:, 0:N], AF.Square, [b_pc1], [b_sq2], scale=1.0 / 16.0)
        mm(pn[:, 0:N], ones[:], sq2[:, 0:N], False, True, [b_ones, b_sq2], [b_pn])
        tsc("dve", rstd[:, 0:N], pn[:, 0:N], EPS, -0.5, ALU.add, ALU.pow, [b_pn], [b_rstd])
        stt("dve", cqT[:, 0, :], pc0[:, 0:N], gc[:, 0:1], rstd[:, 0:N], ALU.mult, ALU.mult,
            [b_pc0, b_gc, b_rstd], [b_cqT])
        stt("dve", cqT[:, 1, :], pc1[:, 0:N], gc[:, 1:2], rstd[:, 0:N], ALU.mult, ALU.mult,
            [b_pc1, b_gc, b_rstd], [b_cqT])

        def uq(c0, M):
            pzt, b_pzt = next_pz()
            for k in range(2):
                mm(pzt[0:M, 0:N], wuq[:, k, c0:c0 + M], cqT[:, k, :], k == 0, k == 1, [b_wuq, b_cqT], [b_pzt])
            return pzt, b_pzt
        for pr in range(4):
            pq, b_pq = uq(pr * 128, 128)
            headnorm(pq[:, 0:N], 128, N, bd64, b_bd64, 64, [b_pq])
            for e in range(2):
                stt("dve", qTm[0:64, 2 * pr + e, :], pq[64 * e:64 * e + 64, 0:N], gc[64 * e:64 * e + 64, 2:3],
                    rstd[64 * e:64 * e + 64, 0:N], ALU.mult, ALU.mult, [b_pq, b_gc, b_rstd], [b_qTm])
        for rp in range(4):
            pq, b_pq = uq(512 + rp * 64, 64)
            headnorm(pq[0:64, 0:N], 64, N, bd32, b_bd32, 32, [b_pq])
            stt("dve", tA[0:64, :], pq[0:64, 0:N], gc[0:64, 3:4], rstd[0:64, 0:N], ALU.mult, ALU.mult,
                [b_pq, b_gc, b_rstd], [b_tA])
            tt("dve", tA[0:64, :], tA[0:64, :], tcosB[:], ALU.mult, [b_tA, b_tcosB], [b_tA])
            pq2, b_pq2 = uq(768 + rp * 64, 64)
            stt("dve", tB[0:64, :], pq2[0:64, 0:N], gc[0:64, 11:12], rstd[0:64, 0:N], ALU.mult, ALU.mult,
                [b_pq2, b_gc, b_rstd], [b_tB])
            tt("dve", tB[0:64, :], tB[0:64, :], tsinB[:], ALU.mult, [b_tB, b_tsinB], [b_tB])
            for e in range(2):
                tt("dve", qTm[64:96, 2 * rp + e, :], tA[32 * e:32 * e + 32, :], tB[32 * e:32 * e + 32, :], ALU.add,
                   [b_tA, b_tB], [b_qTm])
        if cfg.stage < 4:
            return
        pk, b_pk = inproj(18)
        headnorm(pk[:, 0:N], 128, N, ones, b_ones, 128, [b_pk])
        stt("dve", ckvf[:], pk[:, 0:N], gc[:, 4:5], rstd[:, 0:N], ALU.mult, ALU.mult, [b_pk, b_gc, b_rstd], [b_ckvf])
        cp("dve", ckvb[:], ckvf[:], [b_ckvf], [b_ckvb])
        for t in range(1 if sample else 4):
            out_T(ckvf[:, t * 128:(t + 1) * 128], 128, t0 + t * 128, 128,
                  s_ckv[l] if sample else o_ckv[l, t0 + t * 128:t0 + (t + 1) * 128, :], [b_ckvf])
        pr_, b_pr = inproj(19)
        headnorm(pr_[0:64, 0:N], 64, N, bd32, b_bd32, 32, [b_pr])
        stt("dve", tA[0:64, :], pr_[0:64, 0:N], gc[0:64, 5:6], rstd[0:64, 0:N], ALU.mult, ALU.mult,
            [b_pr, b_gc, b_rstd], [b_tA])
        tt("dve", tB[0:32, :], tA[0:32, :], tcosB[0:32, :], ALU.mult, [b_tA, b_tcosB], [b_tB])
        tt("dve", tB[32:64, :], tA[32:64, :], tsinB[32:64, :], ALU.mult, [b_tA, b_tsinB], [b_tB])
        cp("dve", tA[0:32, :], tB[32:64, :], [b_tB], [b_tA])
        tt("dve", krf[:], tB[0:32, :], tA[0:32, :], ALU.add, [b_tA, b_tB], [b_krf])
        for h in range(8):
            cp("pool", kTm[64:96, h, :], krf[:], [b_krf], [b_kTm])
        for t in range(1 if sample else 4):
            out_T(krf[:, t * 128:(t + 1) * 128], 32, t0 + t * 128, 128,
                  s_kr[l] if sample else o_kr[l, t0 + t * 128:t0 + (t + 1) * 128, :], [b_krf])
        if sample:
            gen_kv(kTs_d[l], B_kTs[l], vs_d[l], B_vs[l])
        else:
            gen_kv(kT_d[l][j], B_kT[l][j], v_d[l][j], B_v[l][j])
        if cfg.stage < 5:
            return
        kTb_t, b_kTb = kTb[bi]
        for pr in range(2):
            pq, b_pq = inproj(20 + pr)
            headnorm(pq[:, 0:N], 128, N, bd64, b_bd64, 64, [b_pq])
            for e in range(2):
                stt("dve", qTbz[64 * e:64 * e + 64, 2 * pr + e, :], pq[64 * e:64 * e + 64, 0:N],
                    gc[64 * e:64 * e + 64, 7:8], rstd[64 * e:64 * e + 64, 0:N], ALU.mult, ALU.mult,
                    [b_pq, b_gc, b_rstd], [b_qTbz])
            pk2, b_pk2 = inproj(22 + pr)
            headnorm(pk2[:, 0:N], 128, N, bd64, b_bd64, 64, [b_pk2])
            stt("dve", kbf[:, pr, :], pk2[:, 0:N], gc[:, 8:9], rstd[:, 0:N], ALU.mult, ALU.mult,
                [b_pk2, b_gc, b_rstd], [b_kbf])
            cp("pool", kTb_t[:, pr, :], kbf[:, pr, :], [b_kbf], [b_kTb])
        if sample:
            for pr in range(2):
                out_T(kbf[:, pr, 0:128], 128, 0, 128, s_bk[l, :, pr * 128:(pr + 1) * 128], [b_kbf])
            dma(s_bv[l, :, :], vbf[:, 0, :], [b_vbf], [], b_vbf, is_out=True)
            sample_mixers(l)
            return
        if j == NBLK - 1:
            for t in range(4):
                for pr in range(2):
                    out_T(kbf[:, pr, t * 128:(t + 1) * 128], 128, 0, 128,
                          o_bk[l, t * 128:(t + 1) * 128, pr * 128:(pr + 1) * 128], [b_kbf])
                dma(o_bv[l, t * 128:(t + 1) * 128, :], vbf[:, t, :], [b_vbf], [], b_vbf, is_out=True)

        if cfg.stage < 5.1:
            return
        if j == 0:
            memset("dve", state[:], 0.0, [b_state])
            memset("dve", stpad_t[:], 0.0, [b_stpad])
        for pr in range(2):
            po_t, b_po = pO[pr % 2]
            first = True
            for e in range(2):
                h = 2 * pr + e
                r0 = 64 * e
                for kt in range(4):
                    psT, b_psT = pS[(h * 4 + kt) % 2]
                    l0 = kt * 128
                    mm(psT[:, l0:N], kTr[r0:r0 + 64, pr, l0:l0 + 128], qTr[r0:r0 + 64, pr, l0:N], True, True,
                       [b_kTr, b_qTr], [b_psT])
                    for lt in range(kt, 4):
                        tt("dve", innT[:, lt * 128:(lt + 1) * 128], psT[:, lt * 128:(lt + 1) * 128],
                           dec[:, h, lt - kt, :], ALU.mult, [b_psT, b_dec], [b_innT])
                    mm(po_t[:, l0:N], vpad[:, kt, h, :], innT[:, l0:N], first and kt == 0, False,
                       [b_vpad, b_innT], [b_po])
                    first = False
            if cfg.stage < 5.4:
                continue
            P_state_term(pr, po_t, b_po)
            if cfg.stage < 5.6:
                continue
            headnorm(po_t[:, 0:N], 128, N, bd64, b_bd64, 64, [b_po])
            stt("dve", tmpm[:], po_t[:, 0:N], gc[:, 9 + pr:10 + pr], rstd[:, 0:N], ALU.mult, ALU.mult,
                [b_po, b_gc, b_rstd], [b_tmpm])
            tt("dve", mixT[:, pr, :], tmpm[:], gates[:, pr, :], ALU.mult, [b_tmpm, b_gates], [b_mixT])
        if cfg.stage < 5.8:
            return
        gam = _ret_gamma()
        for h in range(4):
            pzt, b_pzt = next_pz()
            for t in range(4):
                mm(pzt[0:64, 0:64], k_tm[:, t, h * 64:(h + 1) * 64],
                   vdec[:, t, h * 128 + 64 * (h % 2):h * 128 + 64 * (h % 2) + 64],
                   t == 0, t == 3, [b_k_tm, b_vdec], [b_pzt])
            stt("dve", state[:, h, :], state[:, h, :], float(gam[h] ** 512), pzt[0:64, 0:64], ALU.mult, ALU.add,
                [b_state, b_pzt], [b_state])
            cp("dve", stpad_t[64 * (h % 2):64 * (h % 2) + 64, h // 2, 64 * (h % 2):64 * (h % 2) + 64],
               state[:, h, :], [b_state], [b_stpad])
        if j == NBLK - 1 and cfg.stage >= 5.9:
            dma(o_state[l].rearrange("h k v -> k h v"), state[:], [b_state], [], b_state, is_out=True)
        if cfg.stage < 7:
            return
        qscale = 96.0 ** -0.5
        cnt = [0]
        SKEW = 3
        items = [(h, kb, kt) for h in range(8) for kb in range(j + 1) for kt in range(4)]
        loaded = {}
        sbuf_of = {}

        def emit_S(idx):
            h, kb, kt = items[idx]
            if kt == 0:
                kt_t, b_kt = ktb[cnt[0] % 2]
                vt_t, b_vt = vtb[h % 2][(cnt[0] // 1) % 2]
                cnt[0] += 1
                dma(kt_t[:, :], kT_d[l][kb][h], [B_kT[l][kb]], [b_kt], b_kt)
                lo = 0 if h % 2 == 0 else 64
                dma(vt_t[:, :, lo:lo + 64],
                    v_d[l][kb].rearrange("(t p) c -> p t c", p=128)[:, :, h * 64:(h + 1) * 64],
                    [B_v[l][kb]], [b_vt], b_vt)
                loaded[(h, kb)] = (kt_t, b_kt, vt_t, b_vt)
            kt_t, b_kt, vt_t, b_vt = loaded[(h, kb)]
            psT, b_psT = pS[idx % 4]
            pT_t, b_pT = pT[idx % 4]
            mm(psT[:, 0:N], kt_t[:, kt * 128:(kt + 1) * 128], qTm[:, h, :], True, True, [b_kt, b_qTm], [b_psT])
            act(pT_t[:, 0:N], psT[:, 0:N], AF.Exp, [b_psT], [b_pT], scale=qscale)
            if kb == j:
                tt("dve", pT_t[:, 0:N], pT_t[:, 0:N], mmask[:, kt, :], ALU.mult, [b_pT, b_mmask], [b_pT])

        def emit_PV(idx):
            h, kb, kt = items[idx]
            kt_t, b_kt, vt_t, b_vt = loaded[(h, kb)]
            po_t, b_po = pO[h % 2]
            pT_t, b_pT = pT[idx % 4]
            first = (kb == 0 and kt == 0)
            last = (kb == j and kt == 3)
            mm(po_t[:, 0:N], vt_t[:, kt, :], pT_t[:, 0:N], first, last, [b_vt, b_pT], [b_po])
            if last and h % 2 == 1:
                finalize_pair(2 + h // 2)
        for i in range(len(items) + SKEW):
            if i < len(items):
                emit_S(i)
            if i >= SKEW:
                emit_PV(i - SKEW)
        bscale = 0.125
        kcur, b_kcur = kTb[j % 2]
        kprev, b_kprev = kTb[(j - 1) % 2]
        vcur, b_vcur = vba[j % 2]
        vprev, b_vprev = vba[(j - 1) % 2]
        groups = []
        for h in range(4):
            for qt in range(4):
                units = []
                if j > 0:
                    for i in range(qt, 4):
                        units.append((kprev, b_kprev, vprev, b_vprev, i, i - 4 - qt))
                for i in range(0, qt + 1):
                    units.append((kcur, b_kcur, vcur, b_vcur, i, i - qt))
                for g0 in range(0, len(units), 4):
                    groups.append((h, qt, g0, units[g0:g0 + 4], len(units)))

        def band_S(gi):
            h, qt, g0, grp, nun = groups[gi]
            pr = h // 2
            psT, b_psT = pS[gi % 4]
            pT_t, b_pT = pT[gi % 4]
            for ui, (kt_, b_k_, v_, b_v_, i, r) in enumerate(grp):
                mm(psT[:, ui * 128:(ui + 1) * 128], kt_[:, pr, i * 128:(i + 1) * 128],
                   qTbz[:, h, qt * 128:(qt + 1) * 128], True, False, [b_k_, b_qTbz], [b_psT])
                mm(psT[:, ui * 128:(ui + 1) * 128], idb[:], bbias_b[:, r + 4, h, :], False, True,
                   [b_idb, b_bbias_b], [b_psT])
            w = len(grp) * 128
            act(pT_t[:, 0:w], psT[:, 0:w], AF.Exp, [b_psT], [b_pT], scale=bscale)

        def band_PV(gi):
            h, qt, g0, grp, nun = groups[gi]
            po_t, b_po = pO[h % 2]
            pT_t, b_pT = pT[gi % 4]
            for ui, (kt_, b_k_, v_, b_v_, i, r) in enumerate(grp):
                u_ = g0 + ui
                mm(po_t[:, qt * 128:(qt + 1) * 128], v_[:, i, h, :], pT_t[:, ui * 128:(ui + 1) * 128],
                   (qt == 0 and u_ == 0), (qt == 3 and u_ == nun - 1), [b_v_, b_pT], [b_po])
            if h % 2 == 1 and qt == 3 and g0 + len(grp) == nun:
                finalize_pair(6 + h // 2)
        BSK = 2
        for gi in range(len(groups) + BSK):
            if gi < len(groups):
                band_S(gi)
            if gi >= BSK:
                band_PV(gi - BSK)
        for half in range(2):
            wq, b_wq = need_half(l, 3, half)
            for t in range(4):
                pzt, b_pzt = next_pz()
                for ft in range(8):
                    mm(pzt[:, :], mixT[:, ft, t * 128:(t + 1) * 128], wq[:, ft, :],
                       ft == 0, ft == 7, [b_mixT, b_wq], [b_pzt])
                tt("dve", xres[:, t, half * 512:(half + 1) * 512], xres[:, t, half * 512:(half + 1) * 512],
                   pzt[:, :], ALU.add, [b_xr[t], b_pzt], [b_xr[t]])
        for t in range(4):
            if l == DEPTH - 1:
                dma(y_p[t0 + t * 128:t0 + (t + 1) * 128, :], xres[:, t, :], [b_xr[t]], [], b_xr[t], is_out=True)
            else:
                dma(y1_d[j][t * 128:(t + 1) * 128, :], xres[:, t, :], [b_xr[t]], [B_y1[j]], b_xr[t])

    def sample_mixers(l):
        NS = 128
        gam = _ret_gamma()
        dma(stS[:].rearrange("k (e h) v -> k e h v", e=4), st_in[l].rearrange("e h k v -> k e h v"),
            [Bc], [b_stS], b_stS)
        memset("dve", stpadS[:], 0.0, [b_stpadS])
        for e in range(4):
            for h in range(4):
                o = 64 * (h % 2)
                cp("dve", stpadS[o:o + 64, e, h // 2, o:o + 64], stS[:, e * 4 + h, :], [b_stS], [b_stpadS])
        for pr in range(2):
            po_t, b_po = pO[pr % 2]
            for e2 in range(2):
                h = 2 * pr + e2
                r0 = 64 * e2
                psT, b_psT = pS[h % 2]
                mm(psT[:, 0:NS], kTr[r0:r0 + 64, pr, 0:NS], qTr[r0:r0 + 64, pr, 0:NS], True, True,
                   [b_kTr, b_qTr], [b_psT])
                tt("dve", innT[:, 0:NS], psT[:, 0:NS], decS[:, h, :], ALU.mult, [b_psT, b_decS], [b_innT])
                mm(po_t[:, 0:NS], vpad[:, 0, h, :], innT[:, 0:NS], e2 == 0, False, [b_vpad, b_innT], [b_po])
            for e in range(4):
                mm(po_t[:, e * 32:(e + 1) * 32], stpadS[:, e, pr, :], qTd[:, pr, e * 32:(e + 1) * 32], False, e == 3,
                   [b_stpadS, b_qTd], [b_po])
            headnorm(po_t[:, 0:NS], 128, NS, bd64, b_bd64, 64, [b_po])
            stt("dve", tmpm[:, 0:NS], po_t[:, 0:NS], gc[:, 9 + pr:10 + pr], rstd[:, 0:NS], ALU.mult, ALU.mult,
                [b_po, b_gc, b_rstd], [b_tmpm])
            tt("dve", mixT[:, pr, 0:NS], tmpm[:, 0:NS], gates[:, pr, 0:NS], ALU.mult, [b_tmpm, b_gates], [b_mixT])
        for e in range(4):
            pzt, b_pzt = next_pz()
            for h in range(4):
                lo = 64 * (h % 2)
                mm(pzt[0:64, h * 64:(h + 1) * 64], k_tm[:, 0, h * 64:(h + 1) * 64],
                   vdecS[:, e, h * 128 + lo:h * 128 + lo + 64], True, True, [b_k_tm, b_vdecS], [b_pzt])
            for h in range(4):
                stt("dve", stS[:, e * 4 + h, :], stS[:, e * 4 + h, :], float(gam[h] ** 32),
                    pzt[0:64, h * 64:(h + 1) * 64], ALU.mult, ALU.add, [b_stS, b_pzt], [b_stS])
        dma(s_state[l].rearrange("e h k v -> k (e h) v"), stS[:], [b_stS], [], b_stS, is_out=True)
        for e in range(4):
            for hf in range(2):
                dma(stg[:, 0:512], cckvT[l, e][:, hf * 512:(hf + 1) * 512], [Bc], [b_stg], b_stg)
                cp("dve", ckvb[:], stg[:, 0:512], [b_stg], [b_ckvb])
                dma(stg[0:32, 512:1024], ckrT[l, e][:, hf * 512:(hf + 1) * 512], [Bc], [b_stg], b_stg)
                for h in range(8):
                    cp("dve", kTm[64:96, h, :], stg[0:32, 512:1024], [b_stg], [b_kTm])
                gen_kv(kTc_d[l][e][hf], B_kTc[l][e][hf], vc_d[l][e][hf], B_vc[l][e][hf])
        qscale = 96.0 ** -0.5
        cnt = [0]

        def load_kv(h, srcK, BK, srcV, BV):
            kt_t, b_kt = ktb[cnt[0] % 2]
            vt_t, b_vt = vtb[h % 2][cnt[0] % 2]
            cnt[0] += 1
            lo = 0 if h % 2 == 0 else 64
            dma(kt_t[:, :], srcK[h], [BK], [b_kt], b_kt)
            dma(vt_t[:, :, lo:lo + 64], srcV.rearrange("(t p) c -> p t c", p=128)[:, :, h * 64:(h + 1) * 64],
                [BV], [b_vt], b_vt)
            return kt_t, b_kt, vt_t, b_vt
        for h in range(8):
            po_t, b_po = pO[h % 2]
            kt_t, b_kt, vt_t, b_vt = load_kv(h, kTs_d[l], B_kTs[l], vs_d[l], B_vs[l])
            psT, b_psT = pS[0]
            pT_t, b_pT = pT[0]
            mm(psT[:, 0:NS], kt_t[:, 0:128], qTm[:, h, 0:NS], True, True, [b_kt, b_qTm], [b_psT])
            act(pT_t[:, 0:NS], psT[:, 0:NS], AF.Exp, [b_psT], [b_pT], scale=qscale)
            tt("dve", pT_t[:, 0:NS], pT_t[:, 0:NS], smask[:, :], ALU.mult, [b_pT, b_smask], [b_pT])
            mm(po_t[:, 0:NS], vt_t[:, 0, :], pT_t[:, 0:NS], True, False, [b_vt, b_pT], [b_po])
            for e in range(4):
                for hf in range(2):
                    kt_t, b_kt, vt_t, b_vt = load_kv(h, kTc_d[l][e][hf], B_kTc[l][e][hf], vc_d[l][e][hf], B_vc[l][e][hf])
                    psT, b_psT = pS[cnt[0] % 4]
                    pT_t, b_pT = pT[cnt[0] % 4]
                    for kt in range(4):
                        mm(psT[:, kt * 32:(kt + 1) * 32], kt_t[:, kt * 128:(kt + 1) * 128],
                           qTm[:, h, e * 32:(e + 1) * 32], True, True, [b_kt, b_qTm], [b_psT])
                    act(pT_t[:, 0:128], psT[:, 0:128], AF.Exp, [b_psT], [b_pT], scale=qscale)
                    for kt in range(4):
                        mm(po_t[:, e * 32:(e + 1) * 32], vt_t[:, kt, :], pT_t[:, kt * 32:(kt + 1) * 32], False,
                           (e == 3 and hf == 1 and kt == 3), [b_vt, b_pT], [b_po])
            if h % 2 == 1:
                finalize_pair(2 + h // 2, NS)
        kn_t, b_kn = kTb[1]
        vn_t, b_vn = vba[1]
        kc_t, b_kc = kTb[0]
        vc_t, b_vc = vba[0]
        for pr in range(2):
            for e2 in range(2):
                h = 2 * pr + e2
                po_t, b_po = pO[h % 2]
                psT, b_psT = pS[cnt[0] % 4]
                pT_t, b_pT = pT[cnt[0] % 4]
                cnt[0] += 1
                mm(psT[:, 0:NS], kn_t[:, pr, 0:128], qTbz[:, h, 0:NS], True, False, [b_kn, b_qTbz], [b_psT])
                mm(psT[:, 0:NS], idb[:], bbiasS_b[:, 2, h, :], False, True, [b_idb, b_bbiasS_b], [b_psT])
                act(pT_t[:, 0:NS], psT[:, 0:NS], AF.Exp, [b_psT], [b_pT], scale=0.125)
                mm(po_t[:, 0:NS], vn_t[:, 0, h, :], pT_t[:, 0:NS], True, False, [b_vn, b_pT], [b_po])
            for e in range(4):
                dma(stg[:, 0:512], cbkT[l, e][pr * 128:(pr + 1) * 128, :], [Bc], [b_stg], b_stg)
                cp("dve", kc_t[:, 0, :], stg[:, 0:512], [b_stg], [b_kc])
                dma(stg[:, :].rearrange("p (t c) -> p t c", t=4),
                    cbv[l, e].rearrange("(t p) c -> p t c", p=128)[:, :, pr * 256:(pr + 1) * 256],
                    [Bc], [b_stg], b_stg)
                for kt in range(4):
                    tt("dve", vc_t[:, kt, 2 * pr:2 * pr + 2, :].rearrange("p h c -> p (h c)"),
                       stg[:, kt * 256:(kt + 1) * 256], onesp[:, pr * 256:(pr + 1) * 256], ALU.add,
                       [b_stg, b_onesp], [b_vc])
                for e2 in range(2):
                    h = 2 * pr + e2
                    po_t, b_po = pO[h % 2]
                    psT, b_psT = pS[cnt[0] % 4]
                    pT_t, b_pT = pT[cnt[0] % 4]
                    cnt[0] += 1
                    for kt in range(4):
                        mm(psT[:, kt * 32:(kt + 1) * 32], kc_t[:, 0, kt * 128:(kt + 1) * 128],
                           qTbz[:, h, e * 32:(e + 1) * 32], True, False, [b_kc, b_qTbz], [b_psT])
                        mm(psT[:, kt * 32:(kt + 1) * 32], idb[:], bbiasS_b[:, 0 if kt < 3 else 1, h, e * 32:(e + 1) * 32],
                           False, True, [b_idb, b_bbiasS_b], [b_psT])
                    act(pT_t[:, 0:128], psT[:, 0:128], AF.Exp, [b_psT], [b_pT], scale=0.125)
                    for kt in range(4):
                        mm(po_t[:, e * 32:(e + 1) * 32], vc_t[:, kt, h, :], pT_t[:, kt * 32:(kt + 1) * 32], False,
                           (e == 3 and kt == 3), [b_vc, b_pT], [b_po])
            finalize_pair(6 + pr, NS)
        for half in range(2):
            wq, b_wq = need_half(l, 3, half)
            pzt, b_pzt = next_pz()
            for ft in range(8):
                mm(pzt[:, :], mixT[:, ft, 0:128], wq[:, ft, :], ft == 0, ft == 7,
                   [b_mixT, b_wq], [b_pzt])
            tt("dve", xres[:, 0, half * 512:(half + 1) * 512], xres[:, 0, half * 512:(half + 1) * 512],
               pzt[:, :], ALU.add, [b_xr[0], b_pzt], [b_xr[0]])
        if l == DEPTH - 1:
            dma(y_s[:, :], xres[:, 0, :], [b_xr[0]], [], b_xr[0], is_out=True)
        else:
            dma(ys_d[:, :], xres[:, 0, :], [b_xr[0]], [B_ys], b_xr[0])

    def finalize_pair(ft, N=512):
        (pe_t, b_pe), (po2_t, b_po2) = pO
        act(rinv[64:128, 0, 0:N], pe_t[64:128, 0:N], AF.Ln, [b_pe], [b_rinv])
        act(rinv[0:64, 0, 0:N], po2_t[0:64, 0:N], AF.Ln, [b_po2], [b_rinv])
        act(rinv[:, 0, 0:N], rinv[:, 0, 0:N], AF.Exp, [b_rinv], [b_rinv], scale=-1.0)
        mm(pn[:, 0:N], sel[:, 0, :], rinv[:, 0, 0:N], True, True, [b_sel, b_rinv], [b_pn])
        cp("dve", rbs[:, 0:N], pn[:, 0:N], [b_pn], [b_rbs])
        tt("dve", tmpm[0:64, 0:N], pe_t[0:64, 0:N], rbs[0:64, 0:N], ALU.mult, [b_pe, b_rbs], [b_tmpm])
        tt("dve", tmpm[64:128, 0:N], po2_t[64:128, 0:N], rbs[64:128, 0:N], ALU.mult, [b_po2, b_rbs], [b_tmpm])
        tt("dve", mixT[:, ft, 0:N], tmpm[:, 0:N], gates[:, ft, 0:N], ALU.mult, [b_tmpm, b_gates], [b_mixT])

    stpad_t, b_stpad = T("stpad", [128, 2, 128], BF16)

    def P_state_term(pr, po_t, b_po):
        mm(po_t[:, 0:512], stpad_t[:, pr, :], qTd[:, pr, :], False, True, [b_stpad, b_qTd], [b_po])

    for l in range(DEPTH):
        load_weights(l)
        if cfg.with_sample:
            prompt_block(l, 0, sample=True)
        for j in range(NBLK if cfg.stage >= 1 else 0):
            prompt_block(l, j)
    P.finish()
    P.build(nc, es)
    es.close()
    return nc


def _bf(a):
    return np.ascontiguousarray(a.astype(ml_dtypes.bfloat16))


def make_consts(cfg):
    SEQ = cfg.seq
    c = {}
    pos = np.concatenate([np.arange(SEQ), cfg.past + (np.arange(128) % 32), np.zeros(384, np.int64)])
    c["cosA"], c["sinA"] = _rope_tables(pos, 64, 128)
    c["cosB"], c["sinB"] = _rope_tables(pos, 32, 64)
    c["c_idb"] = _bf(np.eye(128, dtype=np.float32))
    c["c_idf"] = np.eye(128, dtype=np.float32)
    bd64 = np.kron(np.eye(2), np.ones((64, 64))).astype(np.float32)
    bd32 = np.kron(np.eye(4), np.ones((32, 32))).astype(np.float32)
    c["c_bd64"] = _bf(bd64)
    c["c_bd32"] = _bf(bd32)
    c["c_ones"] = _bf(np.ones((128, 128), np.float32))
    pm = np.zeros((128, 128), np.float32)
    for m_ in range(128):
        pm[(m_ // 64) * 64 + ((m_ % 64) + 32) % 64, m_] = 1.0
    c["c_perm64"] = _bf(pm)
    sel = np.zeros((128, 2, 128), np.float32)
    for k in range(64):
        sel[64 + k, 0, k] = 1.0
        sel[k, 0, 64 + k] = 1.0
    c["c_sel"] = sel
    kpos = (np.arange(4)[None, :, None] * 128 + np.arange(128)[:, None, None])
    qpos = np.arange(512)[None, None, :]
    vis = (kpos // CHUNK) <= (qpos // CHUNK)
    c["c_mmask"] = _bf(np.where(vis, 1.0, 0.0).astype(np.float32))
    gam = _ret_gamma()
    m = np.arange(128)[:, None]
    ll = np.arange(128)[None, :]
    dec = np.zeros((128, 4, 4, 128), np.float64)
    for h in range(4):
        for d in range(4):
            dist = ll - m + 128 * d
            dec[:, h, d, :] = np.where(dist >= 0, gam[h] ** np.maximum(dist, 0), 0.0)
    c["c_dec"] = _bf(dec.astype(np.float32))
    p = np.arange(512)
    dqt = np.zeros((128, 2, 512), np.float64)
    for pr in range(2):
        for e in range(2):
            dqt[64 * e:64 * e + 64, pr, :] = (gam[2 * pr + e] ** (p + 1.0))[None, :]
    c["c_dq"] = dqt.astype(np.float32)
    dv = np.zeros((128, 4, 4), np.float64)
    onesp = np.zeros((128, 4, 128), np.float32)
    for h in range(4):
        lo = 0 if h % 2 == 0 else 64
        onesp[:, h, 64 - lo:128 - lo] = 1.0
        for t in range(4):
            dv[:, t, h] = 0.125 * gam[h] ** (511.0 - (t * 128 + np.arange(128)))
    c["c_dv"] = dv.astype(np.float32)
    c["c_onesp"] = onesp.reshape(128, 512)
    kp = np.arange(128)[:, None, None]
    qp = np.arange(128)[None, None, :]
    rr = (np.arange(5) - 4)[None, :, None]
    dchunk = -2 * rr + qp // 64 - kp // 64
    c["c_bmk"] = np.where((dchunk >= 0) & (dchunk <= 8), 0.0, -30000.0).astype(np.float32)
    gam = _ret_gamma()
    tok = np.arange(128)
    el, ii = tok // 32, tok % 32
    dqs = np.zeros((128, 2, 512), np.float64)
    for pr in range(2):
        for e in range(2):
            dqs[64 * e:64 * e + 64, pr, :] = (gam[2 * pr + e] ** ((np.arange(512) % 32) + 1.0))[None, :]
    c["c_dqS"] = dqs.astype(np.float32)
    dvs = np.zeros((128, 4, 4), np.float64)
    for e in range(4):
        for h in range(4):
            dvs[:, e, h] = np.where(el == e, 0.125 * gam[h] ** (31.0 - ii), 0.0)
    c["c_dvS"] = dvs.astype(np.float32)
    same = (el[:, None] == el[None, :])
    dif = ii[None, :] - ii[:, None]
    decs = np.zeros((128, 4, 128), np.float64)
    for h in range(4):
        decs[:, h, :] = np.where(same & (dif >= 0), gam[h] ** np.maximum(dif, 0), 0.0)
    c["c_decS"] = _bf(decs.astype(np.float32))
    c["c_smask"] = _bf(same.astype(np.float32))
    bms = np.zeros((128, 3, 128), np.float32)
    bms[:, 2, :] = np.where(same, 0.0, -30000.0)
    c["c_bmkS"] = bms
    return c


def make_weights(cfg, inp):
    L = cfg.depth
    w_in = np.asarray(inp["w_in"])[:L]
    cuts = np.cumsum([256, 256, 256, 256, 256, 128, 32, 512, 256, 256, 256, 256])[:-1]
    a_q, a_k, a_v, a_g, b_cq, b_ckv, b_kr, b_g, c_q, c_k, c_v, c_g = np.split(w_in, cuts, axis=-1)
    z64 = np.zeros(b_kr.shape[:-1] + (64,), np.float32)
    W1 = np.concatenate([a_q, _swap_cols(a_q, 64), a_k, _swap_cols(a_k, 64), a_g, b_g, c_g, b_cq, b_ckv,
                         np.concatenate([b_kr, _swap_cols(b_kr, 32), z64], -1), c_q, c_k], -1)
    assert W1.shape[-1] == NCT * 128, W1.shape
    def padheads(w):
        L_, R_, _ = w.shape
        o = np.zeros((L_, R_, 4, 128), np.float32)
        for h in range(4):
            lo = 0 if h % 2 == 0 else 64
            o[:, :, h, lo:lo + 64] = w[:, :, h * 64:(h + 1) * 64]
        return o.reshape(L_, R_, 512)
    WT = np.concatenate([padheads(a_v), padheads(c_v)], -1)
    wuq = np.asarray(inp["mla_w_uq"])[:L].reshape(L, 256, 8, 96)
    nope = wuq[..., :64].reshape(L, 256, 512)
    rope = wuq[..., 64:].reshape(L, 256, 256)
    WUQ = np.concatenate([nope, rope, _swap_cols(rope, 32)], -1)
    wukv = np.asarray(inp["mla_w_ukv"])[:L].reshape(L, 128, 8, 128)
    WUKV = np.concatenate([wukv[..., :64].reshape(L, 128, 512), wukv[..., 64:].reshape(L, 128, 512)], -1)
    GN = np.broadcast_to(np.asarray(inp["norm_g"])[:L, None, :], (L, 128, D)).copy()
    GC = np.zeros((L, 128, 16), np.float32)
    qa = np.asarray(inp["mla_qa_g"])[:L]
    GC[:, :, 0] = qa[:, :128]
    GC[:, :, 1] = qa[:, 128:]
    GC[:, :, 2] = np.tile(np.asarray(inp["mla_qn_g"])[:L], (1, 2))
    qr = np.asarray(inp["mla_qr_g"])[:L]
    GC[:, :, 3] = np.tile(qr, (1, 4))
    GC[:, :, 11] = np.tile(_swap_cols(qr, 32), (1, 4))
    GC[:, :, 4] = np.asarray(inp["mla_kva_g"])[:L]
    kr = np.asarray(inp["mla_kr_g"])[:L]
    GC[:, :64, 5] = np.concatenate([kr, _swap_cols(kr, 32)], -1)
    GC[:, :, 6] = np.tile(np.asarray(inp["mla_kn_g"])[:L], (1, 2))
    GC[:, :, 7] = np.tile(np.asarray(inp["band_qn_g"])[:L], (1, 2))
    GC[:, :, 8] = np.tile(np.asarray(inp["band_kn_g"])[:L], (1, 2))
    rg = np.asarray(inp["ret_gn_g"])[:L]
    GC[:, :, 9] = rg[:, :128]
    GC[:, :, 10] = rg[:, 128:]
    bb = np.asarray(inp["band_bias"])[:L]
    kp = np.arange(128)[:, None, None]
    qp = np.arange(128)[None, None, :]
    rr = (np.arange(5) - 4)[None, :, None]
    bidx = np.clip(qp - kp - 128 * rr, -128, 128) + 128
    BT = np.ascontiguousarray(bb[:, :, bidx].transpose(0, 2, 3, 1, 4))
    kp = np.arange(128)[:, None]
    iq = (np.arange(128) % 32)[None, :]
    ik = (np.arange(128) % 32)[:, None]
    i0 = np.full((128, 128), 256)
    i1 = np.clip(128 + iq - kp, -128, 128) + 128
    i2 = np.clip(iq - ik, -128, 128) + 128
    sidx = np.stack([i0, i1, i2], 1)
    BTS = np.ascontiguousarray(bb[:, :, sidx].transpose(0, 2, 3, 1, 4))
    return dict(BTS=BTS, W1=np.ascontiguousarray(W1), WT=np.ascontiguousarray(WT), WUQ=np.ascontiguousarray(WUQ),
                WUKV=np.ascontiguousarray(WUKV), WOUT=np.ascontiguousarray(np.asarray(inp["w_out"])[:L]),
                GN=GN, GC=GC, BT=BT)


def make_core_inputs(cfg, inp, c):
    L = cfg.depth
    m = {}
    m["x_p"] = np.ascontiguousarray(np.asarray(inp["x_prompt"])[c // 4, :cfg.seq])
    xs = np.zeros((512, D), np.float32)
    xs[:128] = np.asarray(inp["x_sample"])[4 * c:4 * c + 4].reshape(128, D)
    m["x_s"] = xs
    m["st_in"] = np.ascontiguousarray(np.asarray(inp["state_ret"])[:L, 4 * c:4 * c + 4])
    m["cckvT"] = np.ascontiguousarray(np.asarray(inp["cache_mla_ckv"])[:L, 4 * c:4 * c + 4].transpose(0, 1, 3, 2))
    m["ckrT"] = np.ascontiguousarray(np.asarray(inp["cache_mla_krope"])[:L, 4 * c:4 * c + 4].transpose(0, 1, 3, 2))
    bk = np.asarray(inp["cache_band_k"])[:L, 4 * c:4 * c + 4].reshape(L, 4, 512, 256)
    m["cbkT"] = np.ascontiguousarray(bk.transpose(0, 1, 3, 2))
    bv = np.asarray(inp["cache_band_v"])[:L, 4 * c:4 * c + 4]
    bvp = np.zeros((L, 4, 512, 4, 128), np.float32)
    for h in range(4):
        lo = 0 if h % 2 == 0 else 64
        bvp[:, :, :, h, lo:lo + 64] = bv[:, :, :, h, :]
    m["cbv"] = bvp.reshape(L, 4, 512, 512)
    return m


_PROG = {}


def kernel(**inputs):
    cfg = Cfg(seq=8192, depth=2, past=1024, with_sample=True)
    if "nc" not in _PROG:
        _PROG["nc"] = build_program(cfg)
    nc = _PROG["nc"]
    inp = {k: np.asarray(v) for k, v in inputs.items()}
    consts = make_consts(cfg)
    w = make_weights(cfg, inp)
    maps = []
    for c in range(8):
        m = dict(consts)
        m.update(w)
        m.update(make_core_inputs(cfg, inp, c))
        maps.append(m)
    res = run_bass_kernel_spmd(nc, maps, core_ids=list(range(8))).results
    L, S = cfg.depth, cfg.seq
    f = np.float32
    y_p = np.stack([res[0]["y_p"], res[4]["y_p"]]).astype(f)
    y_s = np.concatenate([res[c]["y_s"].reshape(4, 32, D) for c in range(8)], 0).astype(f)
    p_state = np.stack([res[0]["o_state"], res[4]["o_state"]], 1).astype(f)
    p_ckv = np.stack([res[0]["o_ckv"], res[4]["o_ckv"]], 1).astype(f)
    p_kr = np.stack([res[0]["o_kr"], res[4]["o_kr"]], 1).astype(f)
    p_bk = np.stack([res[0]["o_bk"], res[4]["o_bk"]], 1).reshape(L, 2, 512, 4, 64).astype(f)
    p_bv = np.stack([res[0]["o_bv"], res[4]["o_bv"]], 1).reshape(L, 2, 512, 4, 64).astype(f)
    s_state = np.concatenate([res[c]["s_state"] for c in range(8)], 1).astype(f)
    s_ckv = np.concatenate([res[c]["s_ckv"].reshape(L, 4, 32, 128) for c in range(8)], 1).astype(f)
    s_kr = np.concatenate([res[c]["s_kr"].reshape(L, 4, 32, 32) for c in range(8)], 1).astype(f)
    s_bk = np.concatenate([res[c]["s_bk"].reshape(L, 4, 32, 4, 64) for c in range(8)], 1).astype(f)
    s_bv = np.concatenate([res[c]["s_bv"].reshape(L, 4, 32, 4, 64) for c in range(8)], 1).astype(f)
    return (y_p, y_s, p_state, p_ckv, p_kr, p_bk, p_bv, s_state, s_ckv, s_kr, s_bk, s_bv)
```

```python
import contextlib
import numpy as np
import ml_dtypes
import concourse.bass as bass
import concourse.mybir as mybir
from concourse.bass_utils import run_bass_kernel_spmd

F32 = mybir.dt.float32
BF16 = mybir.dt.bfloat16
ALU = mybir.AluOpType
AF = mybir.ActivationFunctionType
AX = mybir.AxisListType

D = 1024
EPS = 1e-6
NCT = 24
CHUNK = 64


class Buf:
    def __init__(self, name):
        self.name = name
        self.w = []
        self.r = []
        self.sem = None
        self.dma_total = 0


class Op:
    __slots__ = ("eng", "fn", "deps", "is_dma", "dsem", "dval", "sig", "sval", "idx", "tag")

    def __init__(self, eng, fn):
        self.eng = eng
        self.fn = fn
        self.deps = []
        self.is_dma = False
        self.dsem = None
        self.dval = 0
        self.sig = False
        self.sval = 0
        self.tag = ''


class Prog:
    ENGS = ("pe", "act", "dve", "pool", "sp")

    def __init__(self):
        self.ops = {e: [] for e in self.ENGS}
        self.dma_bufs = []
        self.out_dmas = []
        self.trace = None

    SKIP_SAME_ENGINE_WAW_WAR = True

    def _dep(self, op, prod, kind="raw"):
        if prod is op:
            return
        if (not prod.is_dma) and prod.eng == "pe" and op.eng == "pe" and not op.is_dma:
            return
        if (self.SKIP_SAME_ENGINE_WAW_WAR and kind != "raw" and (not prod.is_dma) and (not op.is_dma)
                and prod.eng == op.eng):
            return
        op.deps.append(prod)

    def emit(self, eng, fn, reads=(), writes=(), dma_buf=None, is_out=False):
        op = Op(eng, fn)
        for b in reads:
            for p in b.w:
                self._dep(op, p)
        for b in writes:
            for p in b.w:
                self._dep(op, p, "waw")
            for p in b.r:
                self._dep(op, p, "war")
        if dma_buf is not None:
            op.is_dma = True
            if dma_buf.sem is None:
                dma_buf.sem = len(self.dma_bufs)
                self.dma_bufs.append(dma_buf)
            dma_buf.dma_total += 16
            op.dsem = dma_buf
            op.dval = dma_buf.dma_total
        for b in reads:
            b.r.append(op)
        for b in writes:
            b.w = [op]
            b.r = []
        op.tag = ','.join(b.name for b in reads) + '->' + ','.join(b.name for b in writes)
        self.ops[eng].append(op)
        if is_out:
            self.out_dmas.append(op)
        return op

    def finish(self):
        op = Op("sp", None)
        op.deps = list(self.out_dmas)
        self.ops["sp"].append(op)

    def build(self, nc, es):
        n_dma_sems = len(self.dma_bufs)
        dsems = [es.enter_context(nc.semaphore(f"d{i}")) for i in range(n_dma_sems)]
        esems = {e: es.enter_context(nc.semaphore(f"e_{e}")) for e in self.ENGS}
        for e in self.ENGS:
            for op in self.ops[e]:
                for p in op.deps:
                    if not p.is_dma:
                        p.sig = True
        for e in self.ENGS:
            c = 0
            for op in self.ops[e]:
                if op.sig:
                    c += 1
                    op.sval = c
        block = es.enter_context(nc.Block())

        def run(ename, eng):
            waited = {}
            for op in self.ops[ename]:
                need = {}
                for p in op.deps:
                    if p.is_dma:
                        key = ("d", p.dsem.sem)
                        sem = dsems[p.dsem.sem]
                        val = p.dval
                    else:
                        key = ("e", p.eng)
                        sem = esems[p.eng]
                        val = p.sval
                    if waited.get(key, 0) >= val:
                        continue
                    if key not in need or need[key][1] < val:
                        need[key] = (sem, val)
                for key, (sem, val) in need.items():
                    eng.wait_ge(sem, val)
                    waited[key] = val
                if self.trace is not None:
                    self.trace.append((ename, [(k, v[1]) for k, v in need.items()], op.sval if op.sig else None,
                                       (op.dsem.sem, op.dval) if op.is_dma else None, op.tag))
                if op.fn is None:
                    continue
                ins = op.fn(eng)
                if op.is_dma:
                    ins.then_inc(dsems[op.dsem.sem], 16)
                elif op.sig:
                    ins.then_inc(esems[ename], 1)

        @block.tensor
        def _(e):
            run("pe", e)

        @block.scalar
        def _(e):
            run("act", e)

        @block.vector
        def _(e):
            run("dve", e)

        @block.gpsimd
        def _(e):
            run("pool", e)

        @block.sync
        def _(e):
            run("sp", e)


def _rope_tables(pos, dim, rows):
    half = dim // 2
    inv = (10000.0 ** (-np.arange(half, dtype=np.float32) / half)).astype(np.float32)
    ang = pos.astype(np.float32)[None, :] * inv[:, None]
    cos = np.cos(ang).astype(np.float32)
    sin = np.sin(ang).astype(np.float32)
    c = np.concatenate([cos, cos], 0)
    s = np.concatenate([-sin, sin], 0)
    reps = rows // dim
    return np.tile(c, (reps, 1)).copy(), np.tile(s, (reps, 1)).copy()


def _swap_cols(w, dim):
    n = w.shape[-1]
    idx = np.arange(n).reshape(n // dim, dim)
    idx = np.concatenate([idx[:, dim // 2:], idx[:, :dim // 2]], 1).reshape(-1)
    return w[..., idx]


def _ret_gamma():
    return (1.0 - np.exp2(-5.0 - np.arange(4, dtype=np.float64)))


class Cfg:
    def __init__(self, seq=8192, depth=2, past=1024, with_sample=True):
        self.seq = seq
        self.depth = depth
        self.past = past
        self.nblk = seq // 512
        self.with_sample = with_sample
        self.stage = 99
        self.dbg = 0


def build_program(cfg):
    SEQ, DEPTH, NBLK = cfg.seq, cfg.depth, cfg.nblk
    nc = bass.Bass("TRN2", target_bir_lowering=False)
    P = Prog()
    P.trace = [] if getattr(cfg, 'trace', False) else None
    cfg.P = P
    es = contextlib.ExitStack()

    def din(name, shape, dt=F32):
        return nc.dram_tensor(name, list(shape), dt, kind="ExternalInput").ap()

    def dout(name, shape, dt=F32):
        return nc.dram_tensor(name, list(shape), dt, kind="ExternalOutput").ap()

    def dscr(name, shape, dt):
        return nc.dram_tensor(name, list(shape), dt).ap()

    def sb(name, shape, dt):
        return es.enter_context(nc.sbuf_tensor(name, list(shape), dt))

    def ps(name, shape, dt=F32):
        return es.enter_context(nc.psum_tensor(name, list(shape), dt))

    x_p = din("x_p", [SEQ, D])
    W1 = din("W1", [DEPTH, D, NCT * 128])
    WT = din("WT", [DEPTH, D, 1024])
    WUQ = din("WUQ", [DEPTH, 256, 1024])
    WUKV = din("WUKV", [DEPTH, 128, 1024])
    WOUT = din("WOUT", [DEPTH, D, D])
    GN = din("GN", [DEPTH, 128, D])
    GC = din("GC", [DEPTH, 128, 16])
    BT = din("BT", [DEPTH, 128, 5, 4, 128])
    cosA = din("cosA", [128, SEQ + 512])
    sinA = din("sinA", [128, SEQ + 512])
    cosB = din("cosB", [64, SEQ + 512])
    sinB = din("sinB", [64, SEQ + 512])
    c_idb = din("c_idb", [128, 128], BF16)
    c_idf = din("c_idf", [128, 128])
    c_bd64 = din("c_bd64", [128, 128], BF16)
    c_bd32 = din("c_bd32", [128, 128], BF16)
    c_ones = din("c_ones", [128, 128], BF16)
    c_perm64 = din("c_perm64", [128, 128], BF16)
    c_sel = din("c_sel", [128, 2, 128])
    c_mmask = din("c_mmask", [128, 4, 512], BF16)
    c_dec = din("c_dec", [128, 4, 4, 128], BF16)
    c_dq = din("c_dq", [128, 2, 512])
    c_dv = din("c_dv", [128, 4, 4])
    c_onesp = din("c_onesp", [128, 512])
    c_bmk = din("c_bmk", [128, 5, 128])

    x_s = din("x_s", [512, D])
    st_in = din("st_in", [DEPTH, 4, 4, 64, 64])
    cckvT = din("cckvT", [DEPTH, 4, 128, 1024])
    ckrT = din("ckrT", [DEPTH, 4, 32, 1024])
    cbkT = din("cbkT", [DEPTH, 4, 256, 512])
    cbv = din("cbv", [DEPTH, 4, 512, 512])
    BTS = din("BTS", [DEPTH, 128, 3, 4, 128])
    c_bmkS = din("c_bmkS", [128, 3, 128])
    c_dqS = din("c_dqS", [128, 2, 512])
    c_dvS = din("c_dvS", [128, 4, 4])
    c_decS = din("c_decS", [128, 4, 128], BF16)
    c_smask = din("c_smask", [128, 128], BF16)
    y_s = dout("y_s", [128, D])
    s_state = dout("s_state", [DEPTH, 4, 4, 64, 64])
    s_ckv = dout("s_ckv", [DEPTH, 128, 128])
    s_kr = dout("s_kr", [DEPTH, 128, 32])
    s_bk = dout("s_bk", [DEPTH, 128, 256])
    s_bv = dout("s_bv", [DEPTH, 128, 256])
    kTs_d = [dscr(f"kTs_{l}", [8, 96, 512], BF16) for l in range(DEPTH)]
    vs_d = [dscr(f"vs_{l}", [512, 512], BF16) for l in range(DEPTH)]
    kTc_d = [[[dscr(f"kTc_{l}_{e}_{hf}", [8, 96, 512], BF16) for hf in range(2)] for e in range(4)] for l in range(DEPTH)]
    vc_d = [[[dscr(f"vc_{l}_{e}_{hf}", [512, 512], BF16) for hf in range(2)] for e in range(4)] for l in range(DEPTH)]
    ys_d = dscr("ys_d", [128, D], F32)
    B_kTs = [Buf(f"kTs{l}") for l in range(DEPTH)]
    B_vs = [Buf(f"vs{l}") for l in range(DEPTH)]
    B_kTc = [[[Buf(f"kTc{l}{e}{hf}") for hf in range(2)] for e in range(4)] for l in range(DEPTH)]
    B_vc = [[[Buf(f"vc{l}{e}{hf}") for hf in range(2)] for e in range(4)] for l in range(DEPTH)]
    B_ys = Buf("ys")
    y_p = dout("y_p", [SEQ, D])
    o_state = dout("o_state", [DEPTH, 4, 64, 64])
    o_ckv = dout("o_ckv", [DEPTH, SEQ, 128])
    o_kr = dout("o_kr", [DEPTH, SEQ, 32])
    o_bk = dout("o_bk", [DEPTH, 512, 256])
    o_bv = dout("o_bv", [DEPTH, 512, 256])

    kT_d = [[dscr(f"kT_{l}_{j}", [8, 96, 512], BF16) for j in range(NBLK)] for l in range(DEPTH)]
    v_d = [[dscr(f"v_{l}_{j}", [512, 512], BF16) for j in range(NBLK)] for l in range(DEPTH)]
    y1_d = [dscr(f"y1_{j}", [512, D], F32) for j in range(NBLK)]
    CH = (0, 1, 'T', 2, 3)
    wbf_d = {(l, c, hf): dscr(f"wbf_{l}_{c}_{hf}", [128, 8, 512], BF16) for l in range(DEPTH) for c in CH for hf in range(2)}
    B_wbf = {k: Buf(f"wbf{k}") for k in wbf_d}

    B_kT = [[Buf(f"kT{l}{j}") for j in range(NBLK)] for l in range(DEPTH)]
    B_v = [[Buf(f"v{l}{j}") for j in range(NBLK)] for l in range(DEPTH)]
    B_y1 = [Buf(f"y1{j}") for j in range(NBLK)]

    def T(name, shape, dt):
        t = sb(name, shape, dt)
        return t, Buf(name)

    w1h = [T("w1a", [128, 8, 512], BF16), T("w1b", [128, 8, 512], BF16)]
    wuq, b_wuq = T("wuq", [128, 2, 1024], BF16)
    wukv, b_wukv = T("wukv", [128, 1024], BF16)
    stg, b_stg = T("stg", [128, 1024], F32)
    gn, b_gn = T("gn", [128, D], F32)
    gc, b_gc = T("gc", [128, 16], F32)
    idb, b_idb = T("idb", [128, 128], BF16)
    idf, b_idf = T("idf", [128, 128], F32)
    bd64, b_bd64 = T("bd64", [128, 128], BF16)
    bd32, b_bd32 = T("bd32", [128, 128], BF16)
    ones, b_ones = T("ones", [128, 128], BF16)
    perm64, b_perm64 = T("perm64", [128, 128], BF16)
    sel, b_sel = T("sel", [128, 2, 128], F32)
    mmask, b_mmask = T("mmask", [128, 4, 512], BF16)
    dec, b_dec = T("dec", [128, 4, 4, 128], BF16)
    dq, b_dq = T("dq", [128, 2, 512], F32)
    dvt, b_dvt = T("dvt", [128, 4, 4], F32)
    onesp, b_onesp = T("onesp", [128, 512], F32)
    bbias_b, b_bbias_b = T("bbias_b", [128, 5, 4, 128], BF16)
    bmk, b_bmk = T("bmk", [128, 5, 128], F32)
    rbs, b_rbs = T("rbs", [128, 512], F32)

    decS, b_decS = T("decS", [128, 4, 128], BF16)
    smask, b_smask = T("smask", [128, 128], BF16)
    dvS, b_dvS = T("dvS", [128, 4, 4], F32)
    vdecS, b_vdecS = T("vdecS", [128, 4, 512], BF16)
    stS, b_stS = T("stS", [64, 16, 64], F32)
    stpadS, b_stpadS = T("stpadS", [128, 4, 2, 128], BF16)
    bbiasS_b, b_bbiasS_b = T("bbiasS_b", [128, 3, 4, 128], BF16)
    bmkS, b_bmkS = T("bmkS", [128, 3, 128], F32)
    epsb, b_epsb = T("epsb", [128, 1], F32)
    dmy, b_dmy = T("dmy", [128, 1], F32)
    xres, b_xres = T("xres", [128, 4, D], F32)
    b_xr = [Buf(f"xres{t_}") for t_ in range(4)]
    hb, b_hb = T("hb", [128, D], BF16)
    junk, b_junk = T("junk", [128, D], BF16)
    ss1, b_ss1 = T("ss1", [128, 8], F32)
    b_ss1t = [Buf(f"ss1_{t_}") for t_ in range(4)]
    hT, b_hT = T("hT", [128, 8, 512], BF16)
    tcos, b_tcos = T("tcos", [128, 512], F32)
    tsin, b_tsin = T("tsin", [128, 512], F32)
    tcosB, b_tcosB = T("tcosB", [64, 512], F32)
    tsinB, b_tsinB = T("tsinB", [64, 512], F32)
    tA, b_tA = T("tA", [128, 512], F32)
    tB, b_tB = T("tB", [128, 512], F32)
    sq, b_sq = T("sq", [128, 512], BF16)
    sq2, b_sq2 = T("sq2", [128, 512], BF16)
    rstd, b_rstd = T("rstd", [128, 512], F32)
    qTr, b_qTr = T("qTr", [128, 2, 512], BF16)
    qTd, b_qTd = T("qTd", [128, 2, 512], BF16)
    kTr, b_kTr = T("kTr", [128, 2, 512], BF16)
    k_tm, b_k_tm = T("k_tm", [128, 4, 256], BF16)
    vpad, b_vpad = T("vpad", [128, 4, 4, 128], BF16)
    vdec, b_vdec = T("vdec", [128, 4, 512], BF16)
    gates, b_gates = T("gates", [128, 8, 512], BF16)
    cqT, b_cqT = T("cqT", [128, 2, 512], BF16)
    ckvf, b_ckvf = T("ckvf", [128, 512], F32)
    ckvb, b_ckvb = T("ckvb", [128, 512], BF16)
    krf, b_krf = T("krf", [32, 512], F32)
    qTm, b_qTm = T("qTm", [96, 8, 512], BF16)
    kTm, b_kTm = T("kTm", [96, 8, 512], BF16)
    vtm, b_vtm = T("vtm", [128, 4, 512], BF16)
    qTb, b_qTb = T("qTb", [128, 2, 512], BF16)
    qTbz, b_qTbz = T("qTbz", [128, 4, 512], BF16)
    kTb = [T(f"kTb{i}", [128, 2, 512], BF16) for i in range(2)]
    kbf, b_kbf = T("kbf", [128, 2, 512], F32)
    rinv, b_rinv = kbf, b_kbf
    vbf, b_vbf = T("vbf", [128, 4, 256], F32)
    vba = [T(f"vba{i}", [128, 4, 4, 128], BF16) for i in range(2)]
    ktb = [T(f"ktb{i}", [96, 512], BF16) for i in range(2)]
    vtb = [[T(f"vtb{p}{i}", [128, 4, 128], BF16) for i in range(2)] for p in range(2)]
    pT = [T(f"pT{i}", [128, 512], BF16) for i in range(2)]
    PT_EXTRA = True
    tmpm, b_tmpm = T("tmpm", [128, 512], F32)
    mixT, b_mixT = hT, b_hT
    innT, b_innT = T("innT", [128, 512], BF16)
    state, b_state = T("state", [64, 4, 64], F32)
    otr, b_otr = T("otr", [128, 512], F32)

    pz = [(ps(f"pz{i}", [128, 512]), Buf(f"pz{i}")) for i in range(2)]
    ptr, b_ptr = ps("ptr", [128, 8, 128], BF16), Buf("ptr")
    pS = [(ps(f"pS{i}", [128, 512]), Buf(f"pS{i}")) for i in range(2)]
    pO = [(ps(f"pO{i}", [128, 512]), Buf(f"pO{i}")) for i in range(2)]
    pn, b_pn = ps("pn", [128, 512]), Buf("pn")
    ring4 = [pz[0], pz[1], pS[0], pS[1]]
    pS = ring4

    pT = pT + [(innT, b_innT), (sq2, b_sq2)]
    dq_rr = [0]

    def dma(out, in_, reads, writes, sbuf_buf, is_out=False, eng=None):
        if eng is None:
            eng = ("sp", "pool")[dq_rr[0] % 2] if False else "sp"
        return P.emit(eng, lambda e: e.dma_start(out=out, in_=in_), reads=reads, writes=writes,
                      dma_buf=sbuf_buf, is_out=is_out)

    def mm(out, lhsT, rhs, start, stop, reads, writes):
        return P.emit("pe", lambda e: e.matmul(out, lhsT, rhs, start=start, stop=stop),
                      reads=reads, writes=writes)

    def tr(out, in_, ident, reads, writes):
        return P.emit("pe", lambda e: e.transpose(out, in_, ident), reads=reads, writes=writes)

    def act(out, in_, func, reads, writes, scale=1.0, bias=0.0, accum=None):
        def fn(e):
            kw = {}
            if accum is not None:
                kw["accum_out"] = accum
            return e.activation(out, in_, func, bias=bias, scale=scale, **kw)
        return P.emit("act", fn, reads=reads, writes=writes)

    def tt(eng, out, in0, in1, op, reads, writes):
        return P.emit(eng, lambda e: e.tensor_tensor(out=out, in0=in0, in1=in1, op=op),
                      reads=reads, writes=writes)

    def tsc(eng, out, in0, s1, s2, op0, op1, reads, writes):
        if op1 == ALU.pow:
            P.emit("act", lambda e: e.activation(out, in0, AF.Ln, bias=epsb[0:out.shape[0], 0:1], scale=1.0),
                   reads=list(reads) + [b_epsb], writes=writes)
            return P.emit("act", lambda e: e.activation(out, out, AF.Exp, scale=-0.5), reads=writes, writes=writes)

        def fn(e):
            if op1 is None:
                return e.tensor_scalar(out=out, in0=in0, scalar1=s1, scalar2=None, op0=op0)
            return e.tensor_scalar(out=out, in0=in0, scalar1=s1, scalar2=s2, op0=op0, op1=op1)
        return P.emit(eng, fn, reads=reads, writes=writes)

    def stt(eng, out, in0, scalar, in1, op0, op1, reads, writes):
        return P.emit(eng, lambda e: e.scalar_tensor_tensor(out=out, in0=in0, scalar=scalar, in1=in1,
                                                            op0=op0, op1=op1),
                      reads=reads, writes=writes)

    def cp(eng, out, in_, reads, writes):
        if eng == "act":
            return P.emit(eng, lambda e: e.copy(out, in_), reads=reads, writes=writes)
        return P.emit(eng, lambda e: e.tensor_copy(out=out, in_=in_), reads=reads, writes=writes)

    def memset(eng, ap, val, writes):
        return P.emit(eng, lambda e: e.memset(ap, val), reads=(), writes=writes)

    pz_rr = [0]

    def next_pz():
        i = pz_rr[0] % 4
        pz_rr[0] += 1
        return ring4[i]

    Bc = Buf("consts_dram")
    for t_, b_, src in ((idb, b_idb, c_idb), (idf, b_idf, c_idf), (bd64, b_bd64, c_bd64), (bd32, b_bd32, c_bd32),
                        (ones, b_ones, c_ones), (perm64, b_perm64, c_perm64), (sel, b_sel, c_sel), (mmask, b_mmask, c_mmask),
                        (dec, b_dec, c_dec), (dvt, b_dvt, c_dv), (bmk, b_bmk, c_bmk), (onesp, b_onesp, c_onesp),
                        (decS, b_decS, c_decS), (smask, b_smask, c_smask), (dvS, b_dvS, c_dvS), (bmkS, b_bmkS, c_bmkS)):
        dma(t_[:], src, [Bc], [b_], b_)
    memset("dve", qTbz[:], 0.0, [b_qTbz])
    memset("dve", epsb[:], EPS, [b_epsb])
    for i in range(2):
        memset("dve", vtb[0][i][0][:, :, 64:128], 1.0, [vtb[0][i][1]])
        memset("dve", vtb[1][i][0][:, :, 0:64], 1.0, [vtb[1][i][1]])

    def load_weights(l):
        def chunked(dst, dst_buf, src2d, rows_tiles, cols):
            for k in range(rows_tiles):
                for c0 in range(0, cols, 1024):
                    cw = min(1024, cols - c0)
                    dma(stg[:, 0:cw], src2d[k * 128:(k + 1) * 128, c0:c0 + cw], [Bc], [b_stg], b_stg)
                    if len(dst.shape) == 3:
                        o = dst[:, k, c0:c0 + cw]
                    else:
                        o = dst[:, c0:c0 + cw]
                    cp("pool", o, stg[:, 0:cw], [b_stg], [dst_buf])
        chunked(wuq, b_wuq, WUQ[l], 2, 1024)
        chunked(wukv, b_wukv, WUKV[l], 1, 1024)
        precast_weights(l)
        dma(gn[:], GN[l], [Bc], [b_gn], b_gn)
        dma(gc[:], GC[l], [Bc], [b_gc], b_gc)
        dma(xres[:, 0:3, :].rearrange("p a b -> p (a b)")[:, 0:2560], BT[l].rearrange("p r h q -> p (r h q)"),
            [Bc], b_xr[0:3], b_xr[0])
        xv = xres[:, 0:3, :].rearrange("p a b -> p (a b)")[:, 0:2560].rearrange("p (r h q) -> p r h q", r=5, h=4)
        for h in range(4):
            stt("dve", bbias_b[:, :, h, :], xv[:, :, h, :], 8.0, bmk[:, :, :], ALU.mult, ALU.add,
                b_xr[0:3] + [b_bmk], [b_bbias_b])

        dma(xres[:, 0:2, :].rearrange("p a b -> p (a b)")[:, 0:1536], BTS[l].rearrange("p r h q -> p (r h q)"),
            [Bc], b_xr[0:2], b_xr[0])
        xv2 = xres[:, 0:2, :].rearrange("p a b -> p (a b)")[:, 0:1536].rearrange("p (r h q) -> p r h q", r=3, h=4)
        for h in range(4):
            stt("dve", bbiasS_b[:, :, h, :], xv2[:, :, h, :], 8.0, bmkS[:, :, :], ALU.mult, ALU.add,
                b_xr[0:2] + [b_bmkS], [b_bbiasS_b])

    cur_half = [None, None]

    def need_half(l, c, hf):
        t_, b_ = w1h[hf]
        if cur_half[hf] != (l, c):
            cur_half[hf] = (l, c)
            dma(t_[:], wbf_d[(l, c, hf)], [B_wbf[(l, c, hf)]], [b_], b_)
        return t_, b_

    b_stgh = [Buf("stgA"), Buf("stgB")]

    def stg_barrier():
        P.emit("dve", lambda e: e.memset(dmy[:], 0.0), reads=(), writes=[b_dmy, b_stg, b_stgh[0], b_stgh[1]])

    def precast_weights(l):
        stg_barrier()
        for c in CH:
            src = WOUT[l] if c == 3 else (WT[l] if c == 'T' else W1[l][:, c * 1024:(c + 1) * 1024])
            for k in range(8):
                for hf in range(2):
                    dma(stg[:, hf * 512:(hf + 1) * 512], src[k * 128:(k + 1) * 128, hf * 512:(hf + 1) * 512],
                        [Bc], [b_stgh[hf]], b_stgh[hf])
                    cp("dve", w1h[hf][0][:, k, :], stg[:, hf * 512:(hf + 1) * 512], [b_stgh[hf]], [w1h[hf][1]])
            for hf in range(2):
                dma(wbf_d[(l, c, hf)], w1h[hf][0][:], [w1h[hf][1]], [B_wbf[(l, c, hf)]], w1h[hf][1])
                cur_half[hf] = (l, c)
        stg_barrier()

    GCOL = dict(qa0=0, qa1=1, qn=2, qr=3, kva=4, kr=5, kn=6, bq=7, bk=8, rg0=9, rg1=10)

    rstd_ring = [(rstd, b_rstd), (stg[:, 0:512], b_stg)]
    sq_ring = [(sq, b_sq), (junk[:, 0:512], b_junk)]
    hn_cnt = [0]

    def headnorm(raw, rows, N, bdmat, b_bd, n, reads_raw):
        nonlocal rstd, b_rstd
        hn_cnt[0] += 1
        rstd, b_rstd = rstd_ring[hn_cnt[0] % 2]
        sq_t, b_sq_t = sq_ring[hn_cnt[0] % 2]
        act(sq_t[0:rows, 0:N], raw, AF.Square, reads_raw, [b_sq_t], scale=float(n) ** -0.5)
        mm(pn[0:rows, 0:N], bdmat[0:rows, 0:rows], sq_t[0:rows, 0:N], True, True, [b_bd, b_sq_t], [b_pn])
        tsc("dve", rstd[0:rows, 0:N], pn[0:rows, 0:N], EPS, -0.5, ALU.add, ALU.pow, [b_pn], [b_rstd])

    def out_T(src_f32, rows, tok0, ntok, dst_rows_ap, reads):
        pzt, b_pzt = next_pz()
        tr(pzt[0:ntok, 0:rows], src_f32, idf[0:rows, 0:rows], reads + [b_idf], [b_pzt])
        cp("dve", otr[0:ntok, 0:rows], pzt[0:ntok, 0:rows], [b_pzt], [b_otr])
        dma(dst_rows_ap, otr[0:ntok, 0:rows], [b_otr], [], b_otr, is_out=True)

    def gen_kv(dstK, BK, dstV, BV):
        N = 512
        def k_issue(pr_):
            pz_, b_pz_ = next_pz()
            mm(pz_[:, 0:N], wukv[:, pr_ * 128:(pr_ + 1) * 128], ckvb[:], True, True, [b_wukv, b_ckvb], [b_pz_])
            return pz_, b_pz_
        k_pend = {0: k_issue(0)}
        for pr in range(4):
            if pr + 1 < 4:
                k_pend[pr + 1] = k_issue(pr + 1)
            pzt, b_pzt = k_pend.pop(pr)
            headnorm(pzt[:, 0:N], 128, N, bd64, b_bd64, 64, [b_pzt])
            for e in range(2):
                stt("dve", kTm[0:64, 2 * pr + e, :], pzt[64 * e:64 * e + 64, 0:N], gc[64 * e:64 * e + 64, 6:7],
                    rstd[64 * e:64 * e + 64, 0:N], ALU.mult, ALU.mult, [b_pzt, b_gc, b_rstd], [b_kTm])
        for t in range(4):
            pzt, b_pzt = next_pz()
            mm(pzt[:, :], ckvb[:, t * 128:(t + 1) * 128], wukv[:, 512:1024], True, True, [b_ckvb, b_wukv], [b_pzt])
            cp("dve", vtm[:, t, :], pzt[:, :], [b_pzt], [b_vtm])
        dma(dstK.rearrange("h r t -> r h t"), kTm[:], [b_kTm], [BK], b_kTm)
        dma(dstV.rearrange("(t p) c -> p t c", p=128), vtm[:], [b_vtm], [BV], b_vtm)


    def prompt_block(l, j, sample=False):
        N = 512
        t0 = SEQ if sample else j * 512
        bi = 1 if sample else j % 2
        dma(dq[:], c_dqS if sample else c_dq, [Bc], [b_dq], b_dq)
        dma(tcos[:], cosA[:, t0:t0 + N], [Bc], [b_tcos], b_tcos)
        dma(tsin[:], sinA[:, t0:t0 + N], [Bc], [b_tsin], b_tsin)
        dma(tcosB[:], cosB[:, t0:t0 + N], [Bc], [b_tcosB], b_tcosB)
        dma(tsinB[:], sinB[:, t0:t0 + N], [Bc], [b_tsinB], b_tsinB)
        for t in range(4):
            if sample and (l == 0 or t > 0):
                dma(xres[:, t, :], x_s[t * 128:(t + 1) * 128, :], [Bc], [b_xr[t]], b_xr[t])
            elif sample:
                dma(xres[:, t, :], ys_d[:, :], [B_ys], [b_xr[t]], b_xr[t])
            elif l == 0:
                dma(xres[:, t, :], x_p[t0 + t * 128:t0 + (t + 1) * 128, :], [Bc], [b_xr[t]], b_xr[t])
            else:
                dma(xres[:, t, :], y1_d[j][t * 128:(t + 1) * 128, :], [B_y1[j]], [b_xr[t]], b_xr[t])
            act(junk[:], xres[:, t, :], AF.Square, [b_xr[t]], [b_junk, b_ss1t[t]], scale=1.0 / 32.0,
                accum=ss1[:, 2 * t:2 * t + 1])
            tsc("dve", ss1[:, 2 * t + 1:2 * t + 2], ss1[:, 2 * t:2 * t + 1], EPS, -0.5, ALU.add, ALU.pow,
                [b_ss1t[t]], [b_ss1t[t]])
            stt("dve", hb[:], xres[:, t, :], ss1[:, 2 * t + 1:2 * t + 2], gn[:], ALU.mult, ALU.mult,
                [b_xr[t], b_ss1t[t], b_gn], [b_hb])
            for g in range(2):
                for kk in range(4):
                    k = g * 4 + kk
                    tr(ptr[:, kk, :], hb[:, k * 128:(k + 1) * 128], idb[:], [b_hb, b_idb], [b_ptr])
                cp("dve", hT[:, g * 4:(g + 1) * 4, t * 128:(t + 1) * 128], ptr[:, 0:4, :],
                   [b_ptr], [b_hT])

        if cfg.stage < 2:
            return

        issued = {}
        runs = [[0, 1, 4, 5], list(range(8, 16)), [16, 17], [20, 22, 21, 23]]
        nxt_of = {}
        for r_ in runs:
            for a_, b__ in zip(r_[:-1], r_[1:]):
                nxt_of[a_] = b__

        def issue(ct):
            cc = ct % 8
            wq, b_wq = need_half(l, ct // 8, cc // 4)
            c4 = cc % 4
            pzt, b_pzt = next_pz()
            for k in range(8):
                mm(pzt[:, 0:N], wq[:, k, c4 * 128:(c4 + 1) * 128], hT[:, k, 0:N], k == 0, k == 7,
                   [b_wq, b_hT], [b_pzt])
            issued[ct] = (pzt, b_pzt)

        def inproj(ct):
            if ct not in issued:
                issue(ct)
            if ct in nxt_of and nxt_of[ct] not in issued:
                issue(nxt_of[ct])
            return issued.pop(ct)

        for which, (dst, b_dst) in enumerate(((qTr, b_qTr), (kTr, b_kTr))):
            for pr in range(2):
                praw, b_praw = inproj(which * 4 + pr)
                cp("dve", ckvb[:], praw[:, 0:N], [b_praw], [b_ckvb])
                tt("dve", tA[:], praw[:, 0:N], tcos[:], ALU.mult, [b_praw, b_tcos], [b_tA])
                pswp, b_pswp = next_pz()
                mm(pswp[:, 0:N], perm64[:], ckvb[:], True, True, [b_perm64, b_ckvb], [b_pswp])
                tt("dve", tB[:], pswp[:, 0:N], tsin[:], ALU.mult, [b_pswp, b_tsin], [b_tB])
                tt("dve", dst[:, pr, :], tA[:], tB[:], ALU.add, [b_tA, b_tB], [b_dst])
        for pr in range(2):
            tt("dve", qTd[:, pr, :], qTr[:, pr, :], dq[:, pr, :], ALU.mult, [b_qTr, b_dq], [b_qTd])
        for t in range(4):
            for pr in range(2):
                tr(ptr[:, pr, :], kTr[:, pr, t * 128:(t + 1) * 128], idb[:], [b_kTr, b_idb], [b_ptr])
            cp("dve", k_tm[:, t, :], ptr[:, 0:2, :], [b_ptr], [b_k_tm])
        if cfg.stage < 2.1:
            return
        for ft, ct in enumerate(range(8, 16)):
            pg, b_pg = inproj(ct)
            act(gates[:, ft, :], pg[:, 0:N], AF.Silu, [b_pg], [b_gates])
        if cfg.stage < 2.3:
            return
        vb_t, b_vb = vba[bi]
        def issue_T(hf_, t_):
            wq_, b_wq_ = need_half(l, 'T', hf_)
            pz_, b_pz_ = next_pz()
            for k in range(8):
                mm(pz_[:, :], hT[:, k, t_ * 128:(t_ + 1) * 128], wq_[:, k, :], k == 0, k == 7,
                   [b_hT, b_wq_], [b_pz_])
            return pz_, b_pz_
        seqT = [(hf_, t_) for hf_ in range(2) for t_ in range(4)]
        pendT = {seqT[0]: issue_T(*seqT[0])}

        def get_T(hf_, t_):
            i_ = seqT.index((hf_, t_))
            if i_ + 1 < len(seqT):
                pendT[seqT[i_ + 1]] = issue_T(*seqT[i_ + 1])
            return pendT.pop((hf_, t_))
        for t in range(4):
            pzA, b_pzA = get_T(0, t)
            tsc("dve", vpad[:, t, :, :].rearrange("p h c -> p (h c)"), pzA[:, :], 0.125, None, ALU.mult, None,
                [b_pzA], [b_vpad])
            for h in range(4):
                tsc("dve", vdec[:, t, h * 128:(h + 1) * 128], pzA[:, h * 128:(h + 1) * 128], dvt[:, t, h:h + 1],
                    None, ALU.mult, None, [b_pzA, b_dvt], [b_vdec])
            if sample and t == 0:
                for e_ in range(4):
                    for h in range(4):
                        tsc("dve", vdecS[:, e_, h * 128:(h + 1) * 128], pzA[:, h * 128:(h + 1) * 128],
                            dvS[:, e_, h:h + 1], None, ALU.mult, None, [b_pzA, b_dvS], [b_vdecS])
        for t in range(4):
            pzB, b_pzB = get_T(1, t)
            tt("dve", vb_t[:, t, :, :].rearrange("p h c -> p (h c)"), pzB[:, :], onesp[:], ALU.add,
               [b_pzB, b_onesp], [b_vb])
            if (j == NBLK - 1 and not sample) or (sample and t == 0):
                for h in range(4):
                    lo = 0 if h % 2 == 0 else 64
                    cp("dve", vbf[:, t, h * 64:(h + 1) * 64], pzB[:, h * 128 + lo:h * 128 + lo + 64],
                       [b_pzB], [b_vbf])
        if cfg.stage < 2.6:
            return
        pc0, b_pc0 = inproj(16)
        act(sq[:, 0:N], pc0[:, 0:N], AF.Square, [b_pc0], [b_sq], scale=1.0 / 16.0)
        mm(pn[:, 0:N], ones[:], sq[:, 0:N], True, False, [b_ones, b_sq], [b_pn])
        pc1, b_pc1 = inproj(17)
        act(sq2[:, 0:N], pc1[:, 0:N], AF.Square, [b_pc1], [b_sq2], scale=1.0 / 16.0)
        mm(pn[:, 0:N], ones[:], sq2[:, 0:N], False, True, [b_ones, b_sq2], [b_pn])
        tsc("dve", rstd[:, 0:N], pn[:, 0:N], EPS, -0.5, ALU.add, ALU.pow, [b_pn], [b_rstd])
        stt("dve", cqT[:, 0, :], pc0[:, 0:N], gc[:, 0:1], rstd[:, 0:N], ALU.mult, ALU.mult,
            [b_pc0, b_gc, b_rstd], [b_cqT])
        stt("dve", cqT[:, 1, :], pc1[:, 0:N], gc[:, 1:2], rstd[:, 0:N], ALU.mult, ALU.mult,
            [b_pc1, b_gc, b_rstd], [b_cqT])

        uq_seq = [(pr_ * 128, 128) for pr_ in range(4)]
        for rp_ in range(4):
            uq_seq += [(512 + rp_ * 64, 64), (768 + rp_ * 64, 64)]
        uq_pend = {}

        def uq_issue(c0, M):
            pzt, b_pzt = next_pz()
            for k in range(2):
                mm(pzt[0:M, 0:N], wuq[:, k, c0:c0 + M], cqT[:, k, :], k == 0, k == 1, [b_wuq, b_cqT], [b_pzt])
            uq_pend[(c0, M)] = (pzt, b_pzt)

        def uq(c0, M):
            if (c0, M) not in uq_pend:
                uq_issue(c0, M)
            i_ = uq_seq.index((c0, M))
            if i_ + 1 < len(uq_seq) and uq_seq[i_ + 1] not in uq_pend:
                uq_issue(*uq_seq[i_ + 1])
            return uq_pend.pop((c0, M))
        for pr in range(4):
            pq, b_pq = uq(pr * 128, 128)
            headnorm(pq[:, 0:N], 128, N, bd64, b_bd64, 64, [b_pq])
            for e in range(2):
                stt("dve", qTm[0:64, 2 * pr + e, :], pq[64 * e:64 * e + 64, 0:N], gc[64 * e:64 * e + 64, 2:3],
                    rstd[64 * e:64 * e + 64, 0:N], ALU.mult, ALU.mult, [b_pq, b_gc, b_rstd], [b_qTm])
        for rp in range(4):
            pq, b_pq = uq(512 + rp * 64, 64)
            headnorm(pq[0:64, 0:N], 64, N, bd32, b_bd32, 32, [b_pq])
            stt("dve", tA[0:64, :], pq[0:64, 0:N], gc[0:64, 3:4], rstd[0:64, 0:N], ALU.mult, ALU.mult,
                [b_pq, b_gc, b_rstd], [b_tA])
            tt("dve", tA[0:64, :], tA[0:64, :], tcosB[:], ALU.mult, [b_tA, b_tcosB], [b_tA])
            pq2, b_pq2 = uq(768 + rp * 64, 64)
            stt("dve", tB[0:64, :], pq2[0:64, 0:N], gc[0:64, 11:12], rstd[0:64, 0:N], ALU.mult, ALU.mult,
                [b_pq2, b_gc, b_rstd], [b_tB])
            tt("dve", tB[0:64, :], tB[0:64, :], tsinB[:], ALU.mult, [b_tB, b_tsinB], [b_tB])
            for e in range(2):
                tt("dve", qTm[64:96, 2 * rp + e, :], tA[32 * e:32 * e + 32, :], tB[32 * e:32 * e + 32, :], ALU.add,
                   [b_tA, b_tB], [b_qTm])
        if cfg.stage < 4:
            return
        pk, b_pk = inproj(18)
        headnorm(pk[:, 0:N], 128, N, ones, b_ones, 128, [b_pk])
        stt("dve", ckvf[:], pk[:, 0:N], gc[:, 4:5], rstd[:, 0:N], ALU.mult, ALU.mult, [b_pk, b_gc, b_rstd], [b_ckvf])
        cp("dve", ckvb[:], ckvf[:], [b_ckvf], [b_ckvb])
        for t in range(1 if sample else 4):
            out_T(ckvf[:, t * 128:(t + 1) * 128], 128, t0 + t * 128, 128,
                  s_ckv[l] if sample else o_ckv[l, t0 + t * 128:t0 + (t + 1) * 128, :], [b_ckvf])
        pr_, b_pr = inproj(19)
        headnorm(pr_[0:64, 0:N], 64, N, bd32, b_bd32, 32, [b_pr])
        stt("dve", tA[0:64, :], pr_[0:64, 0:N], gc[0:64, 5:6], rstd[0:64, 0:N], ALU.mult, ALU.mult,
            [b_pr, b_gc, b_rstd], [b_tA])
        tt("dve", tB[0:32, :], tA[0:32, :], tcosB[0:32, :], ALU.mult, [b_tA, b_tcosB], [b_tB])
        tt("dve", tB[32:64, :], tA[32:64, :], tsinB[32:64, :], ALU.mult, [b_tA, b_tsinB], [b_tB])
        cp("dve", tA[0:32, :], tB[32:64, :], [b_tB], [b_tA])
        tt("dve", krf[:], tB[0:32, :], tA[0:32, :], ALU.add, [b_tA, b_tB], [b_krf])
        for h in range(8):
            cp("pool", kTm[64:96, h, :], krf[:], [b_krf], [b_kTm])
        for t in range(1 if sample else 4):
            out_T(krf[:, t * 128:(t + 1) * 128], 32, t0 + t * 128, 128,
                  s_kr[l] if sample else o_kr[l, t0 + t * 128:t0 + (t + 1) * 128, :], [b_krf])
        if sample:
            gen_kv(kTs_d[l], B_kTs[l], vs_d[l], B_vs[l])
        else:
            gen_kv(kT_d[l][j], B_kT[l][j], v_d[l][j], B_v[l][j])
        if cfg.stage < 5:
            return
        kTb_t, b_kTb = kTb[bi]
        for pr in range(2):
            pq, b_pq = inproj(20 + pr)
            headnorm(pq[:, 0:N], 128, N, bd64, b_bd64, 64, [b_pq])
            for e in range(2):
                stt("dve", qTbz[64 * e:64 * e + 64, 2 * pr + e, :], pq[64 * e:64 * e + 64, 0:N],
                    gc[64 * e:64 * e + 64, 7:8], rstd[64 * e:64 * e + 64, 0:N], ALU.mult, ALU.mult,
                    [b_pq, b_gc, b_rstd], [b_qTbz])
            pk2, b_pk2 = inproj(22 + pr)
            headnorm(pk2[:, 0:N], 128, N, bd64, b_bd64, 64, [b_pk2])
            stt("dve", kbf[:, pr, :], pk2[:, 0:N], gc[:, 8:9], rstd[:, 0:N], ALU.mult, ALU.mult,
                [b_pk2, b_gc, b_rstd], [b_kbf])
            cp("pool", kTb_t[:, pr, :], kbf[:, pr, :], [b_kbf], [b_kTb])
        if sample:
            for pr in range(2):
                out_T(kbf[:, pr, 0:128], 128, 0, 128, s_bk[l, :, pr * 128:(pr + 1) * 128], [b_kbf])
            dma(s_bv[l, :, :], vbf[:, 0, :], [b_vbf], [], b_vbf, is_out=True)
            sample_mixers(l)
            return
        if j == NBLK - 1:
            for t in range(4):
                for pr in range(2):
                    out_T(kbf[:, pr, t * 128:(t + 1) * 128], 128, 0, 128,
                          o_bk[l, t * 128:(t + 1) * 128, pr * 128:(pr + 1) * 128], [b_kbf])
                dma(o_bv[l, t * 128:(t + 1) * 128, :], vbf[:, t, :], [b_vbf], [], b_vbf, is_out=True)

        if cfg.stage < 5.1:
            return
        if j == 0:
            memset("dve", state[:], 0.0, [b_state])
            memset("dve", stpad_t[:], 0.0, [b_stpad])
        for pr in range(2):
            po_t, b_po = pO[pr % 2]
            first = True
            for e in range(2):
                h = 2 * pr + e
                r0 = 64 * e
                for kt in range(4):
                    psT, b_psT = pS[(h * 4 + kt) % 2]
                    l0 = kt * 128
                    mm(psT[:, l0:N], kTr[r0:r0 + 64, pr, l0:l0 + 128], qTr[r0:r0 + 64, pr, l0:N], True, True,
                       [b_kTr, b_qTr], [b_psT])
                    for lt in range(kt, 4):
                        tt("dve", innT[:, lt * 128:(lt + 1) * 128], psT[:, lt * 128:(lt + 1) * 128],
                           dec[:, h, lt - kt, :], ALU.mult, [b_psT, b_dec], [b_innT])
                    mm(po_t[:, l0:N], vpad[:, kt, h, :], innT[:, l0:N], first and kt == 0, False,
                       [b_vpad, b_innT], [b_po])
                    first = False
            if cfg.stage < 5.4:
                continue
            P_state_term(pr, po_t, b_po)
            if cfg.stage < 5.6:
                continue
            headnorm(po_t[:, 0:N], 128, N, bd64, b_bd64, 64, [b_po])
            stt("dve", tmpm[:], po_t[:, 0:N], gc[:, 9 + pr:10 + pr], rstd[:, 0:N], ALU.mult, ALU.mult,
                [b_po, b_gc, b_rstd], [b_tmpm])
            tt("dve", mixT[:, pr, :], tmpm[:], gates[:, pr, :], ALU.mult, [b_tmpm, b_gates], [b_mixT])
        if cfg.stage < 5.8:
            return
        gam = _ret_gamma()
        for h in range(4):
            pzt, b_pzt = next_pz()
            for t in range(4):
                mm(pzt[0:64, 0:64], k_tm[:, t, h * 64:(h + 1) * 64],
                   vdec[:, t, h * 128 + 64 * (h % 2):h * 128 + 64 * (h % 2) + 64],
                   t == 0, t == 3, [b_k_tm, b_vdec], [b_pzt])
            stt("dve", state[:, h, :], state[:, h, :], float(gam[h] ** 512), pzt[0:64, 0:64], ALU.mult, ALU.add,
                [b_state, b_pzt], [b_state])
            cp("dve", stpad_t[64 * (h % 2):64 * (h % 2) + 64, h // 2, 64 * (h % 2):64 * (h % 2) + 64],
               state[:, h, :], [b_state], [b_stpad])
        if j == NBLK - 1 and cfg.stage >= 5.9:
            dma(o_state[l].rearrange("h k v -> k h v"), state[:], [b_state], [], b_state, is_out=True)
        if cfg.stage < 7:
            return
        qscale = 96.0 ** -0.5
        cnt = [0]
        SKEW = 3
        items = [(h, kb, kt) for h in range(8) for kb in range(j + 1) for kt in range(4)]
        loaded = {}
        sbuf_of = {}

        def emit_S(idx):
            h, kb, kt = items[idx]
            if kt == 0:
                kt_t, b_kt = ktb[cnt[0] % 2]
                vt_t, b_vt = vtb[h % 2][(cnt[0] // 1) % 2]
                cnt[0] += 1
                dma(kt_t[:, :], kT_d[l][kb][h], [B_kT[l][kb]], [b_kt], b_kt)
                lo = 0 if h % 2 == 0 else 64
                dma(vt_t[:, :, lo:lo + 64],
                    v_d[l][kb].rearrange("(t p) c -> p t c", p=128)[:, :, h * 64:(h + 1) * 64],
                    [B_v[l][kb]], [b_vt], b_vt)
                loaded[(h, kb)] = (kt_t, b_kt, vt_t, b_vt)
            kt_t, b_kt, vt_t, b_vt = loaded[(h, kb)]
            psT, b_psT = pS[idx % 4]
            pT_t, b_pT = pT[idx % 4]
            mm(psT[:, 0:N], kt_t[:, kt * 128:(kt + 1) * 128], qTm[:, h, :], True, True, [b_kt, b_qTm], [b_psT])
            act(pT_t[:, 0:N], psT[:, 0:N], AF.Exp, [b_psT], [b_pT], scale=qscale)
            if kb == j:
                tt("dve", pT_t[:, 0:N], pT_t[:, 0:N], mmask[:, kt, :], ALU.mult, [b_pT, b_mmask], [b_pT])

        def emit_PV(idx):
            h, kb, kt = items[idx]
            kt_t, b_kt, vt_t, b_vt = loaded[(h, kb)]
            po_t, b_po = pO[h % 2]
            pT_t, b_pT = pT[idx % 4]
            first = (kb == 0 and kt == 0)
            last = (kb == j and kt == 3)
            mm(po_t[:, 0:N], vt_t[:, kt, :], pT_t[:, 0:N], first, last, [b_vt, b_pT], [b_po])
            if last and h % 2 == 1:
                finalize_pair(2 + h // 2)
        for i in range(len(items) + SKEW):
            if i < len(items):
                emit_S(i)
            if i >= SKEW:
                emit_PV(i - SKEW)
        bscale = 0.125
        kcur, b_kcur = kTb[j % 2]
        kprev, b_kprev = kTb[(j - 1) % 2]
        vcur, b_vcur = vba[j % 2]
        vprev, b_vprev = vba[(j - 1) % 2]
        groups = []
        for h in range(4):
            for qt in range(4):
                units = []
                if j > 0:
                    for i in range(qt, 4):
                        units.append((kprev, b_kprev, vprev, b_vprev, i, i - 4 - qt))
                for i in range(0, qt + 1):
                    units.append((kcur, b_kcur, vcur, b_vcur, i, i - qt))
                for g0 in range(0, len(units), 4):
                    groups.append((h, qt, g0, units[g0:g0 + 4], len(units)))

        def band_S(gi):
            h, qt, g0, grp, nun = groups[gi]
            pr = h // 2
            psT, b_psT = pS[gi % 4]
            pT_t, b_pT = pT[gi % 4]
            for ui, (kt_, b_k_, v_, b_v_, i, r) in enumerate(grp):
                mm(psT[:, ui * 128:(ui + 1) * 128], kt_[:, pr, i * 128:(i + 1) * 128],
                   qTbz[:, h, qt * 128:(qt + 1) * 128], True, False, [b_k_, b_qTbz], [b_psT])
                mm(psT[:, ui * 128:(ui + 1) * 128], idb[:], bbias_b[:, r + 4, h, :], False, True,
                   [b_idb, b_bbias_b], [b_psT])
            w = len(grp) * 128
            act(pT_t[:, 0:w], psT[:, 0:w], AF.Exp, [b_psT], [b_pT], scale=bscale)

        def band_PV(gi):
            h, qt, g0, grp, nun = groups[gi]
            po_t, b_po = pO[h % 2]
            pT_t, b_pT = pT[gi % 4]
            for ui, (kt_, b_k_, v_, b_v_, i, r) in enumerate(grp):
                u_ = g0 + ui
                mm(po_t[:, qt * 128:(qt + 1) * 128], v_[:, i, h, :], pT_t[:, ui * 128:(ui + 1) * 128],
                   (qt == 0 and u_ == 0), (qt == 3 and u_ == nun - 1), [b_v_, b_pT], [b_po])
            if h % 2 == 1 and qt == 3 and g0 + len(grp) == nun:
                finalize_pair(6 + h // 2)
        BSK = 3
        for gi in range(len(groups) + BSK):
            if gi < len(groups):
                band_S(gi)
            if gi >= BSK:
                band_PV(gi - BSK)
        for half in range(2):
            wq, b_wq = need_half(l, 3, half)
            for t in range(4):
                pzt, b_pzt = next_pz()
                for ft in range(8):
                    mm(pzt[:, :], mixT[:, ft, t * 128:(t + 1) * 128], wq[:, ft, :],
                       ft == 0, ft == 7, [b_mixT, b_wq], [b_pzt])
                tt("dve", xres[:, t, half * 512:(half + 1) * 512], xres[:, t, half * 512:(half + 1) * 512],
                   pzt[:, :], ALU.add, [b_xr[t], b_pzt], [b_xr[t]])
        for t in range(4):
            if l == DEPTH - 1:
                dma(y_p[t0 + t * 128:t0 + (t + 1) * 128, :], xres[:, t, :], [b_xr[t]], [], b_xr[t], is_out=True)
            else:
                dma(y1_d[j][t * 128:(t + 1) * 128, :], xres[:, t, :], [b_xr[t]], [B_y1[j]], b_xr[t])

    def sample_mixers(l):
        NS = 128
        gam = _ret_gamma()
        dma(stS[:].rearrange("k (e h) v -> k e h v", e=4), st_in[l].rearrange("e h k v -> k e h v"),
            [Bc], [b_stS], b_stS)
        memset("dve", stpadS[:], 0.0, [b_stpadS])
        for e in range(4):
            for h in range(4):
                o = 64 * (h % 2)
                cp("dve", stpadS[o:o + 64, e, h // 2, o:o + 64], stS[:, e * 4 + h, :], [b_stS], [b_stpadS])
        for pr in range(2):
            po_t, b_po = pO[pr % 2]
            for e2 in range(2):
                h = 2 * pr + e2
                r0 = 64 * e2
                psT, b_psT = pS[h % 2]
                mm(psT[:, 0:NS], kTr[r0:r0 + 64, pr, 0:NS], qTr[r0:r0 + 64, pr, 0:NS], True, True,
                   [b_kTr, b_qTr], [b_psT])
                tt("dve", innT[:, 0:NS], psT[:, 0:NS], decS[:, h, :], ALU.mult, [b_psT, b_decS], [b_innT])
                mm(po_t[:, 0:NS], vpad[:, 0, h, :], innT[:, 0:NS], e2 == 0, False, [b_vpad, b_innT], [b_po])
            for e in range(4):
                mm(po_t[:, e * 32:(e + 1) * 32], stpadS[:, e, pr, :], qTd[:, pr, e * 32:(e + 1) * 32], False, e == 3,
                   [b_stpadS, b_qTd], [b_po])
            headnorm(po_t[:, 0:NS], 128, NS, bd64, b_bd64, 64, [b_po])
            stt("dve", tmpm[:, 0:NS], po_t[:, 0:NS], gc[:, 9 + pr:10 + pr], rstd[:, 0:NS], ALU.mult, ALU.mult,
                [b_po, b_gc, b_rstd], [b_tmpm])
            tt("dve", mixT[:, pr, 0:NS], tmpm[:, 0:NS], gates[:, pr, 0:NS], ALU.mult, [b_tmpm, b_gates], [b_mixT])
        for e in range(4):
            pzt, b_pzt = next_pz()
            for h in range(4):
                lo = 64 * (h % 2)
                mm(pzt[0:64, h * 64:(h + 1) * 64], k_tm[:, 0, h * 64:(h + 1) * 64],
                   vdecS[:, e, h * 128 + lo:h * 128 + lo + 64], True, True, [b_k_tm, b_vdecS], [b_pzt])
            for h in range(4):
                stt("dve", stS[:, e * 4 + h, :], stS[:, e * 4 + h, :], float(gam[h] ** 32),
                    pzt[0:64, h * 64:(h + 1) * 64], ALU.mult, ALU.add, [b_stS, b_pzt], [b_stS])
        dma(s_state[l].rearrange("e h k v -> k (e h) v"), stS[:], [b_stS], [], b_stS, is_out=True)
        for e in range(4):
            for hf in range(2):
                dma(stg[:, 0:512], cckvT[l, e][:, hf * 512:(hf + 1) * 512], [Bc], [b_stg], b_stg)
                cp("dve", ckvb[:], stg[:, 0:512], [b_stg], [b_ckvb])
                dma(stg[0:32, 512:1024], ckrT[l, e][:, hf * 512:(hf + 1) * 512], [Bc], [b_stg], b_stg)
                for h in range(8):
                    cp("dve", kTm[64:96, h, :], stg[0:32, 512:1024], [b_stg], [b_kTm])
                gen_kv(kTc_d[l][e][hf], B_kTc[l][e][hf], vc_d[l][e][hf], B_vc[l][e][hf])
        qscale = 96.0 ** -0.5
        cnt = [0]

        def load_kv(h, srcK, BK, srcV, BV):
            kt_t, b_kt = ktb[cnt[0] % 2]
            vt_t, b_vt = vtb[h % 2][cnt[0] % 2]
            cnt[0] += 1
            lo = 0 if h % 2 == 0 else 64
            dma(kt_t[:, :], srcK[h], [BK], [b_kt], b_kt)
            dma(vt_t[:, :, lo:lo + 64], srcV.rearrange("(t p) c -> p t c", p=128)[:, :, h * 64:(h + 1) * 64],
                [BV], [b_vt], b_vt)
            return kt_t, b_kt, vt_t, b_vt
        for h in range(8):
            po_t, b_po = pO[h % 2]
            kt_t, b_kt, vt_t, b_vt = load_kv(h, kTs_d[l], B_kTs[l], vs_d[l], B_vs[l])
            psT, b_psT = pS[0]
            pT_t, b_pT = pT[0]
            mm(psT[:, 0:NS], kt_t[:, 0:128], qTm[:, h, 0:NS], True, True, [b_kt, b_qTm], [b_psT])
            act(pT_t[:, 0:NS], psT[:, 0:NS], AF.Exp, [b_psT], [b_pT], scale=qscale)
            tt("dve", pT_t[:, 0:NS], pT_t[:, 0:NS], smask[:, :], ALU.mult, [b_pT, b_smask], [b_pT])
            mm(po_t[:, 0:NS], vt_t[:, 0, :], pT_t[:, 0:NS], True, False, [b_vt, b_pT], [b_po])
            for e in range(4):
                for hf in range(2):
                    kt_t, b_kt, vt_t, b_vt = load_kv(h, kTc_d[l][e][hf], B_kTc[l][e][hf], vc_d[l][e][hf], B_vc[l][e][hf])
                    psT, b_psT = pS[cnt[0] % 4]
                    pT_t, b_pT = pT[cnt[0] % 4]
                    for kt in range(4):
                        mm(psT[:, kt * 32:(kt + 1) * 32], kt_t[:, kt * 128:(kt + 1) * 128],
                           qTm[:, h, e * 32:(e + 1) * 32], True, True, [b_kt, b_qTm], [b_psT])
                    act(pT_t[:, 0:128], psT[:, 0:128], AF.Exp, [b_psT], [b_pT], scale=qscale)
                    for kt in range(4):
                        mm(po_t[:, e * 32:(e + 1) * 32], vt_t[:, kt, :], pT_t[:, kt * 32:(kt + 1) * 32], False,
                           (e == 3 and hf == 1 and kt == 3), [b_vt, b_pT], [b_po])
            if h % 2 == 1:
                finalize_pair(2 + h // 2, NS)
        kn_t, b_kn = kTb[1]
        vn_t, b_vn = vba[1]
        kc_t, b_kc = kTb[0]
        vc_t, b_vc = vba[0]
        for pr in range(2):
            for e2 in range(2):
                h = 2 * pr + e2
                po_t, b_po = pO[h % 2]
                psT, b_psT = pS[cnt[0] % 4]
                pT_t, b_pT = pT[cnt[0] % 4]
                cnt[0] += 1
                mm(psT[:, 0:NS], kn_t[:, pr, 0:128], qTbz[:, h, 0:NS], True, False, [b_kn, b_qTbz], [b_psT])
                mm(psT[:, 0:NS], idb[:], bbiasS_b[:, 2, h, :], False, True, [b_idb, b_bbiasS_b], [b_psT])
                act(pT_t[:, 0:NS], psT[:, 0:NS], AF.Exp, [b_psT], [b_pT], scale=0.125)
                mm(po_t[:, 0:NS], vn_t[:, 0, h, :], pT_t[:, 0:NS], True, False, [b_vn, b_pT], [b_po])
            for e in range(4):
                dma(stg[:, 0:512], cbkT[l, e][pr * 128:(pr + 1) * 128, :], [Bc], [b_stg], b_stg)
                cp("dve", kc_t[:, 0, :], stg[:, 0:512], [b_stg], [b_kc])
                dma(stg[:, :].rearrange("p (t c) -> p t c", t=4),
                    cbv[l, e].rearrange("(t p) c -> p t c", p=128)[:, :, pr * 256:(pr + 1) * 256],
                    [Bc], [b_stg], b_stg)
                for kt in range(4):
                    tt("dve", vc_t[:, kt, 2 * pr:2 * pr + 2, :].rearrange("p h c -> p (h c)"),
                       stg[:, kt * 256:(kt + 1) * 256], onesp[:, pr * 256:(pr + 1) * 256], ALU.add,
                       [b_stg, b_onesp], [b_vc])
                for e2 in range(2):
                    h = 2 * pr + e2
                    po_t, b_po = pO[h % 2]
                    psT, b_psT = pS[cnt[0] % 4]
                    pT_t, b_pT = pT[cnt[0] % 4]
                    cnt[0] += 1
                    for kt in range(4):
                        mm(psT[:, kt * 32:(kt + 1) * 32], kc_t[:, 0, kt * 128:(kt + 1) * 128],
                           qTbz[:, h, e * 32:(e + 1) * 32], True, False, [b_kc, b_qTbz], [b_psT])
                        mm(psT[:, kt * 32:(kt + 1) * 32], idb[:], bbiasS_b[:, 0 if kt < 3 else 1, h, e * 32:(e + 1) * 32],
                           False, True, [b_idb, b_bbiasS_b], [b_psT])
                    act(pT_t[:, 0:128], psT[:, 0:128], AF.Exp, [b_psT], [b_pT], scale=0.125)
                    for kt in range(4):
                        mm(po_t[:, e * 32:(e + 1) * 32], vc_t[:, kt, h, :], pT_t[:, kt * 32:(kt + 1) * 32], False,
                           (e == 3 and kt == 3), [b_vc, b_pT], [b_po])
            finalize_pair(6 + pr, NS)
        for half in range(2):
            wq, b_wq = need_half(l, 3, half)
            pzt, b_pzt = next_pz()
            for ft in range(8):
                mm(pzt[:, :], mixT[:, ft, 0:128], wq[:, ft, :], ft == 0, ft == 7,
                   [b_mixT, b_wq], [b_pzt])
            tt("dve", xres[:, 0, half * 512:(half + 1) * 512], xres[:, 0, half * 512:(half + 1) * 512],
               pzt[:, :], ALU.add, [b_xr[0], b_pzt], [b_xr[0]])
        if l == DEPTH - 1:
            dma(y_s[:, :], xres[:, 0, :], [b_xr[0]], [], b_xr[0], is_out=True)
        else:
            dma(ys_d[:, :], xres[:, 0, :], [b_xr[0]], [B_ys], b_xr[0])

    def finalize_pair(ft, N=512):
        (pe_t, b_pe), (po2_t, b_po2) = pO
        act(rinv[64:128, 0, 0:N], pe_t[64:128, 0:N], AF.Ln, [b_pe], [b_rinv])
        act(rinv[0:64, 0, 0:N], po2_t[0:64, 0:N], AF.Ln, [b_po2], [b_rinv])
        act(rinv[:, 0, 0:N], rinv[:, 0, 0:N], AF.Exp, [b_rinv], [b_rinv], scale=-1.0)
        mm(pn[:, 0:N], sel[:, 0, :], rinv[:, 0, 0:N], True, True, [b_sel, b_rinv], [b_pn])
        cp("dve", rbs[:, 0:N], pn[:, 0:N], [b_pn], [b_rbs])
        tt("dve", tmpm[0:64, 0:N], pe_t[0:64, 0:N], rbs[0:64, 0:N], ALU.mult, [b_pe, b_rbs], [b_tmpm])
        tt("dve", tmpm[64:128, 0:N], po2_t[64:128, 0:N], rbs[64:128, 0:N], ALU.mult, [b_po2, b_rbs], [b_tmpm])
        tt("dve", mixT[:, ft, 0:N], tmpm[:, 0:N], gates[:, ft, 0:N], ALU.mult, [b_tmpm, b_gates], [b_mixT])

    stpad_t, b_stpad = T("stpad", [128, 2, 128], BF16)

    def P_state_term(pr, po_t, b_po):
        mm(po_t[:, 0:512], stpad_t[:, pr, :], qTd[:, pr, :], False, True, [b_stpad, b_qTd], [b_po])

    for l in range(DEPTH):
        load_weights(l)
        if cfg.with_sample:
            prompt_block(l, 0, sample=True)
        for j in range(NBLK if cfg.stage >= 1 else 0):
            prompt_block(l, j)
    P.finish()
    P.build(nc, es)
    es.close()
    return nc


def _bf(a):
    return np.ascontiguousarray(a.astype(ml_dtypes.bfloat16))


def make_consts(cfg):
    SEQ = cfg.seq
    c = {}
    pos = np.concatenate([np.arange(SEQ), cfg.past + (np.arange(128) % 32), np.zeros(384, np.int64)])
    c["cosA"], c["sinA"] = _rope_tables(pos, 64, 128)
    c["cosB"], c["sinB"] = _rope_tables(pos, 32, 64)
    c["c_idb"] = _bf(np.eye(128, dtype=np.float32))
    c["c_idf"] = np.eye(128, dtype=np.float32)
    bd64 = np.kron(np.eye(2), np.ones((64, 64))).astype(np.float32)
    bd32 = np.kron(np.eye(4), np.ones((32, 32))).astype(np.float32)
    c["c_bd64"] = _bf(bd64)
    c["c_bd32"] = _bf(bd32)
    c["c_ones"] = _bf(np.ones((128, 128), np.float32))
    pm = np.zeros((128, 128), np.float32)
    for m_ in range(128):
        pm[(m_ // 64) * 64 + ((m_ % 64) + 32) % 64, m_] = 1.0
    c["c_perm64"] = _bf(pm)
    sel = np.zeros((128, 2, 128), np.float32)
    for k in range(64):
        sel[64 + k, 0, k] = 1.0
        sel[k, 0, 64 + k] = 1.0
    c["c_sel"] = sel
    kpos = (np.arange(4)[None, :, None] * 128 + np.arange(128)[:, None, None])
    qpos = np.arange(512)[None, None, :]
    vis = (kpos // CHUNK) <= (qpos // CHUNK)
    c["c_mmask"] = _bf(np.where(vis, 1.0, 0.0).astype(np.float32))
    gam = _ret_gamma()
    m = np.arange(128)[:, None]
    ll = np.arange(128)[None, :]
    dec = np.zeros((128, 4, 4, 128), np.float64)
    for h in range(4):
        for d in range(4):
            dist = ll - m + 128 * d
            dec[:, h, d, :] = np.where(dist >= 0, gam[h] ** np.maximum(dist, 0), 0.0)
    c["c_dec"] = _bf(dec.astype(np.float32))
    p = np.arange(512)
    dqt = np.zeros((128, 2, 512), np.float64)
    for pr in range(2):
        for e in range(2):
            dqt[64 * e:64 * e + 64, pr, :] = (gam[2 * pr + e] ** (p + 1.0))[None, :]
    c["c_dq"] = dqt.astype(np.float32)
    dv = np.zeros((128, 4, 4), np.float64)
    onesp = np.zeros((128, 4, 128), np.float32)
    for h in range(4):
        lo = 0 if h % 2 == 0 else 64
        onesp[:, h, 64 - lo:128 - lo] = 1.0
        for t in range(4):
            dv[:, t, h] = 0.125 * gam[h] ** (511.0 - (t * 128 + np.arange(128)))
    c["c_dv"] = dv.astype(np.float32)
    c["c_onesp"] = onesp.reshape(128, 512)
    kp = np.arange(128)[:, None, None]
    qp = np.arange(128)[None, None, :]
    rr = (np.arange(5) - 4)[None, :, None]
    dchunk = -2 * rr + qp // 64 - kp // 64
    c["c_bmk"] = np.where((dchunk >= 0) & (dchunk <= 8), 0.0, -30000.0).astype(np.float32)
    gam = _ret_gamma()
    tok = np.arange(128)
    el, ii = tok // 32, tok % 32
    dqs = np.zeros((128, 2, 512), np.float64)
    for pr in range(2):
        for e in range(2):
            dqs[64 * e:64 * e + 64, pr, :] = (gam[2 * pr + e] ** ((np.arange(512) % 32) + 1.0))[None, :]
    c["c_dqS"] = dqs.astype(np.float32)
    dvs = np.zeros((128, 4, 4), np.float64)
    for e in range(4):
        for h in range(4):
            dvs[:, e, h] = np.where(el == e, 0.125 * gam[h] ** (31.0 - ii), 0.0)
    c["c_dvS"] = dvs.astype(np.float32)
    same = (el[:, None] == el[None, :])
    dif = ii[None, :] - ii[:, None]
    decs = np.zeros((128, 4, 128), np.float64)
    for h in range(4):
        decs[:, h, :] = np.where(same & (dif >= 0), gam[h] ** np.maximum(dif, 0), 0.0)
    c["c_decS"] = _bf(decs.astype(np.float32))
    c["c_smask"] = _bf(same.astype(np.float32))
    bms = np.zeros((128, 3, 128), np.float32)
    bms[:, 2, :] = np.where(same, 0.0, -30000.0)
    c["c_bmkS"] = bms
    return c


def make_weights(cfg, inp):
    L = cfg.depth
    w_in = np.asarray(inp["w_in"])[:L]
    cuts = np.cumsum([256, 256, 256, 256, 256, 128, 32, 512, 256, 256, 256, 256])[:-1]
    a_q, a_k, a_v, a_g, b_cq, b_ckv, b_kr, b_g, c_q, c_k, c_v, c_g = np.split(w_in, cuts, axis=-1)
    z64 = np.zeros(b_kr.shape[:-1] + (64,), np.float32)
    W1 = np.concatenate([a_q, _swap_cols(a_q, 64), a_k, _swap_cols(a_k, 64), a_g, b_g, c_g, b_cq, b_ckv,
                         np.concatenate([b_kr, _swap_cols(b_kr, 32), z64], -1), c_q, c_k], -1)
    assert W1.shape[-1] == NCT * 128, W1.shape
    def padheads(w):
        L_, R_, _ = w.shape
        o = np.zeros((L_, R_, 4, 128), np.float32)
        for h in range(4):
            lo = 0 if h % 2 == 0 else 64
            o[:, :, h, lo:lo + 64] = w[:, :, h * 64:(h + 1) * 64]
        return o.reshape(L_, R_, 512)
    WT = np.concatenate([padheads(a_v), padheads(c_v)], -1)
    wuq = np.asarray(inp["mla_w_uq"])[:L].reshape(L, 256, 8, 96)
    nope = wuq[..., :64].reshape(L, 256, 512)
    rope = wuq[..., 64:].reshape(L, 256, 256)
    WUQ = np.concatenate([nope, rope, _swap_cols(rope, 32)], -1)
    wukv = np.asarray(inp["mla_w_ukv"])[:L].reshape(L, 128, 8, 128)
    WUKV = np.concatenate([wukv[..., :64].reshape(L, 128, 512), wukv[..., 64:].reshape(L, 128, 512)], -1)
    GN = np.broadcast_to(np.asarray(inp["norm_g"])[:L, None, :], (L, 128, D)).copy()
    GC = np.zeros((L, 128, 16), np.float32)
    qa = np.asarray(inp["mla_qa_g"])[:L]
    GC[:, :, 0] = qa[:, :128]
    GC[:, :, 1] = qa[:, 128:]
    GC[:, :, 2] = np.tile(np.asarray(inp["mla_qn_g"])[:L], (1, 2))
    qr = np.asarray(inp["mla_qr_g"])[:L]
    GC[:, :, 3] = np.tile(qr, (1, 4))
    GC[:, :, 11] = np.tile(_swap_cols(qr, 32), (1, 4))
    GC[:, :, 4] = np.asarray(inp["mla_kva_g"])[:L]
    kr = np.asarray(inp["mla_kr_g"])[:L]
    GC[:, :64, 5] = np.concatenate([kr, _swap_cols(kr, 32)], -1)
    GC[:, :, 6] = np.tile(np.asarray(inp["mla_kn_g"])[:L], (1, 2))
    GC[:, :, 7] = np.tile(np.asarray(inp["band_qn_g"])[:L], (1, 2))
    GC[:, :, 8] = np.tile(np.asarray(inp["band_kn_g"])[:L], (1, 2))
    rg = np.asarray(inp["ret_gn_g"])[:L]
    GC[:, :, 9] = rg[:, :128]
    GC[:, :, 10] = rg[:, 128:]
    bb = np.asarray(inp["band_bias"])[:L]
    kp = np.arange(128)[:, None, None]
    qp = np.arange(128)[None, None, :]
    rr = (np.arange(5) - 4)[None, :, None]
    bidx = np.clip(qp - kp - 128 * rr, -128, 128) + 128
    BT = np.ascontiguousarray(bb[:, :, bidx].transpose(0, 2, 3, 1, 4))
    kp = np.arange(128)[:, None]
    iq = (np.arange(128) % 32)[None, :]
    ik = (np.arange(128) % 32)[:, None]
    i0 = np.full((128, 128), 256)
    i1 = np.clip(128 + iq - kp, -128, 128) + 128
    i2 = np.clip(iq - ik, -128, 128) + 128
    sidx = np.stack([i0, i1, i2], 1)
    BTS = np.ascontiguousarray(bb[:, :, sidx].transpose(0, 2, 3, 1, 4))
    return dict(BTS=BTS, W1=np.ascontiguousarray(W1), WT=np.ascontiguousarray(WT), WUQ=np.ascontiguousarray(WUQ),
                WUKV=np.ascontiguousarray(WUKV), WOUT=np.ascontiguousarray(np.asarray(inp["w_out"])[:L]),
                GN=GN, GC=GC, BT=BT)


def make_core_inputs(cfg, inp, c):
    L = cfg.depth
    m = {}
    m["x_p"] = np.ascontiguousarray(np.asarray(inp["x_prompt"])[c // 4, :cfg.seq])
    xs = np.zeros((512, D), np.float32)
    xs[:128] = np.asarray(inp["x_sample"])[4 * c:4 * c + 4].reshape(128, D)
    m["x_s"] = xs
    m["st_in"] = np.ascontiguousarray(np.asarray(inp["state_ret"])[:L, 4 * c:4 * c + 4])
    m["cckvT"] = np.ascontiguousarray(np.asarray(inp["cache_mla_ckv"])[:L, 4 * c:4 * c + 4].transpose(0, 1, 3, 2))
    m["ckrT"] = np.ascontiguousarray(np.asarray(inp["cache_mla_krope"])[:L, 4 * c:4 * c + 4].transpose(0, 1, 3, 2))
    bk = np.asarray(inp["cache_band_k"])[:L, 4 * c:4 * c + 4].reshape(L, 4, 512, 256)
    m["cbkT"] = np.ascontiguousarray(bk.transpose(0, 1, 3, 2))
    bv = np.asarray(inp["cache_band_v"])[:L, 4 * c:4 * c + 4]
    bvp = np.zeros((L, 4, 512, 4, 128), np.float32)
    for h in range(4):
        lo = 0 if h % 2 == 0 else 64
        bvp[:, :, :, h, lo:lo + 64] = bv[:, :, :, h, :]
    m["cbv"] = bvp.reshape(L, 4, 512, 512)
    return m


_PROG = {}


def kernel(**inputs):
    cfg = Cfg(seq=8192, depth=2, past=1024, with_sample=True)
    if "nc" not in _PROG:
        _PROG["nc"] = build_program(cfg)
    nc = _PROG["nc"]
    inp = {k: np.asarray(v) for k, v in inputs.items()}
    consts = make_consts(cfg)
    w = make_weights(cfg, inp)
    maps = []
    for c in range(8):
        m = dict(consts)
        m.update(w)
        m.update(make_core_inputs(cfg, inp, c))
        maps.append(m)
    res = run_bass_kernel_spmd(nc, maps, core_ids=list(range(8))).results
    L, S = cfg.depth, cfg.seq
    f = np.float32
    y_p = np.stack([res[0]["y_p"], res[4]["y_p"]]).astype(f)
    y_s = np.concatenate([res[c]["y_s"].reshape(4, 32, D) for c in range(8)], 0).astype(f)
    p_state = np.stack([res[0]["o_state"], res[4]["o_state"]], 1).astype(f)
    p_ckv = np.stack([res[0]["o_ckv"], res[4]["o_ckv"]], 1).astype(f)
    p_kr = np.stack([res[0]["o_kr"], res[4]["o_kr"]], 1).astype(f)
    p_bk = np.stack([res[0]["o_bk"], res[4]["o_bk"]], 1).reshape(L, 2, 512, 4, 64).astype(f)
    p_bv = np.stack([res[0]["o_bv"], res[4]["o_bv"]], 1).reshape(L, 2, 512, 4, 64).astype(f)
    s_state = np.concatenate([res[c]["s_state"] for c in range(8)], 1).astype(f)
    s_ckv = np.concatenate([res[c]["s_ckv"].reshape(L, 4, 32, 128) for c in range(8)], 1).astype(f)
    s_kr = np.concatenate([res[c]["s_kr"].reshape(L, 4, 32, 32) for c in range(8)], 1).astype(f)
    s_bk = np.concatenate([res[c]["s_bk"].reshape(L, 4, 32, 4, 64) for c in range(8)], 1).astype(f)
    s_bv = np.concatenate([res[c]["s_bv"].reshape(L, 4, 32, 4, 64) for c in range(8)], 1).astype(f)
    return (y_p, y_s, p_state, p_ckv, p_kr, p_bk, p_bv, s_state, s_ckv, s_kr, s_bk, s_bv)
```
